# Optimizing a Trainium2 kernel written in Bass

```python
import math
import jax, jax.numpy as jnp
from jax import lax
import numpy as np

D_MODEL = 1024
BATCH = 8
SEQ = 2048
DEPTH = 2

CHUNK = 64
N_MEM = 256
EPS = 1e-6
N_SUBLAYER_NORMS = 6

GDN_HEADS = 4
GDN_DK = 128
GDN_DV = 128
GDN_QK = GDN_HEADS * GDN_DK
GDN_VW = GDN_HEADS * GDN_DV
GDN_QKV = 2 * GDN_QK + GDN_VW
CONV_W = 4

MLA_HEADS = 4
MLA_Q_RANK = 256
MLA_KV_RANK = 256
MLA_NOPE = 128
MLA_ROPE = 64
MLA_V = 128
MLA_VW = MLA_HEADS * MLA_V
MLA_SCALE = (MLA_NOPE + MLA_ROPE) ** -0.5
ROPE_BASE = 10000.0
Q_BLOCK = 128

E_SECTIONS = (GDN_QKV, GDN_VW, GDN_HEADS, GDN_HEADS, MLA_Q_RANK, MLA_KV_RANK, MLA_ROPE)
E_IN = GDN_QKV + GDN_VW + 2 * GDN_HEADS + MLA_Q_RANK + MLA_KV_RANK + MLA_ROPE
E_MIX = GDN_VW + MLA_VW

LRU_WIDTH = D_MODEL
LRU_BLOCKS = 4
LRU_BW = LRU_WIDTH // LRU_BLOCKS
LRU_C = 8.0

XA_HEADS = 4
XA_HD = D_MODEL // XA_HEADS

D_FF = ((8 * D_MODEL + 3 * 256 - 1) // (3 * 256)) * 256

N_EVEN = (DEPTH + 1) // 2
N_ODD = DEPTH // 2

kernel_name = "hybrid_gdn_mla_rglru_streaming_block"


def rmsnorm(x, g):
    xf = x.astype(jnp.float32)
    y = xf * lax.rsqrt(jnp.mean(xf * xf, axis=-1, keepdims=True) + EPS)
    return (y * g.astype(jnp.float32)).astype(x.dtype)


def l2norm(x):
    return x * lax.rsqrt(jnp.sum(x * x, axis=-1, keepdims=True) + EPS)


def causal_conv(x, w):
    c = x.shape[-1]
    return lax.conv_general_dilated(
        x, w[:, None, :].astype(x.dtype), window_strides=(1,),
        padding=[(w.shape[0] - 1, 0)], dimension_numbers=("NWC", "WIO", "NWC"),
        feature_group_count=c)


def rope_tables(positions):
    inv_freq = ROPE_BASE ** (-jnp.arange(0, MLA_ROPE, 2, dtype=jnp.float32) / MLA_ROPE)
    ang = positions.astype(jnp.float32)[..., None] * inv_freq
    return jnp.cos(ang), jnp.sin(ang)


def apply_rope(x, cos, sin):
    x1, x2 = jnp.split(x, 2, axis=-1)
    cos = cos.astype(x.dtype)
    sin = sin.astype(x.dtype)
    return jnp.concatenate([x1 * cos - x2 * sin, x1 * sin + x2 * cos], axis=-1)


def gated_delta_rule(q, k, v, g, beta):
    b_, t_, h_, dk = q.shape
    dv = v.shape[-1]
    n = t_ // CHUNK

    def to_chunks(a):
        a = jnp.moveaxis(a, 2, 1)
        return a.reshape(b_, h_, n, CHUNK, *a.shape[3:])

    q, k, v, g, beta = map(to_chunks, (q, k, v, g, beta))
    gc = jnp.cumsum(g, axis=-1)
    idx = jnp.arange(CHUNK)
    causal = idx[:, None] >= idx[None, :]
    strict = idx[:, None] > idx[None, :]
    decay = jnp.exp(jnp.where(causal, gc[..., :, None] - gc[..., None, :], -jnp.inf))
    kb = k * beta[..., None]
    vb = v * beta[..., None]
    lower = jnp.where(strict, jnp.einsum("bhncd,bhnsd->bhncs", kb, k) * decay, 0.0)
    rhs = jnp.concatenate([vb, kb * jnp.exp(gc)[..., None]], axis=-1)
    sol = lax.linalg.triangular_solve(lower, rhs, left_side=True, lower=True,
                                      unit_diagonal=True)
    u, w = sol[..., :dv], sol[..., dv:]
    a_qk = jnp.where(causal, jnp.einsum("bhncd,bhnsd->bhncs", q, k) * decay, 0.0)

    def step(state, xs):
        q_i, k_i, u_i, w_i, gc_i, a_i = xs
        v_new = u_i - jnp.einsum("bhck,bhkv->bhcv", w_i, state)
        o_i = (jnp.einsum("bhck,bhkv->bhcv", q_i * jnp.exp(gc_i)[..., None], state)
               + jnp.einsum("bhcs,bhsv->bhcv", a_i, v_new))
        g_last = gc_i[..., -1]
        k_dec = k_i * jnp.exp(g_last[..., None] - gc_i)[..., None]
        state = state * jnp.exp(g_last)[..., None, None] + jnp.einsum(
            "bhck,bhcv->bhkv", k_dec, v_new)
        return state, o_i

    xs = tuple(jnp.moveaxis(a, 2, 0) for a in (q, k, u, w, gc, a_qk))
    s0 = jnp.zeros((b_, h_, dk, dv), jnp.float32)
    _, o = lax.scan(step, s0, xs)
    o = jnp.moveaxis(o, 0, 2).reshape(b_, h_, t_, dv)
    return jnp.moveaxis(o, 1, 2)


def mla_attention(q_nope, q_pe, k_nope, k_pe, v):
    b_, t_, h_, _ = q_nope.shape
    nqb = t_ // Q_BLOCK
    key_chunk = jnp.arange(t_) // CHUNK

    def block(args):
        i, qn, qp = args
        s = (jnp.einsum("bqhd,bkhd->bhqk", qn, k_nope)
             + jnp.einsum("bqhd,bkd->bhqk", qp, k_pe))
        s = s.astype(jnp.float32) * MLA_SCALE
        q_chunk = (i * Q_BLOCK + jnp.arange(Q_BLOCK)) // CHUNK
        s = jnp.where(key_chunk[None, :] <= q_chunk[:, None], s, -jnp.inf)
        p = jax.nn.softmax(s, axis=-1).astype(v.dtype)
        return jnp.einsum("bhqk,bkhd->bqhd", p, v)

    def to_blocks(a):
        return jnp.moveaxis(a.reshape(b_, nqb, Q_BLOCK, *a.shape[2:]), 1, 0)

    out = lax.map(block, (jnp.arange(nqb), to_blocks(q_nope), to_blocks(q_pe)))
    return jnp.moveaxis(out, 0, 1).reshape(b_, t_, h_, MLA_V)


def gdn_mla_mixer(h, cos, sin, w_in, conv_w, a_log, dt_bias, o_norm,
                  q_norm, kv_norm, w_uq, w_ukv, w_out):
    b_, t_, _ = h.shape
    f32 = jnp.float32
    cuts = np.cumsum(E_SECTIONS)[:-1].tolist()
    qkv, z, a, b, c_q, c_kv, k_rope = jnp.split(h @ w_in, cuts, axis=-1)

    qkv = jax.nn.silu(causal_conv(qkv, conv_w)).astype(f32)
    q, k, v = jnp.split(qkv, [GDN_QK, 2 * GDN_QK], axis=-1)
    q = l2norm(q.reshape(b_, t_, GDN_HEADS, GDN_DK)) * (GDN_DK ** -0.5)
    k = l2norm(k.reshape(b_, t_, GDN_HEADS, GDN_DK))
    v = v.reshape(b_, t_, GDN_HEADS, GDN_DV)
    beta = jax.nn.sigmoid(b.astype(f32))
    g = -jnp.exp(a_log.astype(f32)) * jax.nn.softplus(a.astype(f32) + dt_bias.astype(f32))
    o = gated_delta_rule(q, k, v, g, beta)
    o = rmsnorm(o, o_norm) * jax.nn.silu(z.reshape(b_, t_, GDN_HEADS, GDN_DV).astype(f32))
    out_a = o.reshape(b_, t_, GDN_VW).astype(h.dtype)

    qf = (rmsnorm(c_q, q_norm) @ w_uq).reshape(b_, t_, MLA_HEADS, MLA_NOPE + MLA_ROPE)
    q_nope, q_pe = qf[..., :MLA_NOPE], qf[..., MLA_NOPE:]
    kvf = (rmsnorm(c_kv, kv_norm) @ w_ukv).reshape(b_, t_, MLA_HEADS, MLA_NOPE + MLA_V)
    k_nope, v_b = kvf[..., :MLA_NOPE], kvf[..., MLA_NOPE:]
    q_pe = apply_rope(q_pe, cos[:, :, None, :], sin[:, :, None, :])
    k_pe = apply_rope(k_rope, cos, sin)
    out_b = mla_attention(q_nope, q_pe, k_nope, k_pe, v_b).reshape(b_, t_, MLA_VW)

    return jnp.concatenate([out_a, out_b], axis=-1) @ w_out


def _lru_combine(c1, c2):
    a1, b1 = c1
    a2, b2 = c2
    return a1 * a2, a2 * b1 + b2


def rglru_mixer(h, w_in, conv_w, conv_b, gate_a_w, gate_a_b, gate_x_w, gate_x_b,
                a_param, w_out):
    b_, t_, _ = h.shape
    f32 = jnp.float32
    xb, yb = jnp.split(h @ w_in, 2, axis=-1)
    gate = jax.nn.gelu(yb)
    xb = causal_conv(xb, conv_w) + conv_b
    xr = xb.reshape(b_, t_, LRU_BLOCKS, LRU_BW)
    r = jax.nn.sigmoid((jnp.einsum("btnd,nde->btne", xr, gate_a_w)
                        .reshape(b_, t_, LRU_WIDTH) + gate_a_b).astype(f32))
    i = jax.nn.sigmoid((jnp.einsum("btnd,nde->btne", xr, gate_x_w)
                        .reshape(b_, t_, LRU_WIDTH) + gate_x_b).astype(f32))
    log_a = -LRU_C * r * jax.nn.softplus(-a_param.astype(f32))
    a = jnp.exp(log_a)
    u = jnp.sqrt(-jnp.expm1(2.0 * log_a)) * (i * xb.astype(f32))
    _, hs = lax.associative_scan(_lru_combine, (a, u), axis=1)
    return (hs.astype(h.dtype) * gate) @ w_out


def memory_cross_attention(h, mem_n, wq, wkv, wo):
    b_, t_, _ = h.shape
    q = (h @ wq).reshape(b_, t_, XA_HEADS, XA_HD)
    k, v = jnp.split(mem_n @ wkv, 2, axis=-1)
    k = k.reshape(b_, N_MEM, XA_HEADS, XA_HD)
    v = v.reshape(b_, N_MEM, XA_HEADS, XA_HD)
    s = jnp.einsum("bthd,bmhd->bhtm", q, k).astype(jnp.float32) * (XA_HD ** -0.5)
    p = jax.nn.softmax(s, axis=-1).astype(v.dtype)
    o = jnp.einsum("bhtm,bmhd->bthd", p, v).reshape(b_, t_, D_MODEL)
    return o @ wo


def swiglu(h, w_in, w_out):
    gate, up = jnp.split(h @ w_in, 2, axis=-1)
    return (jax.nn.silu(gate) * up) @ w_out


def setup_inputs(seed: int = 0) -> dict:
    key = jax.random.key(seed)
    ks = list(jax.random.split(key, 40))
    f32 = jnp.float32

    def nrm(shape, fan_in):
        return jax.random.normal(ks.pop(), shape, f32) * (fan_in ** -0.5)

    def gain(shape):
        return 1.0 + 0.05 * jax.random.normal(ks.pop(), shape, f32)

    def small(shape):
        return 0.01 * jax.random.normal(ks.pop(), shape, f32)

    x = jax.random.normal(ks.pop(), (BATCH, SEQ, D_MODEL), f32)
    mem = jax.random.normal(ks.pop(), (BATCH, N_MEM, D_MODEL), f32)
    offset = jax.random.randint(ks.pop(), (BATCH, 1), 0, 64, dtype=jnp.int32) * CHUNK
    positions = (offset + jnp.arange(SEQ, dtype=jnp.int32)[None, :]).astype(jnp.int32)

    e_a_log = jnp.log(jax.random.uniform(ks.pop(), (N_EVEN, GDN_HEADS), f32, 1.0, 16.0))
    dt = jnp.exp(jax.random.uniform(ks.pop(), (N_EVEN, GDN_HEADS), f32,
                                    math.log(1e-3), math.log(1e-1)))
    e_dt_bias = dt + jnp.log(-jnp.expm1(-dt))
    a0 = jax.random.uniform(ks.pop(), (N_ODD, LRU_WIDTH), f32, 0.9, 0.999)
    o_a_param = jnp.log(a0) - jnp.log1p(-a0)

    return {
        "x": x,
        "mem": mem,
        "positions": positions,
        "norm_gains": gain((DEPTH, N_SUBLAYER_NORMS, D_MODEL)),
        "mem_norm": gain((D_MODEL,)),
        "e_w_in": nrm((N_EVEN, D_MODEL, E_IN), D_MODEL),
        "e_conv_w": nrm((N_EVEN, CONV_W, GDN_QKV), CONV_W),
        "e_a_log": e_a_log,
        "e_dt_bias": e_dt_bias,
        "e_o_norm": gain((N_EVEN, GDN_DV)),
        "e_q_norm": gain((N_EVEN, MLA_Q_RANK)),
        "e_kv_norm": gain((N_EVEN, MLA_KV_RANK)),
        "e_w_uq": nrm((N_EVEN, MLA_Q_RANK, MLA_HEADS * (MLA_NOPE + MLA_ROPE)), MLA_Q_RANK),
        "e_w_ukv": nrm((N_EVEN, MLA_KV_RANK, MLA_HEADS * (MLA_NOPE + MLA_V)), MLA_KV_RANK),
        "e_w_out": nrm((N_EVEN, E_MIX, D_MODEL), E_MIX),
        "o_w_in": nrm((N_ODD, D_MODEL, 2 * LRU_WIDTH), D_MODEL),
        "o_conv_w": nrm((N_ODD, CONV_W, LRU_WIDTH), CONV_W),
        "o_conv_b": small((N_ODD, LRU_WIDTH)),
        "o_gate_a_w": nrm((N_ODD, LRU_BLOCKS, LRU_BW, LRU_BW), LRU_BW),
        "o_gate_a_b": small((N_ODD, LRU_WIDTH)),
        "o_gate_x_w": nrm((N_ODD, LRU_BLOCKS, LRU_BW, LRU_BW), LRU_BW),
        "o_gate_x_b": small((N_ODD, LRU_WIDTH)),
        "o_a_param": o_a_param,
        "o_w_out": nrm((N_ODD, LRU_WIDTH, D_MODEL), LRU_WIDTH),
        "xa_wq": nrm((DEPTH, D_MODEL, D_MODEL), D_MODEL),
        "xa_wkv": nrm((DEPTH, D_MODEL, 2 * D_MODEL), D_MODEL),
        "xa_wo": nrm((DEPTH, D_MODEL, D_MODEL), D_MODEL),
        "ffn_w_in": nrm((DEPTH, D_MODEL, 2 * D_FF), D_MODEL),
        "ffn_w_out": nrm((DEPTH, D_FF, D_MODEL), D_FF),
    }


def reference(x, mem, positions, norm_gains, mem_norm,
              e_w_in, e_conv_w, e_a_log, e_dt_bias, e_o_norm, e_q_norm, e_kv_norm,
              e_w_uq, e_w_ukv, e_w_out,
              o_w_in, o_conv_w, o_conv_b, o_gate_a_w, o_gate_a_b, o_gate_x_w, o_gate_x_b,
              o_a_param, o_w_out,
              xa_wq, xa_wkv, xa_wo, ffn_w_in, ffn_w_out):
    cos, sin = rope_tables(positions)
    mem_n = rmsnorm(mem, mem_norm)
    for layer in range(DEPTH):
        g = norm_gains[layer]
        h = rmsnorm(x, g[0])
        if layer % 2 == 0:
            e = layer // 2
            y = gdn_mla_mixer(h, cos, sin, e_w_in[e], e_conv_w[e], e_a_log[e], e_dt_bias[e],
                              e_o_norm[e], e_q_norm[e], e_kv_norm[e], e_w_uq[e], e_w_ukv[e],
                              e_w_out[e])
        else:
            o = layer // 2
            y = rglru_mixer(h, o_w_in[o], o_conv_w[o], o_conv_b[o], o_gate_a_w[o],
                            o_gate_a_b[o], o_gate_x_w[o], o_gate_x_b[o], o_a_param[o],
                            o_w_out[o])
        x = x + rmsnorm(y, g[1])
        h = rmsnorm(x, g[2])
        x = x + rmsnorm(memory_cross_attention(h, mem_n, xa_wq[layer], xa_wkv[layer],
                                               xa_wo[layer]), g[3])
        h = rmsnorm(x, g[4])
        x = x + rmsnorm(swiglu(h, ffn_w_in[layer], ffn_w_out[layer]), g[5])
    return x
```

```python
import numpy as np
from contextlib import ExitStack
import ml_dtypes
import concourse.bass as bass
import concourse.mybir as mybir
from concourse.bass_utils import run_bass_kernel_spmd

F32 = mybir.dt.float32
BF16 = mybir.dt.bfloat16
I32 = mybir.dt.int32
U8 = mybir.dt.uint8
AF = mybir.ActivationFunctionType
ALU = mybir.AluOpType
AX = mybir.AxisListType
DSZ = {F32: 4, BF16: 2, I32: 4, U8: 1}

T = 2048
D = 1024
NT = 16
DFF = 2816
NFC = 22
EPS = 1e-6
E_IN = 2632
SB_BYTES = 207 * 1024
PS_BYTES = 16 * 1024
NDMA = 24
NEG = -1.0e30


class _Rec:
    __slots__ = ("p0", "p1", "lo", "hi", "writer", "readers", "pages")


class Sched:
    ENG = ("pe", "act", "dve", "pool", "sp")

    def __init__(self, nc, es):
        self.nc = nc
        self.h = {"pe": nc.tensor, "act": nc.scalar, "dve": nc.vector,
                  "pool": nc.gpsimd, "sp": nc.sync}
        self.sem = {e: es.enter_context(nc.semaphore("sem_" + e)) for e in self.ENG}
        self.dsem = [es.enter_context(nc.semaphore("dsem%d" % i)) for i in range(NDMA)]
        self.dcnt = [0] * NDMA
        self.dnext = 0
        self.dnext_q = {}
        self.ops = {e: [] for e in self.ENG}
        self.known = {e: {} for e in self.ENG}
        self.pages = {"arena": {}, "psum": {}}
        self.pbytes = {"arena": SB_BYTES, "psum": PS_BYTES}
        self.pgsz = {"arena": 1024, "psum": 512}
        self.extra = {}
        self.tag = ""
        self.annot = False

    def regions(self, ap):
        name = ap.tensor.name
        if name not in self.pbytes:
            return []
        esz = DSZ[ap.dtype]
        pb = self.pbytes[name]
        offb = ap.offset * esz
        p0 = offb // pb
        lo = offb % pb
        dims = list(ap.ap)
        npart = dims[0][1]
        free = [(abs(s), c) for s, c in dims[1:] if c > 1]
        out = []

        def rec(base, fd):
            if not fd:
                out.append((name, p0, p0 + npart, base, base + esz))
                return
            ext_in = 1 + sum((c - 1) * s for s, c in fd[1:])
            s0, c0 = fd[0]
            if len(fd) >= 2 and s0 > ext_in and c0 <= 64 and len(out) < 256:
                for i in range(c0):
                    rec(base + i * s0 * esz, fd[1:])
            else:
                ext = 1 + sum((c - 1) * s for s, c in fd)
                out.append((name, p0, p0 + npart, base, base + ext * esz))

        free.sort(key=lambda t: -t[0])
        rec(lo, free)
        if name == "psum":
            banks = sorted({b for (_, _, _, l, h) in out for b in range(l // 2048, (h - 1) // 2048 + 1)})
            out = [(name, p0, p0 + npart, b * 2048, (b + 1) * 2048) for b in banks]
        return out

    def _overl(self, r):
        name, p0, p1, lo, hi = r
        pg = self.pgsz[name]
        pages = self.pages[name]
        seen = set()
        res = []
        for k in range(lo // pg, (hi - 1) // pg + 1):
            for rc in pages.get(k, ()):
                if id(rc) in seen:
                    continue
                seen.add(id(rc))
                if rc.lo < hi and lo < rc.hi and rc.p0 < p1 and p0 < rc.p1:
                    res.append(rc)
        return res

    def _add_rec(self, r, writer):
        name, p0, p1, lo, hi = r
        pg = self.pgsz[name]
        rc = _Rec()
        rc.p0, rc.p1, rc.lo, rc.hi = p0, p1, lo, hi
        rc.writer = writer
        rc.readers = {}
        rc.pages = (name, lo // pg, (hi - 1) // pg)
        pages = self.pages[name]
        for k in range(rc.pages[1], rc.pages[2] + 1):
            pages.setdefault(k, []).append(rc)
        return rc

    def _del_rec(self, rc):
        name, k0, k1 = rc.pages
        pages = self.pages[name]
        for k in range(k0, k1 + 1):
            pages[k].remove(rc)

    def _deps(self, reads, writes):
        deps = {}

        def add(src, val):
            if deps.get(src, -1) < val:
                deps[src] = val
        rregs = []
        for ap in reads:
            rregs.extend(self.regions(ap))
        wregs = []
        for ap in writes:
            wregs.extend(self.regions(ap))
        for r in rregs:
            for rc in self._overl(r):
                if rc.writer is not None:
                    add(*rc.writer)
        for r in wregs:
            for rc in self._overl(r):
                if rc.writer is not None:
                    add(*rc.writer)
                for s, v in rc.readers.items():
                    add(s, v)
        return deps, rregs, wregs

    def _commit(self, me, rregs, wregs):
        for r in rregs:
            ov = self._overl(r)
            if not ov:
                rc = self._add_rec(r, None)
                ov = [rc]
            for rc in ov:
                if rc.readers.get(me[0], -1) < me[1]:
                    rc.readers[me[0]] = me[1]
        for r in wregs:
            name, p0, p1, lo, hi = r
            for rc in self._overl(r):
                if rc.lo >= lo and rc.hi <= hi and rc.p0 >= p0 and rc.p1 <= p1:
                    self._del_rec(rc)
            self._add_rec(r, me)

    def op(self, e, fn, reads=(), writes=()):
        deps, rregs, wregs = self._deps(reads, writes)
        waits = []
        kn = self.known[e]
        for src, val in deps.items():
            if kn.get(src, -1) < val:
                kn[src] = val
                waits.append((src, val))
        idx = len(self.ops[e])
        self.ops[e].append([fn, waits, False, None, self.tag])
        self._commit((e, idx), rregs, wregs)

    def dma(self, q, out, in_, reads=None, writes=None):
        reads = [in_] if reads is None else reads
        writes = [out] if writes is None else writes
        deps, rregs, wregs = self._deps(reads, writes)
        lo, hi = (0, 8) if q == "sp" else (8, NDMA)
        cur = self.dnext_q.get(q, lo)
        slot = cur
        self.dnext_q[q] = lo + (cur + 1 - lo) % (hi - lo)
        src = ("d", slot)
        prev = self.dcnt[slot]
        if prev > 0:
            if deps.get(src, -1) < prev:
                deps[src] = prev
        self.dcnt[slot] = prev + 16
        waits = []
        kn = self.known[q]
        for s, val in deps.items():
            if kn.get(s, -1) < val:
                kn[s] = val
                waits.append((s, val))
        self.ops[q].append([lambda h: h.dma_start(out=out, in_=in_), waits, False, slot, self.tag])
        self._commit((src, self.dcnt[slot]), rregs, wregs)

    def final_wait(self, e):
        waits = [(("d", s), self.dcnt[s]) for s in range(NDMA) if self.dcnt[s] > 0]
        self.ops[e].append([None, waits, False, None, "final"])

    def emit(self):
        for e in self.ENG:
            for o in self.ops[e]:
                for src, val in o[1]:
                    if not isinstance(src, tuple):
                        self.ops[src][val][2] = True
        rank = {}
        for e in self.ENG:
            r = 0
            rk = []
            for o in self.ops[e]:
                if o[2]:
                    r += 1
                rk.append(r)
            rank[e] = rk
        n_ins = 0
        for e in self.ENG:
            h = self.h[e]
            for o in self.ops[e]:
                fn, waits, needed, slot, tag = o[:5]
                for src, val in waits:
                    if isinstance(src, tuple):
                        h.wait_ge(self.dsem[src[1]], val)
                    else:
                        h.wait_ge(self.sem[src], rank[src][val])
                    n_ins += 1
                if fn is None:
                    continue
                ins = fn(h)
                n_ins += 1
                if self.annot and tag:
                    ins.annotate(tag)
                if slot is not None:
                    ins.then_inc(self.dsem[slot], 16)
                elif needed:
                    ins.then_inc(self.sem[e], 1)
        return n_ins

    def mm(self, out, pairs, start=True, stop=True):
        n = len(pairs)
        pairs = [(p if len(p) == 3 else (out, p[0], p[1])) for p in pairs]
        reads = []
        for o, a, b in pairs:
            reads += [a, b]

        def fn(h):
            ins = None
            for i, (o, a, b) in enumerate(pairs):
                ins = h.matmul(o, a, b, start=(start and i == 0), stop=(stop and i == n - 1))
            return ins
        rd = reads if start else reads + [out]
        self.op("pe", fn, rd, [out])

    def tr(self, out, in_, ident):
        self.op("pe", lambda h: h.transpose(out, in_, ident), [in_, ident], [out])

    def act(self, out, in_, func, bias=None, scale=None, accum_out=None, eng="act"):
        reads = [in_]
        kw = {}
        if bias is not None:
            kw["bias"] = bias
            if not isinstance(bias, (int, float)):
                reads.append(bias)
        if scale is not None:
            kw["scale"] = scale
            if not isinstance(scale, (int, float)):
                reads.append(scale)
        writes = [out]
        if accum_out is not None:
            kw["accum_out"] = accum_out
            writes.append(accum_out)
        self.op("act", lambda h: h.activation(out=out, in_=in_, func=func, **kw), reads, writes)

    def tt(self, e, out, in0, in1, op):
        self.op(e, lambda h: h.tensor_tensor(out=out, in0=in0, in1=in1, op=op), [in0, in1], [out])

    def ts(self, e, out, in0, s1, s2, op0, op1=None):
        reads = [in0]
        for s in (s1, s2):
            if s is not None and not isinstance(s, (int, float)):
                reads.append(s)
        if op1 is None:
            self.op(e, lambda h: h.tensor_scalar(out=out, in0=in0, scalar1=s1, scalar2=None, op0=op0),
                    reads, [out])
        else:
            self.op(e, lambda h: h.tensor_scalar(out=out, in0=in0, scalar1=s1, scalar2=s2, op0=op0, op1=op1),
                    reads, [out])

    def stt(self, e, out, in0, scalar, in1, op0, op1):
        reads = [in0, in1]
        if not isinstance(scalar, (int, float)):
            reads.append(scalar)
        self.op(e, lambda h: h.scalar_tensor_tensor(out=out, in0=in0, scalar=scalar, in1=in1, op0=op0, op1=op1),
                reads, [out])

    def copy(self, e, out, in_):
        if e == "act":
            self.op(e, lambda h: h.copy(out=out, in_=in_), [in_], [out])
        else:
            self.op(e, lambda h: h.tensor_copy(out=out, in_=in_), [in_], [out])

    def memset(self, e, out, val):
        self.op(e, lambda h: h.memset(out, val), [], [out])

    def reduce(self, e, out, in_, op, axis=AX.X):
        self.op(e, lambda h: h.tensor_reduce(out=out, in_=in_, axis=axis, op=op), [in_], [out])

    def scan(self, out, d0, d1, init, op0, op1):
        reads = [d0, d1]
        if not isinstance(init, (int, float)):
            reads.append(init)
        self.op("dve", lambda h: h.tensor_tensor_scan(out=out, data0=d0, data1=d1, initial=init, op0=op0, op1=op1),
                reads, [out])


def pipeline(gens, depth=2, skew=1):
    it = iter(gens)
    active = []
    exhausted = False
    while True:
        if not exhausted and len(active) < depth and (not active or active[-1][1] >= skew):
            try:
                active.append([next(it), 0])
            except StopIteration:
                exhausted = True
        if not active:
            if exhausted:
                break
            continue
        for a in list(active):
            try:
                next(a[0])
                a[1] += 1
            except StopIteration:
                active.remove(a)


class Arena:
    def __init__(self, base, nbytes):
        self.base = base
        self.nbytes = nbytes
        self.top = 0
        self.peak = 0

    def alloc(self, free_shape, dtype, parts=128):
        n = 1
        for s in free_shape:
            n *= s
        nb = n * DSZ[dtype]
        off = (self.top + 63) // 64 * 64
        assert off + nb <= self.nbytes, "arena overflow: need %d at %d (cap %d)" % (nb, off, self.nbytes)
        self.top = off + nb
        self.peak = max(self.peak, self.top)
        v = self.base[0:parts, off:off + nb]
        if dtype != U8:
            v = v.bitcast(dtype)
        if len(free_shape) > 1:
            names = " ".join("d%d" % i for i in range(len(free_shape)))
            kw = {"d%d" % i: free_shape[i] for i in range(1, len(free_shape))}
            v = v.rearrange("p (%s) -> p %s" % (names, names), **kw)
        return v

    def mark(self):
        return self.top

    def release(self, m):
        self.top = m


class Prog:
    def __init__(self, cfg):
        self.cfg = cfg
        self.nc = bass.Bass("TRN2", target_bir_lowering=False)
        self.es = ExitStack()
        nc = self.nc
        es = self.es
        self.dram = {}
        sb = es.enter_context(nc.sbuf_tensor("arena", [128, SB_BYTES], U8))
        ps = es.enter_context(nc.psum_tensor("psum", [128, PS_BYTES // 4], F32))
        self.S = Sched(nc, es)
        self.S.annot = bool(cfg.get("annot"))
        self.A = Arena(sb, SB_BYTES)
        self.ps = ps

    def din(self, name, shape, dtype=F32):
        t = self.nc.dram_tensor(name, list(shape), dtype, kind="ExternalInput").ap()
        self.dram[name] = t
        return t

    def dout(self, name, shape, dtype=F32):
        t = self.nc.dram_tensor(name, list(shape), dtype, kind="ExternalOutput").ap()
        self.dram[name] = t
        return t

    def bank(self, b, n=1):
        return self.ps[:, b * 512:(b + n) * 512]

    def bank_bf(self, b, n=1):
        return self.ps[:, b * 512:(b + n) * 512].bitcast(BF16)

    def load_gain(self, dst, src_row, mult):
        S = self.S
        S.dma("sp", dst, src_row.partition_broadcast(128))
        S.ts("dve", dst, dst, float(mult), None, ALU.mult)

    def rstd(self, out, ss, c):
        S = self.S
        S.act(out, ss, AF.Ln, bias=float(c))
        S.act(out, out, AF.Exp, scale=-0.5)

    def prenorm(self, tiles, gb, hT, junk, col0=0):
        S, A = self.S, self.A
        m = A.mark()
        n = len(tiles)
        ss = A.alloc((n,), F32)
        rs = A.alloc((n,), F32)
        nb = len(self.tr_banks)
        if A.top + nb * 2048 + 256 > A.nbytes:
            nb = 2
        hb = [A.alloc((D,), BF16) for _ in range(nb)]
        junk2 = hb[1]
        for i, t in enumerate(tiles):
            if i % 2 == 0:
                S.act(junk, self.x[:, t, :], AF.Square, accum_out=ss[:, i:i + 1])
            else:
                xt = self.x[:, t, :]
                S.op("dve", (lambda xt, i: (lambda h: h.scalar_tensor_tensor(
                    out=junk2, in0=xt, scalar=1.0, in1=xt, op0=ALU.mult, op1=ALU.mult,
                    accum_out=ss[:, i:i + 1])))(xt, i), [xt], [junk2, ss[:, i:i + 1]])
        self.rstd(rs, ss, D * EPS)
        for i, t in enumerate(tiles):
            xt = self.x[:, t, :]
            h = hb[i % nb]
            S.stt("dve", h, xt, rs[:, i:i + 1], gb, ALU.mult, ALU.mult)
            pb = self.bank_bf(self.tr_banks[i % nb])
            for c in range(8):
                S.tr(pb[:, c * 128:(c + 1) * 128], h[:, c * 128:(c + 1) * 128], self.ident)
            dst = hT[:, :, col0 + i * 128: col0 + (i + 1) * 128]
            src = pb.rearrange("p (c t) -> p c t", t=128)
            S.copy("act", dst, src)
        A.release(m)

    def postnorm_add(self, psy, gb, t, ss, rs, i, junk, tmp):
        S = self.S
        S.act(junk, psy, AF.Square, accum_out=ss[:, i:i + 1])
        self.rstd(rs[:, i:i + 1], ss[:, i:i + 1], D * EPS)
        S.stt("dve", tmp, psy, rs[:, i:i + 1], gb, ALU.mult, ALU.mult)
        xt = self.x[:, t, :]
        S.tt("pool", xt, xt, tmp, ALU.add)

    def ffn(self, l):
        S, A = self.S, self.A
        m0 = A.mark()
        w_in = self.dram["ffn_w_in"][l].rearrange("(kc kp) n -> kp kc n", kp=128)
        w_out = self.dram["ffn_w_out"][l].rearrange("(fc fp) n -> fp fc n", fp=128)
        gslot = A.alloc((D,), F32)
        wout = A.alloc((NFC, D), BF16)
        hT = A.alloc((8, 1024), BF16)
        actT = A.alloc((NFC, 1024), BF16)
        GC = 256
        NG = DFF // GC
        wg = [A.alloc((8, GC), BF16) for _ in range(2)]
        wu = [A.alloc((8, GC), BF16) for _ in range(2)]
        tmp = [A.alloc((D,), F32) for _ in range(2)]
        sg = [tmp[0][:, 0:512], tmp[0][:, 512:1024]]
        junk = A.alloc((D,), BF16)
        ss = A.alloc((16,), F32)
        rs = A.alloc((16,), F32)
        self.tr_banks = (0, 1, 2, 3)
        for half in range(2):
            tiles = list(range(half * 8, half * 8 + 8))

            def load_g(g):
                s = g % 2
                S.dma("pool", wg[s], w_in[:, :, g * GC:(g + 1) * GC])
                S.dma("pool", wu[s], w_in[:, :, DFF + g * GC: DFF + (g + 1) * GC])
            load_g(0)
            self.load_gain(gslot, self.dram["norm_gains"][l, 4, :], 32.0)
            self.prenorm(tiles, gslot, hT, junk)
            if half == 0:
                for q in range(2):
                    S.dma("pool", wout[:, q * 11:(q + 1) * 11, :], w_out[:, q * 11:(q + 1) * 11, :])
            k = 0
            for g in range(NG):
                if g + 1 < NG:
                    load_g(g + 1)
                s = g % 2
                for j in range(GC // 128):
                    fc = g * (GC // 128) + j
                    for tb in range(2):
                        pg = self.bank(0 + (k % 2) * 2)
                        pu = self.bank(1 + (k % 2) * 2)
                        S.mm(pg, [(wg[s][:, kc, j * 128:(j + 1) * 128], hT[:, kc, tb * 512:(tb + 1) * 512])
                                  for kc in range(8)])
                        S.mm(pu, [(wu[s][:, kc, j * 128:(j + 1) * 128], hT[:, kc, tb * 512:(tb + 1) * 512])
                                  for kc in range(8)])
                        sgb = sg[k % 2]
                        S.act(sgb, pg, AF.Silu)
                        S.tt("dve", actT[:, fc, tb * 512:(tb + 1) * 512], sgb, pu, ALU.mult)
                        k += 1
            self.load_gain(gslot, self.dram["norm_gains"][l, 5, :], 32.0)
            for i, t in enumerate(tiles):
                py = self.bank(4 + (i % 2) * 2, 2)
                for nb in range(2):
                    S.mm(py[:, nb * 512:(nb + 1) * 512],
                         [(actT[:, fc, i * 128:(i + 1) * 128], wout[:, fc, nb * 512:(nb + 1) * 512])
                          for fc in range(NFC)])
                self.postnorm_add(py, gslot, t, ss, rs, half * 8 + i, junk, tmp[i % 2])
        A.release(m0)

    def setup_mem(self):
        S, A = self.S, self.A
        self.mem_nT = A.alloc((8, 256), BF16)
        m = A.mark()
        memt = A.alloc((2, D), F32)
        gb = A.alloc((D,), F32)
        junk = A.alloc((D,), BF16)
        ss = A.alloc((2,), F32)
        hb = [A.alloc((D,), BF16) for _ in range(2)]
        S.dma("sp", memt, self.dram["mem"].rearrange("(t p) d -> p t d", p=128))
        self.load_gain(gb, self.dram["mem_norm"], 32.0)
        for t in range(2):
            S.act(junk, memt[:, t, :], AF.Square, accum_out=ss[:, t:t + 1])
        self.rstd(ss, ss, D * EPS)
        for t in range(2):
            S.stt("dve", hb[t], memt[:, t, :], ss[:, t:t + 1], gb, ALU.mult, ALU.mult)
            pb = self.bank_bf(t)
            for c in range(8):
                S.tr(pb[:, c * 128:(c + 1) * 128], hb[t][:, c * 128:(c + 1) * 128], self.ident)
            S.copy("act", self.mem_nT[:, :, t * 128:(t + 1) * 128], pb.rearrange("p (c t) -> p c t", t=128))
        A.release(m)

    def xattn(self, l):
        S, A = self.S, self.A
        m0 = A.mark()
        self.setup_mem()
        scale = 256.0 ** -0.5
        wq = self.dram["xa_wq"][l].rearrange("(kc kp) n -> kp kc n", kp=128)
        wkv = self.dram["xa_wkv"][l].rearrange("(kc kp) n -> kp kc n", kp=128)
        wo = self.dram["xa_wo"][l].rearrange("(kc kp) n -> kp kc n", kp=128)
        gslot = A.alloc((D,), F32)
        hT = A.alloc((8, T), BF16)
        qT = A.alloc((8, T), BF16)
        kT = A.alloc((8, 256), BF16)
        V = A.alloc((2, D), BF16)
        wb = [A.alloc((8, 512), BF16) for _ in range(2)]
        junk = A.alloc((D,), BF16)
        tmp = [A.alloc((D,), F32) for _ in range(2)]
        ss = A.alloc((16,), F32)
        rs = A.alloc((16,), F32)
        P = [A.alloc((4, 256), BF16) for _ in range(2)]
        PT = [A.alloc((8, 128), BF16) for _ in range(2)]
        mx = [A.alloc((4,), F32) for _ in range(2)]
        nb_ = [A.alloc((4,), F32) for _ in range(2)]
        sm = [A.alloc((4,), F32) for _ in range(2)]
        rinv = [A.alloc((4,), F32) for _ in range(2)]
        self.tr_banks = (0, 1, 2, 3)
        S.tag = "xa.kv"
        for g in range(4):
            S.dma("pool", wb[g % 2], wkv[:, :, g * 512:(g + 1) * 512])
            w = wb[g % 2]
            if g < 2:
                for j in range(4):
                    c = 4 * g + j
                    pb = self.bank(j % 2)
                    S.mm(pb[:, 0:256], [(w[:, kc, j * 128:(j + 1) * 128], self.mem_nT[:, kc, :]) for kc in range(8)])
                    S.copy("act" if j % 2 == 0 else "dve", kT[:, c, :], pb[:, 0:256])
            else:
                for mt in range(2):
                    pb = self.bank(mt)
                    S.mm(pb, [(self.mem_nT[:, kc, mt * 128:(mt + 1) * 128], w[:, kc, :]) for kc in range(8)])
                    S.copy("act" if mt == 0 else "dve", V[:, mt, (g - 2) * 512:(g - 1) * 512], pb)
        S.tag = "xa.pre"
        self.load_gain(gslot, self.dram["norm_gains"][l, 2, :], 32.0)
        S.dma("pool", wb[0], wq[:, :, 0:512])
        S.dma("pool", wb[1], wq[:, :, 512:1024])
        self.prenorm(list(range(16)), gslot, hT, junk)
        S.tag = "xa.q"
        k = 0
        for g in range(2):
            w = wb[g]
            for j in range(4):
                c = 4 * g + j
                for tb in range(4):
                    pb = self.bank(4 + k % 4)
                    S.mm(pb, [(w[:, kc, j * 128:(j + 1) * 128], hT[:, kc, tb * 512:(tb + 1) * 512]) for kc in range(8)])
                    S.copy("act" if k % 2 == 0 else "dve", qT[:, c, tb * 512:(tb + 1) * 512], pb)
                    k += 1
        S.dma("pool", wb[0], wo[:, :, 0:512])
        S.dma("pool", wb[1], wo[:, :, 512:1024])
        self.load_gain(gslot, self.dram["norm_gains"][l, 3, :], 32.0)
        oT = hT
        S.tag = "xa.attn"
        def xa_block(i):
            p = i % 2
            blk = slice(i * 128, (i + 1) * 128)
            psS = self.bank(2 * p, 2)
            for h in range(4):
                S.mm(psS[:, h * 256:(h + 1) * 256], [(qT[:, 2 * h + c, blk], kT[:, 2 * h + c, :]) for c in range(2)])
            yield
            S.reduce("dve", mx[p], psS.rearrange("p (h m) -> p h m", m=256), ALU.max)
            S.ts("dve", nb_[p], mx[p], -scale, None, ALU.mult)
            Pi = P[p]
            for h in range(4):
                S.act(Pi[:, h, :], psS[:, h * 256:(h + 1) * 256], AF.Exp, bias=nb_[p][:, h:h + 1], scale=scale,
                      accum_out=sm[p][:, h:h + 1])
            S.op("dve", lambda hh: hh.reciprocal(out=rinv[p], in_=sm[p]), [sm[p]], [rinv[p]])
            S.tt("dve", Pi, Pi, rinv[p].unsqueeze(2).to_broadcast([128, 4, 256]), ALU.mult)
            yield
            pt = self.bank_bf(4 + p)
            for h in range(4):
                for mc in range(2):
                    S.tr(pt[:, (h * 2 + mc) * 128:(h * 2 + mc + 1) * 128], Pi[:, h, mc * 128:(mc + 1) * 128], self.ident)
            PTi = PT[p]
            S.copy("act", PTi, pt.rearrange("p (c t) -> p c t", t=128))
            yield
            psO = self.bank(6, 2)
            for h in range(4):
                for c in range(2):
                    cc = 2 * h + c
                    S.mm(psO[:, cc * 128:(cc + 1) * 128],
                         [(V[:, mc, h * 256 + c * 128: h * 256 + (c + 1) * 128], PTi[:, h * 2 + mc, :]) for mc in range(2)])
            S.copy("dve", oT[:, :, blk], psO.rearrange("p (c t) -> p c t", t=128))
        pipeline((xa_block(i) for i in range(16)), depth=2)
        S.tag = "xa.out"
        for i in range(16):
            py = self.bank(2 * (i % 2), 2)
            for nb in range(2):
                S.mm(py[:, nb * 512:(nb + 1) * 512],
                     [(oT[:, kc, i * 128:(i + 1) * 128], wb[nb][:, kc, :]) for kc in range(8)])
            self.postnorm_add(py, gslot, i, ss, rs, i, junk, tmp[i % 2])
        A.release(m0)

    def load_cols(self, pcol, rows):
        S, A = self.S, self.A
        m = A.mark()
        nrow = len(rows)
        nch = rows[0].shape[1] // 128
        prow = A.alloc((nch * 128,), F32, parts=nrow)
        for j, r in enumerate(rows):
            S.dma("sp", prow[j:j + 1, :], r)
        ps = self.bank(0)
        for c in range(nch):
            S.tr(ps[:, c * nrow:(c + 1) * nrow], prow[0:nrow, c * 128:(c + 1) * 128], self.identf[0:nrow, 0:nrow])
        S.copy("dve", pcol, ps[:, 0:nch * nrow].rearrange("p (c r) -> p c r", r=nrow))
        A.release(m)

    def mixer(self, l):
        if l % 2 == 0:
            self.gdn_mla(l)
        else:
            self.rglru(l)

    def rglru(self, l):
        S, A = self.S, self.A
        o = l // 2
        dr = self.dram
        m0 = A.mark()
        w_in = dr["o_w_in"][o].rearrange("(kc kp) n -> kp kc n", kp=128)
        gslot = A.alloc((D,), F32)
        hT = A.alloc((8, T), BF16)
        mT = A.alloc((8, T), BF16)
        junk = A.alloc((D,), BF16)
        pcol = A.alloc((8, 8), F32)
        c1 = A.alloc((8,), F32)
        c2 = A.alloc((8,), F32)
        self.tr_banks = (4, 5, 6, 7)
        self.load_gain(gslot, dr["norm_gains"][l, 0, :], 32.0)
        rows = [dr["o_conv_w"][o, j:j + 1, :] for j in range(4)]
        rows += [dr["o_conv_b"][o:o + 1, :], dr["o_gate_a_b"][o:o + 1, :], dr["o_gate_x_b"][o:o + 1, :],
                 dr["o_a_param"][o:o + 1, :]]
        self.load_cols(pcol, rows)
        S.act(c1, pcol[:, :, 7], AF.Exp, scale=-1.0)
        S.act(c1, c1, AF.Ln, bias=1.0)
        S.ts("dve", c2, c1, -16.0, None, ALU.mult)
        S.ts("dve", c1, c1, -8.0, None, ALU.mult)
        self.prenorm(list(range(16)), gslot, hT, junk)
        m1 = A.mark()
        S.tag = "lru.blocks"
        HT = 1024
        wx = [A.alloc((8, 256), BF16) for _ in range(2)]
        wy = [A.alloc((8, 256), BF16) for _ in range(2)]
        ga = [A.alloc((2, 256), BF16) for _ in range(2)]
        gx = [A.alloc((2, 256), BF16) for _ in range(2)]
        prev3 = A.alloc((8, 3), F32)
        carry = A.alloc((8,), F32)

        class U:
            pass
        units = []
        g1v = gslot.bitcast(BF16)
        for ui in range(2):
            u = U()
            u.xc = [A.alloc((HT,), F32) for _ in range(2)]
            u.xcb = A.alloc((2, HT), BF16)
            u.A1 = A.alloc((HT,), F32)
            u.A2 = A.alloc((HT,), F32)
            u.A3 = A.alloc((HT,), F32)
            u.G1 = g1v[:, ui * HT:(ui + 1) * HT]
            units.append(u)

        def load_blk(n):
            sl = n % 2
            S.dma("pool", wx[sl], w_in[:, :, n * 256:(n + 1) * 256])
            S.dma("pool", wy[sl], w_in[:, :, D + n * 256: D + (n + 1) * 256])
            S.dma("pool", ga[sl], dr["o_gate_a_w"][o, n].rearrange("(dc dp) e -> dp dc e", dp=128))
            S.dma("pool", gx[sl], dr["o_gate_x_w"][o, n].rearrange("(dc dp) e -> dp dc e", dp=128))

        def unit(n, hf, k):
            u = units[k % 2]
            sl = n % 2
            tok0 = hf * HT
            pbase = 4 * (k % 2)
            X = [self.bank(pbase, 2), self.bank(pbase + 2, 2)]
            for c in range(2):
                for tb in range(2):
                    S.mm(X[c][:, tb * 512:(tb + 1) * 512],
                         [(wx[sl][:, kc, c * 128:(c + 1) * 128], hT[:, kc, tok0 + tb * 512: tok0 + (tb + 1) * 512])
                          for kc in range(8)])
            yield
            for c in range(2):
                ch = 2 * n + c
                xc = u.xc[c]
                S.ts("dve", xc, X[c], pcol[:, ch, 3:4], pcol[:, ch, 4:5], ALU.mult, ALU.add)
                for sh in (1, 2, 3):
                    S.stt("dve", xc[:, sh:], X[c][:, 0:HT - sh], pcol[:, ch, 3 - sh:4 - sh], xc[:, sh:], ALU.mult, ALU.add)
                if hf == 0:
                    S.copy("dve", prev3[:, ch, :], X[c][:, HT - 3:HT])
                else:
                    S.stt("dve", xc[:, 0:3], prev3[:, ch, 0:3], pcol[:, ch, 0:1], xc[:, 0:3], ALU.mult, ALU.add)
                    S.stt("dve", xc[:, 0:2], prev3[:, ch, 1:3], pcol[:, ch, 1:2], xc[:, 0:2], ALU.mult, ALU.add)
                    S.stt("dve", xc[:, 0:1], prev3[:, ch, 2:3], pcol[:, ch, 2:3], xc[:, 0:1], ALU.mult, ALU.add)
                S.copy("act", u.xcb[:, c, :], xc)
            yield
            for ec in range(2):
                ch = 2 * n + ec
                Ga = self.bank(pbase, 2)
                Gx = self.bank(pbase + 2, 2)
                for tb in range(2):
                    S.mm(Ga[:, tb * 512:(tb + 1) * 512],
                         [(ga[sl][:, dc, ec * 128:(ec + 1) * 128], u.xcb[:, dc, tb * 512:(tb + 1) * 512]) for dc in range(2)])
                for tb in range(2):
                    S.mm(Gx[:, tb * 512:(tb + 1) * 512],
                         [(gx[sl][:, dc, ec * 128:(ec + 1) * 128], u.xcb[:, dc, tb * 512:(tb + 1) * 512]) for dc in range(2)])
                yield
                S.act(u.A1, Ga, AF.Sigmoid, bias=pcol[:, ch, 5:6])
                S.act(u.A2, Gx, AF.Sigmoid, bias=pcol[:, ch, 6:7])
                Y = Ga
                for tb in range(2):
                    S.mm(Y[:, tb * 512:(tb + 1) * 512],
                         [(wy[sl][:, kc, ec * 128:(ec + 1) * 128], hT[:, kc, tok0 + tb * 512: tok0 + (tb + 1) * 512])
                          for kc in range(8)])
                S.act(u.A3, u.A1, AF.Exp, scale=c1[:, ch:ch + 1])
                S.act(u.A1, u.A1, AF.Exp, scale=c2[:, ch:ch + 1])
                S.act(u.G1, Y, AF.Gelu_apprx_tanh)
                yield
                S.ts("dve", u.A1, u.A1, -1.0, -1e-30, ALU.add, ALU.min)
                S.act(u.A1, u.A1, AF.Sqrt, scale=-1.0)
                yield
                S.tt("dve", u.A2, u.A2, u.A1, ALU.mult)
                S.tt("dve", u.A2, u.A2, u.xc[ec], ALU.mult)
                init = 0.0 if hf == 0 else carry[:, ch:ch + 1]
                S.scan(u.A1, u.A3, u.A2, init, ALU.mult, ALU.add)
                if hf == 0:
                    S.copy("dve", carry[:, ch:ch + 1], u.A1[:, HT - 1:HT])
                S.tt("dve", mT[:, ch, tok0:tok0 + HT], u.A1, u.G1, ALU.mult)
                if ec == 0:
                    yield
            if hf == 1 and n + 2 < 4:
                load_blk(n + 2)
        load_blk(0)
        load_blk(1)
        pipeline((unit(n, hf, 2 * n + hf) for n in range(4) for hf in range(2)), depth=2, skew=1)
        A.release(m1)
        wout = A.alloc((8, D), BF16)
        tmp = [A.alloc((D,), F32) for _ in range(2)]
        ss = A.alloc((16,), F32)
        rs = A.alloc((16,), F32)
        w_o = dr["o_w_out"][o].rearrange("(kc kp) n -> kp kc n", kp=128)
        S.dma("pool", wout[:, 0:4, :], w_o[:, 0:4, :])
        S.dma("pool", wout[:, 4:8, :], w_o[:, 4:8, :])
        self.load_gain(gslot, dr["norm_gains"][l, 1, :], 32.0)
        for i in range(16):
            py = self.bank(4 + 2 * (i % 2), 2)
            for nb in range(2):
                S.mm(py[:, nb * 512:(nb + 1) * 512],
                     [(mT[:, kc, i * 128:(i + 1) * 128], wout[:, kc, nb * 512:(nb + 1) * 512]) for kc in range(8)])
            self.postnorm_add(py, gslot, i, ss, rs, i, junk, tmp[i % 2])
        A.release(m0)

    def cload(self, name, shape, dtype, parts=128):
        t = self.A.alloc(shape, dtype, parts=parts)
        self.S.dma("sp", t, self.dram[name])
        return t

    def rope(self, x1, x2, cos, sin, o1, o2, rt):
        S = self.S
        S.tt("dve", rt[0], x1, cos, ALU.mult)
        S.tt("dve", rt[1], x2, sin, ALU.mult)
        S.tt("pool", o1, rt[0], rt[1], ALU.subtract)
        S.tt("dve", rt[2], x1, sin, ALU.mult)
        S.tt("dve", rt[3], x2, cos, ALU.mult)
        S.tt("dve", o2, rt[2], rt[3], ALU.add)

    def gdn_mla(self, l):
        S, A = self.S, self.A
        e = l // 2
        dr = self.dram
        m0 = A.mark()
        PI = float(np.pi)
        w_in = dr["e_w_in"][e].rearrange("(kc kp) n -> kp kc n", kp=128)
        gslot = A.alloc((D,), F32)
        junk = A.alloc((D,), BF16)
        mixT = A.alloc((8, T), BF16)
        ones_bf = self.cload("ones_bf", (128,), BF16)
        self.tr_banks = (4, 5, 6, 7)
        if not self.cfg.get("skip_mla"):
            self.mla(l, gslot, junk, mixT, ones_bf, w_in)
        else:
            S.memset("pool", mixT[:, 4:8, :], 0.0)
        if not self.cfg.get("skip_gdn"):
            self.gdn(l, gslot, junk, mixT, ones_bf, w_in)
        else:
            S.memset("pool", mixT[:, 0:4, :], 0.0)
        S.tag = "mix0.out"
        wout = A.alloc((8, D), BF16)
        tmp = [A.alloc((D,), F32) for _ in range(2)]
        ss = A.alloc((16,), F32)
        rs = A.alloc((16,), F32)
        w_o = dr["e_w_out"][e].rearrange("(kc kp) n -> kp kc n", kp=128)
        S.dma("pool", wout[:, 0:4, :], w_o[:, 0:4, :])
        S.dma("pool", wout[:, 4:8, :], w_o[:, 4:8, :])
        self.load_gain(gslot, dr["norm_gains"][l, 1, :], 32.0)
        for i in range(16):
            py = self.bank(4 + 2 * (i % 2), 2)
            for nb in range(2):
                S.mm(py[:, nb * 512:(nb + 1) * 512],
                     [(mixT[:, kc, i * 128:(i + 1) * 128], wout[:, kc, nb * 512:(nb + 1) * 512]) for kc in range(8)])
            self.postnorm_add(py, gslot, i, ss, rs, i, junk, tmp[i % 2])
        A.release(m0)

    def mla(self, l, gslot, junk, mixT, ones_bf, w_in):
        S, A = self.S, self.A
        e = l // 2
        dr = self.dram
        PI = float(np.pi)
        scale = 192.0 ** -0.5
        mm0 = A.mark()
        c_qnT = A.alloc((2, T), BF16)
        c_kvnT = A.alloc((2, T), BF16)
        kr = A.alloc((T,), BF16)
        S.memset("pool", kr[64:128, :], 0.0)
        cos = A.alloc((T,), F32)
        sin = A.alloc((T,), F32)
        pcoln = A.alloc((2, 2), F32)
        bigP = A.alloc((T,), BF16)
        bigPT = A.alloc((16, 128), BF16)
        rt = [bigP[:, 0:1024].bitcast(F32), bigP[:, 1024:2048].bitcast(F32),
              bigPT[:, 0:8, :].rearrange("p a b -> p (a b)").bitcast(F32),
              bigPT[:, 8:16, :].rearrange("p a b -> p (a b)").bitcast(F32)]
        negmask = self.cload("negmask_bf", (128,), BF16)
        self.load_cols(pcoln, [dr["e_q_norm"][e:e + 1, :], dr["e_kv_norm"][e:e + 1, :]])
        S.tag = "mla.rope"
        m1 = A.mark()
        invf = self.cload("inv_freq", (1,), F32, parts=32)
        posi = A.alloc((T,), I32)
        ang = A.alloc((T,), F32)
        tf = A.alloc((T,), F32)
        ti = A.alloc((T,), I32)
        S.dma("sp", posi[0:32, :], dr["positions"].partition_broadcast(32))
        S.copy("dve", ang[0:32, :], posi[0:32, :])
        S.ts("dve", ang[0:32, :], ang[0:32, :], invf[0:32, 0:1], None, ALU.mult)
        for dst, shift in ((sin, 0.0), (cos, PI / 2)):
            S.ts("dve", tf[0:32, :], ang[0:32, :], shift, 1.0 / (2 * PI), ALU.add, ALU.mult)
            S.copy("dve", ti[0:32, :], tf[0:32, :])
            S.copy("dve", tf[0:32, :], ti[0:32, :])
            S.stt("dve", tf[0:32, :], tf[0:32, :], -2 * PI, ang[0:32, :], ALU.mult, ALU.add)
            S.ts("dve", tf[0:32, :], tf[0:32, :], shift, 3.1415925, ALU.add, ALU.min)
            S.ts("dve", tf[0:32, :], tf[0:32, :], -3.1415925, None, ALU.max)
            S.act(dst[0:32, :], tf[0:32, :], AF.Sin)
        A.release(m1)
        S.tag = "mla.a"
        m1 = A.mark()
        hT = A.alloc((8, T), BF16)
        w576 = A.alloc((8, 576), BF16)
        sq = [A.alloc((512,), BF16) for _ in range(4)]
        rq = A.alloc((512,), F32)
        rkv = A.alloc((512,), F32)
        S.dma("pool", w576, w_in[:, :, 2056:2632])
        self.load_gain(gslot, dr["norm_gains"][l, 0, :], 32.0)
        self.prenorm(list(range(16)), gslot, hT, junk)
        for tb in range(4):
            tbs = slice(tb * 512, (tb + 1) * 512)
            for cc in range(4):
                S.mm(self.bank(cc), [(w576[:, kc, cc * 128:(cc + 1) * 128], hT[:, kc, tbs]) for kc in range(8)])
                S.act(sq[cc], self.bank(cc), AF.Square)
            S.mm(self.bank(4), [(ones_bf, sq[0]), (ones_bf, sq[1])])
            S.mm(self.bank(5), [(ones_bf, sq[2]), (ones_bf, sq[3])])
            S.act(rq, self.bank(4), AF.Ln, scale=1.0 / 256, bias=EPS)
            S.act(rq, rq, AF.Exp, scale=-0.5)
            S.act(rkv, self.bank(5), AF.Ln, scale=1.0 / 256, bias=EPS)
            S.act(rkv, rkv, AF.Exp, scale=-0.5)
            for cc in range(2):
                S.stt("dve", c_qnT[:, cc, tbs], self.bank(cc), pcoln[:, cc, 0:1], rq, ALU.mult, ALU.mult)
                S.stt("dve", c_kvnT[:, cc, tbs], self.bank(2 + cc), pcoln[:, cc, 1:2], rkv, ALU.mult, ALU.mult)
            x1 = self.bank(6)[0:32, :]
            x2 = self.bank(7)[0:32, :]
            S.mm(x1, [(w576[:, kc, 512:544], hT[:, kc, tbs]) for kc in range(8)])
            S.mm(x2, [(w576[:, kc, 544:576], hT[:, kc, tbs]) for kc in range(8)])
            self.rope(x1, x2, cos[0:32, tbs], sin[0:32, tbs], kr[0:32, tbs], kr[32:64, tbs],
                      [r[0:32, :] for r in rt])
        A.release(m1)
        S.tag = "mla.c"
        wuq = A.alloc((2, 768), BF16)
        wukv = A.alloc((2, 1024), BF16)
        S.dma("pool", wuq, dr["e_w_uq"][e].rearrange("(kc kp) n -> kp kc n", kp=128))
        S.dma("pool", wukv, dr["e_w_ukv"][e].rearrange("(kc kp) n -> kp kc n", kp=128))
        qn = [A.alloc((T,), BF16) for _ in range(2)]
        qr = [A.alloc((T,), BF16) for _ in range(2)]
        for t_ in qr:
            S.memset("pool", t_[64:128, :], 0.0)
        kn = [A.alloc((T,), BF16) for _ in range(2)]
        Vh = [A.alloc((16, 128), BF16) for _ in range(2)]

        class Res:
            pass
        big, small = Res(), Res()
        big.S = self.bank(0, 4)
        small.S = self.bank(4, 2)
        b6 = self.bank_bf(6)
        b7 = self.bank_bf(7)
        big.ptb = [b6[:, 0:512], b6[:, 512:1024]]
        small.ptb = [b7[:, 512:1024]]
        big.pso = self.bank(7)[:, 0:128]
        small.pso = self.bank(7)[:, 128:256]
        big.P, big.PT = bigP, bigPT
        small.P = A.alloc((1024,), BF16)
        small.PT = A.alloc((8, 128), BF16)
        for R in (big, small):
            R.mx = A.alloc((1,), F32)
            R.nb = A.alloc((1,), F32)
            R.sm = A.alloc((1,), F32)
            R.rinv = A.alloc((1,), F32)
        kk = [0]

        def proj(h):
            hb = h % 2
            for tb in range(4):
                tbs = slice(tb * 512, (tb + 1) * 512)

                def nbank():
                    b = self.bank(4 + kk[0] % 4)
                    kk[0] += 1
                    return b
                pb = nbank()
                S.mm(pb, [(wuq[:, kc, h * 192:h * 192 + 128], c_qnT[:, kc, tbs]) for kc in range(2)])
                S.copy("act", qn[hb][:, tbs], pb)
                pb = nbank()
                S.mm(pb, [(wukv[:, kc, h * 256:h * 256 + 128], c_kvnT[:, kc, tbs]) for kc in range(2)])
                S.copy("act", kn[hb][:, tbs], pb)
                x1 = nbank()[0:32, :]
                x2 = nbank()[0:32, :]
                S.mm(x1, [(wuq[:, kc, h * 192 + 128:h * 192 + 160], c_qnT[:, kc, tbs]) for kc in range(2)])
                S.mm(x2, [(wuq[:, kc, h * 192 + 160:h * 192 + 192], c_qnT[:, kc, tbs]) for kc in range(2)])
                self.rope(x1, x2, cos[0:32, tbs], sin[0:32, tbs], qr[hb][0:32, tbs], qr[hb][32:64, tbs],
                          [r[0:32, :] for r in rt])
                pb = nbank()
                for j in range(4):
                    t = tb * 4 + j
                    S.mm(pb[:, j * 128:(j + 1) * 128],
                         [(c_kvnT[:, kc, t * 128:(t + 1) * 128], wukv[:, kc, h * 256 + 128:h * 256 + 256]) for kc in range(2)])
                S.copy("dve", Vh[hb][:, tb * 4:(tb + 1) * 4, :], pb.rearrange("p (j d) -> p j d", d=128))

        def attn_block(h, i, R):
            hb = h % 2
            blk = slice(i * 128, (i + 1) * 128)
            nk = (i + 1) * 128
            Sb = R.S
            nb4 = (nk + 511) // 512
            for b4 in range(nb4):
                cols = slice(b4 * 512, min(nk, (b4 + 1) * 512))
                grp = [(qn[hb][:, blk], kn[hb][:, cols]),
                       (qr[hb][:, blk], kr[:, cols])]
                if b4 == nb4 - 1:
                    grp.append((Sb[:, blk], self.ident, negmask))
                S.mm(Sb[:, cols], grp)
            yield
            S.reduce("dve", R.mx, Sb[:, 0:nk], ALU.max)
            S.ts("dve", R.nb, R.mx, -scale, None, ALU.mult)
            Pi = R.P
            S.act(Pi[:, 0:nk], Sb[:, 0:nk], AF.Exp, bias=R.nb[:, 0:1], scale=scale, accum_out=R.sm[:, 0:1])
            S.op("dve", lambda hh: hh.reciprocal(out=R.rinv, in_=R.sm), [R.sm], [R.rinv])
            S.ts("dve", Pi[:, 0:nk], Pi[:, 0:nk], R.rinv[:, 0:1], None, ALU.mult)
            yield
            ng = (i + 4) // 4
            for g4 in range(ng):
                pt = R.ptb[g4 % len(R.ptb)]
                n4 = min(4, i + 1 - g4 * 4)
                for j in range(n4):
                    kb = g4 * 4 + j
                    S.tr(pt[:, j * 128:(j + 1) * 128], Pi[:, kb * 128:(kb + 1) * 128], self.ident)
                S.copy("act" if g4 % 2 == 0 else "dve", R.PT[:, g4 * 4:g4 * 4 + n4, :],
                       pt[:, 0:n4 * 128].rearrange("p (c t) -> p c t", t=128))
            yield
            S.mm(R.pso, [(Vh[hb][:, kb, :], R.PT[:, kb, :]) for kb in range(i + 1)])
            S.copy("act", mixT[:, 4 + h, blk], R.pso)

        proj(0)
        for h in range(4):
            if h + 1 < 4:
                proj(h + 1)
            order = []
            for j in range(8):
                order.append(attn_block(h, 8 + j, big))
                order.append(attn_block(h, j, small))
            pipeline(order, depth=2)
        A.release(mm0)

    def gdn(self, l, gslot, junk, mixT, ones_bf, w_in):
        S, A = self.S, self.A
        e = l // 2
        dr = self.dram
        mg0 = A.mark()
        qkvT = A.alloc((12, T), BF16)
        zs = A.alloc((16, 512), BF16)
        abraw = A.alloc((16, 8), F32)
        S.tag = "gdn.proj"
        m1 = A.mark()
        hTh = A.alloc((8, 1024), BF16)
        wsl = [A.alloc((8, 128), BF16) for _ in range(2)]
        wab = A.alloc((8, 8), BF16)
        sq = A.alloc((1024,), BF16)
        prev3 = A.alloc((12, 3), F32)
        pcolc = A.alloc((12, 4), F32)
        xcs = [mixT[:, 0, :].bitcast(F32), mixT[:, 1, :].bitcast(F32)]
        rsts = [mixT[:, 2, :].bitcast(F32), mixT[:, 3, :].bitcast(F32)]
        sqs = [sq, junk]
        self.load_cols(pcolc, [dr["e_conv_w"][e, j:j + 1, :] for j in range(4)])
        S.dma("pool", wab, w_in[:, :, 2048:2056])
        nload = [0]

        def wload(col0):
            sl = wsl[nload[0] % 2]
            nload[0] += 1
            S.dma("pool", sl, w_in[:, :, col0:col0 + 128])
            return sl
        qs = 128.0 ** -0.5
        for hf in range(2):
            tiles = list(range(hf * 8, hf * 8 + 8))
            hcol = slice(hf * 1024, (hf + 1) * 1024)
            self.load_gain(gslot, dr["norm_gains"][l, 0, :], 32.0)
            self.prenorm(tiles, gslot, hTh, junk)
            nxt_box = [wload(0)]

            def proj_chunk(c, hf=hf, hcol=hcol):
                p = c % 2
                w = nxt_box[0]
                nxt_box[0] = wload((c + 1) * 128) if c + 1 < 12 else wload(1536)
                X = self.bank(2 * p, 2)
                for tb in range(2):
                    S.mm(X[:, tb * 512:(tb + 1) * 512],
                         [(w[:, kc, :], hTh[:, kc, tb * 512:(tb + 1) * 512]) for kc in range(8)])
                yield
                xcp = xcs[p]
                S.ts("dve", xcp, X, pcolc[:, c, 3:4], None, ALU.mult)
                for sh in (1, 2, 3):
                    S.stt("dve", xcp[:, sh:], X[:, 0:1024 - sh], pcolc[:, c, 3 - sh:4 - sh], xcp[:, sh:], ALU.mult, ALU.add)
                if hf == 0:
                    S.copy("dve", prev3[:, c, :], X[:, 1021:1024])
                else:
                    S.stt("dve", xcp[:, 0:3], prev3[:, c, 0:3], pcolc[:, c, 0:1], xcp[:, 0:3], ALU.mult, ALU.add)
                    S.stt("dve", xcp[:, 0:2], prev3[:, c, 1:3], pcolc[:, c, 1:2], xcp[:, 0:2], ALU.mult, ALU.add)
                    S.stt("dve", xcp[:, 0:1], prev3[:, c, 2:3], pcolc[:, c, 2:3], xcp[:, 0:1], ALU.mult, ALU.add)
                yield
                if c >= 8:
                    S.act(qkvT[:, c, hcol], xcp, AF.Silu)
                    return
                S.act(xcp, xcp, AF.Silu)
                S.act(sqs[p], xcp, AF.Square)
                yield
                SS = self.bank(4 + 2 * p, 2)
                for tb in range(2):
                    S.mm(SS[:, tb * 512:(tb + 1) * 512], [(ones_bf, sqs[p][:, tb * 512:(tb + 1) * 512])])
                S.act(rsts[p], SS, AF.Ln, bias=EPS)
                S.act(rsts[p], rsts[p], AF.Exp, scale=-0.5)
                yield
                S.stt("dve", qkvT[:, c, hcol], xcp, (qs if c < 4 else 1.0), rsts[p], ALU.mult, ALU.mult)
            pipeline((proj_chunk(c) for c in range(12)), depth=2)
            nxt = nxt_box[0]
            for zc in range(4):
                w = nxt
                if zc + 1 < 4:
                    nxt = wload(1536 + (zc + 1) * 128)
                for tg in range(2):
                    pb = self.bank(6 + tg)
                    for j in range(4):
                        t = tg * 4 + j
                        S.mm(pb[:, j * 128:(j + 1) * 128],
                             [(hTh[:, kc, t * 128:(t + 1) * 128], w[:, kc, :]) for kc in range(8)])
                    S.act(zs[:, hf * 8 + tg * 4: hf * 8 + tg * 4 + 4, zc * 128:(zc + 1) * 128],
                          pb.rearrange("p (j d) -> p j d", d=128), AF.Silu)
            pb = self.bank(4)
            for t in range(8):
                S.mm(pb[:, t * 8:(t + 1) * 8], [(hTh[:, kc, t * 128:(t + 1) * 128], wab[:, kc, :]) for kc in range(8)])
            S.copy("dve", abraw[:, hf * 8:(hf + 1) * 8, :], pb[:, 0:64].rearrange("p (t c) -> p t c", c=8))
        A.release(m1)
        S.tag = "gdn.gates"
        g_all = A.alloc((16, 4), F32)
        beta_all = A.alloc((16, 4), F32)
        alog = A.alloc((4,), F32)
        dtb = A.alloc((4,), F32)
        onb = A.alloc((128,), F32)
        S.dma("sp", alog, dr["e_a_log"][e].partition_broadcast(128))
        S.dma("sp", dtb, dr["e_dt_bias"][e].partition_broadcast(128))
        S.dma("sp", onb, dr["e_o_norm"][e].partition_broadcast(128))
        S.act(alog, alog, AF.Exp)
        S.ts("dve", alog, alog, -1.0, None, ALU.mult)
        S.tt("dve", g_all, abraw[:, :, 0:4], dtb.unsqueeze(1).to_broadcast([128, 16, 4]), ALU.add)
        S.act(g_all, g_all, AF.Exp)
        S.act(g_all, g_all, AF.Ln, bias=1.0)
        S.tt("dve", g_all, g_all, alog.unsqueeze(1).to_broadcast([128, 16, 4]), ALU.mult)
        S.act(beta_all, abraw[:, :, 4:8], AF.Sigmoid)
        S.tag = "gdn.core"
        triu = self.cload("triu_f32", (128,), F32)
        ones_f = self.cload("ones_f32", (128,), F32)
        maskL = self.cload("maskL_f32", (128,), F32)
        maskA = self.cload("maskA_f32", (128,), F32)
        same01 = self.cload("same01_bf", (128,), BF16)
        off01 = self.cload("off01_bf", (128,), BF16)
        H4 = (4, 128)
        ngtri = A.alloc(H4, F32)
        gbc = A.alloc(H4, F32)
        decL = A.alloc(H4, F32)
        decA = A.alloc(H4, F32)
        egcR = A.alloc(H4, F32)
        Lb = A.alloc(H4, BF16)
        Lo = A.alloc(H4, BF16)
        Pbuf = [A.alloc(H4, BF16) for _ in range(2)]
        Qbuf = [A.alloc(H4, BF16) for _ in range(2)]
        Ybuf = [A.alloc(H4, BF16) for _ in range(2)]
        U0 = A.alloc(H4, BF16)
        aqkT = A.alloc(H4, BF16)
        kbg = A.alloc(H4, BF16)
        kdec = A.alloc(H4, BF16)
        vb = A.alloc(H4, BF16)
        u_sb = A.alloc(H4, F32)
        wT = A.alloc(H4, BF16)
        qgT = A.alloc(H4, BF16)
        vnew = A.alloc(H4, BF16)
        S_f = A.alloc(H4, F32)
        S_b = A.alloc(H4, BF16)
        outa = A.alloc(H4, BF16)
        sqo = gslot[:, 0:512].rearrange("p (h d) -> p h d", d=128)
        on = gslot[:, 512:1024].rearrange("p (h d) -> p h d", d=128)
        gz = junk.bitcast(F32).rearrange("p (h d) -> p h d", d=128)
        ng = A.alloc((4,), F32)
        gcs = A.alloc((4,), F32)
        egc = A.alloc((4,), F32)
        bg = A.alloc((4,), F32)
        edec = A.alloc((4,), F32)
        ssn = A.alloc((4,), F32)
        rsn = A.alloc((4,), F32)
        S.memset("dve", S_f, 0.0)
        S.memset("dve", S_b, 0.0)

        def b4(b):
            return self.bank(b).rearrange("p (h d) -> p h d", d=128)

        def bc_h(t2):
            return t2.unsqueeze(1).to_broadcast([128, 4, 128])

        def bc_d(t4):
            return t4.unsqueeze(2).to_broadcast([128, 4, 128])
        identb = self.ident
        for i in range(self.cfg.get("gdn_blocks", 16)):
            blk = slice(i * 128, (i + 1) * 128)
            g_i = g_all[:, i, :]
            beta_i = beta_all[:, i, :]
            qTb = qkvT[:, 0:4, blk]
            kTb = qkvT[:, 4:8, blk]
            vTb = qkvT[:, 8:12, blk]
            S.ts("dve", ng, g_i, -1.0, None, ALU.mult)
            S.tt("dve", ngtri, bc_h(triu), bc_d(ng), ALU.mult)
            S.copy("dve", gbc, bc_d(g_i))
            Dm = b4(0)
            Rn = b4(1)
            for h in range(4):
                S.mm(Dm[:, h, :], [(triu, gbc[:, h, :]), (ones_f, ngtri[:, h, :])])
            for h in range(4):
                S.mm(Rn[:, h, :], [(ones_f, ngtri[:, h, :])])
            gcp = self.bank(2)[:, 0:4]
            S.mm(gcp, [(triu, g_i)])
            S.tt("dve", decL, Dm, bc_h(maskL), ALU.add)
            S.act(decL, decL, AF.Exp)
            S.tt("dve", decA, bc_h(maskA), Dm, ALU.subtract)
            S.act(decA, decA, AF.Exp)
            S.act(egcR, Rn, AF.Exp, scale=-1.0)
            S.act(egc, gcp, AF.Exp)
            S.act(edec, Dm[:, :, 127], AF.Exp, scale=-1.0)
            S.tt("dve", bg, beta_i, egc, ALU.mult)
            if self.cfg.get("gdn_cut", 99) <= 1:
                continue
            KK = b4(3)
            QK = b4(4)
            for h in range(4):
                S.mm(KK[:, h, :], [(kTb[:, h, :], kTb[:, h, :])])
            for h in range(4):
                S.mm(QK[:, h, :], [(kTb[:, h, :], qTb[:, h, :])])
            for h in range(4):
                S.stt("dve", Lb[:, h, :], KK[:, h, :], beta_i[:, h:h + 1], decL[:, h, :], ALU.mult, ALU.mult)
            S.tt("dve", aqkT, QK, decA, ALU.mult)
            if self.cfg.get("gdn_cut", 99) <= 2:
                continue
            tb5 = self.bank_bf(5)
            for h in range(4):
                S.tr(tb5[:, h * 128:(h + 1) * 128], kTb[:, h, :], identb)
            for h in range(4):
                S.tr(tb5[:, (4 + h) * 128:(5 + h) * 128], vTb[:, h, :], identb)
            kTM = tb5[:, 0:512].rearrange("p (h d) -> p h d", d=128)
            vTM = tb5[:, 512:1024].rearrange("p (h d) -> p h d", d=128)
            S.tt("dve", kbg, kTM, bc_d(bg), ALU.mult)
            S.tt("dve", kdec, kTM, bc_d(edec), ALU.mult)
            S.tt("dve", vb, vTM, bc_d(beta_i), ALU.mult)
            if self.cfg.get("gdn_cut", 99) <= 3:
                continue
            S.tt("dve", Lo, Lb, bc_h(off01), ALU.mult)
            S.tt("dve", Lb, Lb, bc_h(same01), ALU.mult)
            tb6 = self.bank_bf(6)
            for h in range(4):
                S.tr(tb6[:, h * 128:(h + 1) * 128], Lb[:, h, :], identb)
            S.copy("act", U0, tb6[:, 0:512].rearrange("p (h d) -> p h d", d=128))
            Yc = Ybuf[0]
            S.tt("dve", Yc, bc_h(identb), U0, ALU.subtract)
            Pc, Qc = Lb, U0
            for j in range(1, 6):
                PB = b4(3)
                for h in range(4):
                    S.mm(PB[:, h, :], [(Qc[:, h, :], Pc[:, h, :])])
                Pn = Pbuf[j % 2]
                S.copy("act", Pn, PB)
                if j < 5:
                    QB = b4(4)
                    for h in range(4):
                        S.mm(QB[:, h, :], [(Pc[:, h, :], Qc[:, h, :])])
                    Qn = Qbuf[j % 2]
                    S.copy("act", Qn, QB)
                else:
                    Qn = None
                YB = b4(7)
                for h in range(4):
                    S.mm(YB[:, h, :], [(Pn[:, h, :], Yc[:, h, :])])
                Yn = Ybuf[j % 2]
                S.tt("dve", Yn, YB, Yc, ALU.add)
                Pc, Qc, Yc = Pn, Qn, Yn
            Ydt = Yc
            Td = Qbuf[1]
            tb6 = self.bank_bf(6)
            for h in range(4):
                S.tr(tb6[:, h * 128:(h + 1) * 128], Ydt[:, h, :], identb)
            S.copy("act", Td, tb6[:, 0:512].rearrange("p (h d) -> p h d", d=128))
            AB = b4(3)
            for h in range(4):
                S.mm(AB[:, h, :], [(Lo[:, h, :], Ydt[:, h, :])])
            Ab = Pbuf[0]
            S.copy("act", Ab, AB)
            TB = b4(4)
            for h in range(4):
                S.mm(TB[:, h, :], [(Td[:, h, :], Ab[:, h, :])])
            Tt = Ybuf[0]
            S.tt("dve", Tt, Ydt, TB, ALU.subtract)
            if self.cfg.get("gdn_cut", 99) <= 4:
                continue
            UB = b4(2)
            for h in range(4):
                S.mm(UB[:, h, :], [(Tt[:, h, :], vb[:, h, :])])
            S.copy("act", u_sb, UB)
            WB = b4(3)
            for h in range(4):
                S.mm(WB[:, h, :], [(kbg[:, h, :], Tt[:, h, :])])
            S.copy("act", wT, WB)
            S.tt("dve", qgT, qTb, egcR, ALU.mult)
            if self.cfg.get("gdn_cut", 99) <= 5:
                continue
            sub = self.cfg.get("gdn_sub", 99)
            P1 = b4(4)
            for h in range(4):
                S.mm(P1[:, h, :], [(wT[:, h, :], S_b[:, h, :])])
            if sub <= 1:
                continue
            S.tt("dve", vnew, u_sb, P1, ALU.subtract)
            if sub <= 2:
                continue
            PO = b4(2)
            for h in range(4):
                S.mm(PO[:, h, :], [(qgT[:, h, :], S_b[:, h, :]), (aqkT[:, h, :], vnew[:, h, :])])
            if sub <= 3:
                continue
            PS = b4(7)
            for h in range(4):
                S.mm(PS[:, h, :], [(kdec[:, h, :], vnew[:, h, :])])
            if sub <= 4:
                continue
            S.tt("dve", S_f, S_f, egcR[:, :, 127:128].to_broadcast([128, 4, 128]), ALU.mult)
            S.tt("dve", S_f, PS, S_f, ALU.add)
            if sub <= 5:
                continue
            S.copy("act", S_b, S_f)
            if self.cfg.get("gdn_cut", 99) <= 6:
                continue
            S.act(sqo, PO, AF.Square)
            S.reduce("dve", ssn, sqo, ALU.add)
            S.act(rsn, ssn, AF.Ln, scale=1.0 / 128, bias=EPS)
            S.act(rsn, rsn, AF.Exp, scale=-0.5)
            S.tt("dve", on, PO, bc_d(rsn), ALU.mult)
            S.tt("pool", gz, zs[:, i, :].rearrange("p (h d) -> p h d", d=128), bc_h(onb), ALU.mult)
            S.tt("dve", outa, on, gz, ALU.mult)
            tb5 = self.bank_bf(5)
            for h in range(4):
                S.tr(tb5[:, h * 128:(h + 1) * 128], outa[:, h, :], identb)
            S.copy("act", mixT[:, 0:4, blk], tb5[:, 0:512].rearrange("p (h d) -> p h d", d=128))
        A.release(mg0)

    def build(self):
        cfg = self.cfg
        S, A = self.S, self.A
        x_in = self.din("x", (T, D))
        self.din("norm_gains", (2, 6, D))
        self.din("ffn_w_in", (2, D, 2 * DFF))
        self.din("ffn_w_out", (2, DFF, D))
        self.din("mem", (256, D))
        self.din("mem_norm", (D,))
        self.din("xa_wq", (2, D, D))
        self.din("xa_wkv", (2, D, 2 * D))
        self.din("xa_wo", (2, D, D))
        self.din("o_w_in", (1, D, 2 * D))
        self.din("o_conv_w", (1, 4, D))
        self.din("o_conv_b", (1, D))
        self.din("o_gate_a_w", (1, 4, 256, 256))
        self.din("o_gate_a_b", (1, D))
        self.din("o_gate_x_w", (1, 4, 256, 256))
        self.din("o_gate_x_b", (1, D))
        self.din("o_a_param", (1, D))
        self.din("o_w_out", (1, D, D))
        identf_in = self.din("ident_f32", (128, 128), F32)
        self.din("positions", (T,), I32)
        self.din("e_w_in", (1, D, E_IN))
        self.din("e_conv_w", (1, 4, 1536))
        self.din("e_a_log", (1, 4))
        self.din("e_dt_bias", (1, 4))
        self.din("e_o_norm", (1, 128))
        self.din("e_q_norm", (1, 256))
        self.din("e_kv_norm", (1, 256))
        self.din("e_w_uq", (1, 256, 768))
        self.din("e_w_ukv", (1, 256, 1024))
        self.din("e_w_out", (1, D, D))
        self.din("ones_bf", (128, 128), BF16)
        self.din("negmask_bf", (128, 128), BF16)
        self.din("inv_freq", (32, 1), F32)
        self.din("ones_f32", (128, 128), F32)
        self.din("triu_f32", (128, 128), F32)
        self.din("maskL_f32", (128, 128), F32)
        self.din("maskA_f32", (128, 128), F32)
        self.din("same01_bf", (128, 128), BF16)
        self.din("off01_bf", (128, 128), BF16)
        ident_in = self.din("ident_bf", (128, 128), BF16)
        y_out = self.dout("y", (T, D))

        self.x = A.alloc((NT, D), F32)
        self.ident = A.alloc((128,), BF16)
        S.dma("sp", self.ident, ident_in)
        self.identf = A.alloc((128,), F32)
        S.dma("sp", self.identf, identf_in)
        xin_v = x_in.rearrange("(t p) d -> p t d", p=128)
        for q in range(4):
            S.dma("sp", self.x[:, q * 4:(q + 1) * 4, :], xin_v[:, q * 4:(q + 1) * 4, :])

        for step in cfg["steps"]:
            kind, l = step
            S.tag = "%s%d" % (kind, l)
            if kind == "ffn":
                self.ffn(l)
            elif kind == "xa":
                self.xattn(l)
            elif kind == "mix":
                self.mixer(l)

        yv = y_out.rearrange("(t p) d -> p t d", p=128)
        for q in range(4):
            S.dma("sp", yv[:, q * 4:(q + 1) * 4, :], self.x[:, q * 4:(q + 1) * 4, :])
        S.final_wait("sp")
        n = S.emit()
        self.n_ins = n
        self.es.close()
        return self.nc


FULL_STEPS = [("mix", 0), ("xa", 0), ("ffn", 0), ("mix", 1), ("xa", 1), ("ffn", 1)]

_CONST = {}


def consts():
    if not _CONST:
        _CONST["ident_bf"] = np.eye(128, dtype=np.float32).astype(ml_dtypes.bfloat16)
        _CONST["ident_f32"] = np.eye(128, dtype=np.float32)
        _CONST["ones_bf"] = np.ones((128, 128), np.float32).astype(ml_dtypes.bfloat16)
        _CONST["ones_f32"] = np.ones((128, 128), np.float32)
        qi = np.arange(128)[:, None]
        ki = np.arange(128)[None, :]
        _CONST["negmask_bf"] = np.where((qi < 64) & (ki >= 64), NEG, 0.0).astype(np.float32).astype(ml_dtypes.bfloat16)
        _CONST["inv_freq"] = (np.float32(10000.0) ** (-np.arange(0, 64, 2, dtype=np.float32) / np.float32(64))
                              ).astype(np.float32).reshape(32, 1)
        _CONST["triu_f32"] = (qi <= ki).astype(np.float32)
        _CONST["maskL_f32"] = np.where(qi > ki, 0.0, NEG).astype(np.float32)
        _CONST["maskA_f32"] = np.where(ki >= qi, 0.0, NEG).astype(np.float32)
        same = (qi // 64) == (ki // 64)
        _CONST["same01_bf"] = same.astype(np.float32).astype(ml_dtypes.bfloat16)
        _CONST["off01_bf"] = (~same).astype(np.float32).astype(ml_dtypes.bfloat16)
    return _CONST


def kernel(**inputs):
    cfg = {"steps": FULL_STEPS}
    return run(cfg, inputs)


def run(cfg, inputs, trace=False):
    p = Prog(cfg)
    nc = p.build()
    c = consts()
    names = [n for n in p.dram if n != "y"]
    in_maps = []
    for b in range(8):
        m = {}
        for n in names:
            if n in c:
                m[n] = c[n]
            elif n in ("x", "mem", "positions"):
                m[n] = np.ascontiguousarray(inputs[n][b])
            else:
                m[n] = np.ascontiguousarray(inputs[n])
        in_maps.append(m)
    res = run_bass_kernel_spmd(nc, in_maps, core_ids=list(range(8)), trace=trace)
    out = np.stack([np.asarray(r["y"]) for r in res.results], axis=0)
    if trace:
        return out, res
    return out
```

```python
import numpy as np
from contextlib import ExitStack
import ml_dtypes
import concourse.bass as bass
import concourse.mybir as mybir
from concourse.bass_utils import run_bass_kernel_spmd

F32 = mybir.dt.float32
BF16 = mybir.dt.bfloat16
I32 = mybir.dt.int32
U8 = mybir.dt.uint8
AF = mybir.ActivationFunctionType
ALU = mybir.AluOpType
AX = mybir.AxisListType
DSZ = {F32: 4, BF16: 2, I32: 4, U8: 1}

T = 2048
D = 1024
NT = 16
DFF = 2816
NFC = 22
EPS = 1e-6
E_IN = 2632
SB_BYTES = 207 * 1024
PS_BYTES = 16 * 1024
NDMA = 24
NEG = -1.0e30


class _Rec:
    __slots__ = ("p0", "p1", "lo", "hi", "writer", "readers", "pages")


class Sched:
    ENG = ("pe", "act", "dve", "pool", "sp")

    def __init__(self, nc, es):
        self.nc = nc
        self.h = {"pe": nc.tensor, "act": nc.scalar, "dve": nc.vector,
                  "pool": nc.gpsimd, "sp": nc.sync}
        self.sem = {e: es.enter_context(nc.semaphore("sem_" + e)) for e in self.ENG}
        self.dsem = [es.enter_context(nc.semaphore("dsem%d" % i)) for i in range(NDMA)]
        self.dcnt = [0] * NDMA
        self.dnext = 0
        self.dnext_q = {}
        self.ops = {e: [] for e in self.ENG}
        self.known = {e: {} for e in self.ENG}
        self.pages = {"arena": {}, "psum": {}}
        self.pbytes = {"arena": SB_BYTES, "psum": PS_BYTES}
        self.pgsz = {"arena": 1024, "psum": 512}
        self.extra = {}
        self.tag = ""
        self.annot = False

    def regions(self, ap):
        name = ap.tensor.name
        if name not in self.pbytes:
            return []
        esz = DSZ[ap.dtype]
        pb = self.pbytes[name]
        offb = ap.offset * esz
        p0 = offb // pb
        lo = offb % pb
        dims = list(ap.ap)
        npart = dims[0][1]
        free = [(abs(s), c) for s, c in dims[1:] if c > 1]
        out = []

        def rec(base, fd):
            if not fd:
                out.append((name, p0, p0 + npart, base, base + esz))
                return
            ext_in = 1 + sum((c - 1) * s for s, c in fd[1:])
            s0, c0 = fd[0]
            if len(fd) >= 2 and s0 > ext_in and c0 <= 64 and len(out) < 256:
                for i in range(c0):
                    rec(base + i * s0 * esz, fd[1:])
            else:
                ext = 1 + sum((c - 1) * s for s, c in fd)
                out.append((name, p0, p0 + npart, base, base + ext * esz))

        free.sort(key=lambda t: -t[0])
        rec(lo, free)
        if name == "psum":
            banks = sorted({b for (_, _, _, l, h) in out for b in range(l // 2048, (h - 1) // 2048 + 1)})
            out = [(name, p0, p0 + npart, b * 2048, (b + 1) * 2048) for b in banks]
        return out

    def _overl(self, r):
        name, p0, p1, lo, hi = r
        pg = self.pgsz[name]
        pages = self.pages[name]
        seen = set()
        res = []
        for k in range(lo // pg, (hi - 1) // pg + 1):
            for rc in pages.get(k, ()):
                if id(rc) in seen:
                    continue
                seen.add(id(rc))
                if rc.lo < hi and lo < rc.hi and rc.p0 < p1 and p0 < rc.p1:
                    res.append(rc)
        return res

    def _add_rec(self, r, writer):
        name, p0, p1, lo, hi = r
        pg = self.pgsz[name]
        rc = _Rec()
        rc.p0, rc.p1, rc.lo, rc.hi = p0, p1, lo, hi
        rc.writer = writer
        rc.readers = {}
        rc.pages = (name, lo // pg, (hi - 1) // pg)
        pages = self.pages[name]
        for k in range(rc.pages[1], rc.pages[2] + 1):
            pages.setdefault(k, []).append(rc)
        return rc

    def _del_rec(self, rc):
        name, k0, k1 = rc.pages
        pages = self.pages[name]
        for k in range(k0, k1 + 1):
            pages[k].remove(rc)

    def _deps(self, reads, writes):
        deps = {}

        def add(src, val):
            if deps.get(src, -1) < val:
                deps[src] = val
        rregs = []
        for ap in reads:
            rregs.extend(self.regions(ap))
        wregs = []
        for ap in writes:
            wregs.extend(self.regions(ap))
        for r in rregs:
            for rc in self._overl(r):
                if rc.writer is not None:
                    add(*rc.writer)
        for r in wregs:
            for rc in self._overl(r):
                if rc.writer is not None:
                    add(*rc.writer)
                for s, v in rc.readers.items():
                    add(s, v)
        return deps, rregs, wregs

    def _commit(self, me, rregs, wregs):
        for r in rregs:
            ov = self._overl(r)
            if not ov:
                rc = self._add_rec(r, None)
                ov = [rc]
            for rc in ov:
                if rc.readers.get(me[0], -1) < me[1]:
                    rc.readers[me[0]] = me[1]
        for r in wregs:
            name, p0, p1, lo, hi = r
            for rc in self._overl(r):
                if rc.lo >= lo and rc.hi <= hi and rc.p0 >= p0 and rc.p1 <= p1:
                    self._del_rec(rc)
            self._add_rec(r, me)

    def op(self, e, fn, reads=(), writes=()):
        deps, rregs, wregs = self._deps(reads, writes)
        if e == "pe":
            deps.pop("pe", None)
        waits = []
        kn = self.known[e]
        for src, val in deps.items():
            if kn.get(src, -1) < val:
                kn[src] = val
                waits.append((src, val))
        idx = len(self.ops[e])
        self.ops[e].append([fn, waits, False, None, self.tag])
        self._commit((e, idx), rregs, wregs)

    def dma(self, q, out, in_, reads=None, writes=None):
        reads = [in_] if reads is None else reads
        writes = [out] if writes is None else writes
        deps, rregs, wregs = self._deps(reads, writes)
        lo, hi = (0, 8) if q == "sp" else (8, NDMA)
        cur = self.dnext_q.get(q, lo)
        slot = cur
        self.dnext_q[q] = lo + (cur + 1 - lo) % (hi - lo)
        src = ("d", slot)
        prev = self.dcnt[slot]
        if prev > 0:
            if deps.get(src, -1) < prev:
                deps[src] = prev
        self.dcnt[slot] = prev + 16
        waits = []
        kn = self.known[q]
        for s, val in deps.items():
            if kn.get(s, -1) < val:
                kn[s] = val
                waits.append((s, val))
        self.ops[q].append([lambda h: h.dma_start(out=out, in_=in_), waits, False, slot, self.tag])
        self._commit((src, self.dcnt[slot]), rregs, wregs)

    def final_wait(self, e):
        waits = [(("d", s), self.dcnt[s]) for s in range(NDMA) if self.dcnt[s] > 0]
        self.ops[e].append([None, waits, False, None, "final"])

    def emit(self):
        for e in self.ENG:
            for o in self.ops[e]:
                for src, val in o[1]:
                    if not isinstance(src, tuple):
                        self.ops[src][val][2] = True
        rank = {}
        for e in self.ENG:
            r = 0
            rk = []
            for o in self.ops[e]:
                if o[2]:
                    r += 1
                rk.append(r)
            rank[e] = rk
        n_ins = 0
        for e in self.ENG:
            h = self.h[e]
            for o in self.ops[e]:
                fn, waits, needed, slot, tag = o[:5]
                for src, val in waits:
                    if isinstance(src, tuple):
                        h.wait_ge(self.dsem[src[1]], val)
                    else:
                        h.wait_ge(self.sem[src], rank[src][val])
                    n_ins += 1
                if fn is None:
                    continue
                ins = fn(h)
                n_ins += 1
                if self.annot and tag:
                    ins.annotate(tag)
                if slot is not None:
                    ins.then_inc(self.dsem[slot], 16)
                elif needed:
                    ins.then_inc(self.sem[e], 1)
        return n_ins

    def mm(self, out, pairs, start=True, stop=True):
        n = len(pairs)
        pairs = [(p if len(p) == 3 else (out, p[0], p[1])) for p in pairs]
        reads = []
        for o, a, b in pairs:
            reads += [a, b]

        def fn(h):
            ins = None
            for i, (o, a, b) in enumerate(pairs):
                ins = h.matmul(o, a, b, start=(start and i == 0), stop=(stop and i == n - 1))
            return ins
        rd = reads if start else reads + [out]
        self.op("pe", fn, rd, [out])

    def tr(self, out, in_, ident):
        self.op("pe", lambda h: h.transpose(out, in_, ident), [in_, ident], [out])

    def act(self, out, in_, func, bias=None, scale=None, accum_out=None, eng="act"):
        reads = [in_]
        kw = {}
        if bias is not None:
            kw["bias"] = bias
            if not isinstance(bias, (int, float)):
                reads.append(bias)
        if scale is not None:
            kw["scale"] = scale
            if not isinstance(scale, (int, float)):
                reads.append(scale)
        writes = [out]
        if accum_out is not None:
            kw["accum_out"] = accum_out
            writes.append(accum_out)
        self.op("act", lambda h: h.activation(out=out, in_=in_, func=func, **kw), reads, writes)

    def tt(self, e, out, in0, in1, op):
        self.op(e, lambda h: h.tensor_tensor(out=out, in0=in0, in1=in1, op=op), [in0, in1], [out])

    def ts(self, e, out, in0, s1, s2, op0, op1=None):
        reads = [in0]
        for s in (s1, s2):
            if s is not None and not isinstance(s, (int, float)):
                reads.append(s)
        if op1 is None:
            self.op(e, lambda h: h.tensor_scalar(out=out, in0=in0, scalar1=s1, scalar2=None, op0=op0),
                    reads, [out])
        else:
            self.op(e, lambda h: h.tensor_scalar(out=out, in0=in0, scalar1=s1, scalar2=s2, op0=op0, op1=op1),
                    reads, [out])

    def stt(self, e, out, in0, scalar, in1, op0, op1):
        reads = [in0, in1]
        if not isinstance(scalar, (int, float)):
            reads.append(scalar)
        self.op(e, lambda h: h.scalar_tensor_tensor(out=out, in0=in0, scalar=scalar, in1=in1, op0=op0, op1=op1),
                reads, [out])

    def copy(self, e, out, in_):
        if e == "act":
            self.op(e, lambda h: h.copy(out=out, in_=in_), [in_], [out])
        else:
            self.op(e, lambda h: h.tensor_copy(out=out, in_=in_), [in_], [out])

    def memset(self, e, out, val):
        self.op(e, lambda h: h.memset(out, val), [], [out])

    def reduce(self, e, out, in_, op, axis=AX.X):
        self.op(e, lambda h: h.tensor_reduce(out=out, in_=in_, axis=axis, op=op), [in_], [out])

    def scan(self, out, d0, d1, init, op0, op1):
        reads = [d0, d1]
        if not isinstance(init, (int, float)):
            reads.append(init)
        self.op("dve", lambda h: h.tensor_tensor_scan(out=out, data0=d0, data1=d1, initial=init, op0=op0, op1=op1),
                reads, [out])


def pipeline(gens, depth=2, skew=1):
    it = iter(gens)
    active = []
    exhausted = False
    while True:
        if not exhausted and len(active) < depth and (not active or active[-1][1] >= skew):
            try:
                active.append([next(it), 0])
            except StopIteration:
                exhausted = True
        if not active:
            if exhausted:
                break
            continue
        for a in list(active):
            try:
                next(a[0])
                a[1] += 1
            except StopIteration:
                active.remove(a)


class Arena:
    def __init__(self, base, nbytes):
        self.base = base
        self.nbytes = nbytes
        self.top = 0
        self.peak = 0

    def alloc(self, free_shape, dtype, parts=128):
        n = 1
        for s in free_shape:
            n *= s
        nb = n * DSZ[dtype]
        off = (self.top + 63) // 64 * 64
        assert off + nb <= self.nbytes, "arena overflow: need %d at %d (cap %d)" % (nb, off, self.nbytes)
        self.top = off + nb
        self.peak = max(self.peak, self.top)
        v = self.base[0:parts, off:off + nb]
        if dtype != U8:
            v = v.bitcast(dtype)
        if len(free_shape) > 1:
            names = " ".join("d%d" % i for i in range(len(free_shape)))
            kw = {"d%d" % i: free_shape[i] for i in range(1, len(free_shape))}
            v = v.rearrange("p (%s) -> p %s" % (names, names), **kw)
        return v

    def mark(self):
        return self.top

    def release(self, m):
        self.top = m


class Prog:
    def __init__(self, cfg):
        self.cfg = cfg
        self.nc = bass.Bass("TRN2", target_bir_lowering=False)
        self.es = ExitStack()
        nc = self.nc
        es = self.es
        self.dram = {}
        sb = es.enter_context(nc.sbuf_tensor("arena", [128, SB_BYTES], U8))
        ps = es.enter_context(nc.psum_tensor("psum", [128, PS_BYTES // 4], F32))
        self.S = Sched(nc, es)
        self.S.annot = bool(cfg.get("annot"))
        self.A = Arena(sb, SB_BYTES)
        self.ps = ps

    def din(self, name, shape, dtype=F32):
        t = self.nc.dram_tensor(name, list(shape), dtype, kind="ExternalInput").ap()
        self.dram[name] = t
        return t

    def dout(self, name, shape, dtype=F32):
        t = self.nc.dram_tensor(name, list(shape), dtype, kind="ExternalOutput").ap()
        self.dram[name] = t
        return t

    def bank(self, b, n=1):
        return self.ps[:, b * 512:(b + n) * 512]

    def bank_bf(self, b, n=1):
        return self.ps[:, b * 512:(b + n) * 512].bitcast(BF16)

    def load_gain(self, dst, src_row, mult):
        S = self.S
        S.dma("sp", dst, src_row.partition_broadcast(128))
        S.ts("dve", dst, dst, float(mult), None, ALU.mult)

    def rstd(self, out, ss, c):
        S = self.S
        S.act(out, ss, AF.Ln, bias=float(c))
        S.act(out, out, AF.Exp, scale=-0.5)

    def prenorm(self, tiles, gb, hT, junk, col0=0):
        S, A = self.S, self.A
        m = A.mark()
        n = len(tiles)
        ss = A.alloc((n,), F32)
        rs = A.alloc((n,), F32)
        nb = len(self.tr_banks)
        if A.top + nb * 2048 + 256 > A.nbytes:
            nb = 2
        hb = [A.alloc((D,), BF16) for _ in range(nb)]
        junk2 = hb[1]
        for i, t in enumerate(tiles):
            if i % 2 == 0:
                S.act(junk, self.x[:, t, :], AF.Square, accum_out=ss[:, i:i + 1])
            else:
                xt = self.x[:, t, :]
                S.op("dve", (lambda xt, i: (lambda h: h.scalar_tensor_tensor(
                    out=junk2, in0=xt, scalar=1.0, in1=xt, op0=ALU.mult, op1=ALU.mult,
                    accum_out=ss[:, i:i + 1])))(xt, i), [xt], [junk2, ss[:, i:i + 1]])
        self.rstd(rs, ss, D * EPS)
        for i, t in enumerate(tiles):
            xt = self.x[:, t, :]
            h = hb[i % nb]
            S.stt("dve", h, xt, rs[:, i:i + 1], gb, ALU.mult, ALU.mult)
            pb = self.bank_bf(self.tr_banks[i % nb])
            for c in range(8):
                S.tr(pb[:, c * 128:(c + 1) * 128], h[:, c * 128:(c + 1) * 128], self.ident)
            dst = hT[:, :, col0 + i * 128: col0 + (i + 1) * 128]
            src = pb.rearrange("p (c t) -> p c t", t=128)
            S.copy("act", dst, src)
        A.release(m)

    def postnorm_add(self, psy, gb, t, ss, rs, i, junk, tmp):
        S = self.S
        S.act(junk, psy, AF.Square, accum_out=ss[:, i:i + 1])
        self.rstd(rs[:, i:i + 1], ss[:, i:i + 1], D * EPS)
        S.stt("dve", tmp, psy, rs[:, i:i + 1], gb, ALU.mult, ALU.mult)
        xt = self.x[:, t, :]
        S.tt("pool", xt, xt, tmp, ALU.add)

    def ffn(self, l):
        S, A = self.S, self.A
        m0 = A.mark()
        w_in = self.dram["ffn_w_in"][l].rearrange("(kc kp) n -> kp kc n", kp=128)
        w_out = self.dram["ffn_w_out"][l].rearrange("(fc fp) n -> fp fc n", fp=128)
        gslot = A.alloc((D,), F32)
        wout = A.alloc((NFC, D), BF16)
        hT = A.alloc((8, 1024), BF16)
        actT = A.alloc((NFC, 1024), BF16)
        GC = 256
        NG = DFF // GC
        wg = [A.alloc((8, GC), BF16) for _ in range(2)]
        wu = [A.alloc((8, GC), BF16) for _ in range(2)]
        tmp = [A.alloc((D,), F32) for _ in range(2)]
        sg = [tmp[0][:, 0:512], tmp[0][:, 512:1024]]
        junk = A.alloc((D,), BF16)
        ss = A.alloc((16,), F32)
        rs = A.alloc((16,), F32)
        self.tr_banks = (0, 1, 2, 3)
        for half in range(2):
            tiles = list(range(half * 8, half * 8 + 8))

            def load_g(g):
                s = g % 2
                S.dma("pool", wg[s], w_in[:, :, g * GC:(g + 1) * GC])
                S.dma("pool", wu[s], w_in[:, :, DFF + g * GC: DFF + (g + 1) * GC])
            load_g(0)
            self.load_gain(gslot, self.dram["norm_gains"][l, 4, :], 32.0)
            self.prenorm(tiles, gslot, hT, junk)
            if half == 0:
                for q in range(2):
                    S.dma("pool", wout[:, q * 11:(q + 1) * 11, :], w_out[:, q * 11:(q + 1) * 11, :])
            k = 0
            for g in range(NG):
                if g + 1 < NG:
                    load_g(g + 1)
                s = g % 2
                for j in range(GC // 128):
                    fc = g * (GC // 128) + j
                    for tb in range(2):
                        pg = self.bank(0 + (k % 2) * 2)
                        pu = self.bank(1 + (k % 2) * 2)
                        S.mm(pg, [(wg[s][:, kc, j * 128:(j + 1) * 128], hT[:, kc, tb * 512:(tb + 1) * 512])
                                  for kc in range(8)])
                        S.mm(pu, [(wu[s][:, kc, j * 128:(j + 1) * 128], hT[:, kc, tb * 512:(tb + 1) * 512])
                                  for kc in range(8)])
                        sgb = sg[k % 2]
                        S.act(sgb, pg, AF.Silu)
                        S.tt("dve", actT[:, fc, tb * 512:(tb + 1) * 512], sgb, pu, ALU.mult)
                        k += 1
            self.load_gain(gslot, self.dram["norm_gains"][l, 5, :], 32.0)
            for i, t in enumerate(tiles):
                py = self.bank(4 + (i % 2) * 2, 2)
                for nb in range(2):
                    S.mm(py[:, nb * 512:(nb + 1) * 512],
                         [(actT[:, fc, i * 128:(i + 1) * 128], wout[:, fc, nb * 512:(nb + 1) * 512])
                          for fc in range(NFC)])
                self.postnorm_add(py, gslot, t, ss, rs, half * 8 + i, junk, tmp[i % 2])
        A.release(m0)

    def setup_mem(self):
        S, A = self.S, self.A
        self.mem_nT = A.alloc((8, 256), BF16)
        m = A.mark()
        memt = A.alloc((2, D), F32)
        gb = A.alloc((D,), F32)
        junk = A.alloc((D,), BF16)
        ss = A.alloc((2,), F32)
        hb = [A.alloc((D,), BF16) for _ in range(2)]
        S.dma("sp", memt, self.dram["mem"].rearrange("(t p) d -> p t d", p=128))
        self.load_gain(gb, self.dram["mem_norm"], 32.0)
        for t in range(2):
            S.act(junk, memt[:, t, :], AF.Square, accum_out=ss[:, t:t + 1])
        self.rstd(ss, ss, D * EPS)
        for t in range(2):
            S.stt("dve", hb[t], memt[:, t, :], ss[:, t:t + 1], gb, ALU.mult, ALU.mult)
            pb = self.bank_bf(t)
            for c in range(8):
                S.tr(pb[:, c * 128:(c + 1) * 128], hb[t][:, c * 128:(c + 1) * 128], self.ident)
            S.copy("act", self.mem_nT[:, :, t * 128:(t + 1) * 128], pb.rearrange("p (c t) -> p c t", t=128))
        A.release(m)

    def xattn(self, l):
        S, A = self.S, self.A
        m0 = A.mark()
        self.setup_mem()
        scale = 256.0 ** -0.5
        wq = self.dram["xa_wq"][l].rearrange("(kc kp) n -> kp kc n", kp=128)
        wkv = self.dram["xa_wkv"][l].rearrange("(kc kp) n -> kp kc n", kp=128)
        wo = self.dram["xa_wo"][l].rearrange("(kc kp) n -> kp kc n", kp=128)
        gslot = A.alloc((D,), F32)
        hT = A.alloc((8, T), BF16)
        qT = A.alloc((8, T), BF16)
        kT = A.alloc((8, 256), BF16)
        V = A.alloc((2, D), BF16)
        wb = [A.alloc((8, 512), BF16) for _ in range(2)]
        junk = A.alloc((D,), BF16)
        tmp = [A.alloc((D,), F32) for _ in range(2)]
        ss = A.alloc((16,), F32)
        rs = A.alloc((16,), F32)
        P = [A.alloc((4, 256), BF16) for _ in range(2)]
        PT = [A.alloc((8, 128), BF16) for _ in range(2)]
        mx = [A.alloc((4,), F32) for _ in range(2)]
        nb_ = [A.alloc((4,), F32) for _ in range(2)]
        sm = [A.alloc((4,), F32) for _ in range(2)]
        rinv = [A.alloc((4,), F32) for _ in range(2)]
        self.tr_banks = (0, 1, 2, 3)
        S.tag = "xa.kv"
        for g in range(4):
            S.dma("pool", wb[g % 2], wkv[:, :, g * 512:(g + 1) * 512])
            w = wb[g % 2]
            if g < 2:
                for j in range(4):
                    c = 4 * g + j
                    pb = self.bank(j % 2)
                    S.mm(pb[:, 0:256], [(w[:, kc, j * 128:(j + 1) * 128], self.mem_nT[:, kc, :]) for kc in range(8)])
                    S.copy("act" if j % 2 == 0 else "dve", kT[:, c, :], pb[:, 0:256])
            else:
                for mt in range(2):
                    pb = self.bank(mt)
                    S.mm(pb, [(self.mem_nT[:, kc, mt * 128:(mt + 1) * 128], w[:, kc, :]) for kc in range(8)])
                    S.copy("act" if mt == 0 else "dve", V[:, mt, (g - 2) * 512:(g - 1) * 512], pb)
        S.tag = "xa.pre"
        self.load_gain(gslot, self.dram["norm_gains"][l, 2, :], 32.0)
        S.dma("pool", wb[0], wq[:, :, 0:512])
        S.dma("pool", wb[1], wq[:, :, 512:1024])
        self.prenorm(list(range(16)), gslot, hT, junk)
        S.tag = "xa.q"
        k = 0
        for g in range(2):
            w = wb[g]
            for j in range(4):
                c = 4 * g + j
                for tb in range(4):
                    pb = self.bank(4 + k % 4)
                    S.mm(pb, [(w[:, kc, j * 128:(j + 1) * 128], hT[:, kc, tb * 512:(tb + 1) * 512]) for kc in range(8)])
                    S.copy("act" if k % 2 == 0 else "dve", qT[:, c, tb * 512:(tb + 1) * 512], pb)
                    k += 1
        S.dma("pool", wb[0], wo[:, :, 0:512])
        S.dma("pool", wb[1], wo[:, :, 512:1024])
        self.load_gain(gslot, self.dram["norm_gains"][l, 3, :], 32.0)
        oT = hT
        S.tag = "xa.attn"
        def xa_block(i):
            p = i % 2
            blk = slice(i * 128, (i + 1) * 128)
            psS = self.bank(2 * p, 2)
            for h in range(4):
                S.mm(psS[:, h * 256:(h + 1) * 256], [(qT[:, 2 * h + c, blk], kT[:, 2 * h + c, :]) for c in range(2)])
            yield
            S.reduce("dve", mx[p], psS.rearrange("p (h m) -> p h m", m=256), ALU.max)
            S.ts("dve", nb_[p], mx[p], -scale, None, ALU.mult)
            Pi = P[p]
            for h in range(4):
                S.act(Pi[:, h, :], psS[:, h * 256:(h + 1) * 256], AF.Exp, bias=nb_[p][:, h:h + 1], scale=scale,
                      accum_out=sm[p][:, h:h + 1])
            S.op("dve", lambda hh: hh.reciprocal(out=rinv[p], in_=sm[p]), [sm[p]], [rinv[p]])
            S.tt("dve", Pi, Pi, rinv[p].unsqueeze(2).to_broadcast([128, 4, 256]), ALU.mult)
            yield
            pt = self.bank_bf(4 + p)
            for h in range(4):
                for mc in range(2):
                    S.tr(pt[:, (h * 2 + mc) * 128:(h * 2 + mc + 1) * 128], Pi[:, h, mc * 128:(mc + 1) * 128], self.ident)
            PTi = PT[p]
            S.copy("act", PTi, pt.rearrange("p (c t) -> p c t", t=128))
            yield
            psO = self.bank(6, 2)
            for h in range(4):
                for c in range(2):
                    cc = 2 * h + c
                    S.mm(psO[:, cc * 128:(cc + 1) * 128],
                         [(V[:, mc, h * 256 + c * 128: h * 256 + (c + 1) * 128], PTi[:, h * 2 + mc, :]) for mc in range(2)])
            S.copy("dve", oT[:, :, blk], psO.rearrange("p (c t) -> p c t", t=128))
        pipeline((xa_block(i) for i in range(16)), depth=2)
        S.tag = "xa.out"
        for i in range(16):
            py = self.bank(2 * (i % 2), 2)
            for nb in range(2):
                S.mm(py[:, nb * 512:(nb + 1) * 512],
                     [(oT[:, kc, i * 128:(i + 1) * 128], wb[nb][:, kc, :]) for kc in range(8)])
            self.postnorm_add(py, gslot, i, ss, rs, i, junk, tmp[i % 2])
        A.release(m0)

    def load_cols(self, pcol, rows):
        S, A = self.S, self.A
        m = A.mark()
        nrow = len(rows)
        nch = rows[0].shape[1] // 128
        prow = A.alloc((nch * 128,), F32, parts=nrow)
        for j, r in enumerate(rows):
            S.dma("sp", prow[j:j + 1, :], r)
        ps = self.bank(0)
        for c in range(nch):
            S.tr(ps[:, c * nrow:(c + 1) * nrow], prow[0:nrow, c * 128:(c + 1) * 128], self.identf[0:nrow, 0:nrow])
        S.copy("dve", pcol, ps[:, 0:nch * nrow].rearrange("p (c r) -> p c r", r=nrow))
        A.release(m)

    def mixer(self, l):
        if l % 2 == 0:
            self.gdn_mla(l)
        else:
            self.rglru(l)

    def rglru(self, l):
        S, A = self.S, self.A
        o = l // 2
        dr = self.dram
        m0 = A.mark()
        w_in = dr["o_w_in"][o].rearrange("(kc kp) n -> kp kc n", kp=128)
        gslot = A.alloc((D,), F32)
        hT = A.alloc((8, T), BF16)
        mT = A.alloc((8, T), BF16)
        junk = A.alloc((D,), BF16)
        pcol = A.alloc((8, 8), F32)
        c1 = A.alloc((8,), F32)
        c2 = A.alloc((8,), F32)
        self.tr_banks = (4, 5, 6, 7)
        self.load_gain(gslot, dr["norm_gains"][l, 0, :], 32.0)
        rows = [dr["o_conv_w"][o, j:j + 1, :] for j in range(4)]
        rows += [dr["o_conv_b"][o:o + 1, :], dr["o_gate_a_b"][o:o + 1, :], dr["o_gate_x_b"][o:o + 1, :],
                 dr["o_a_param"][o:o + 1, :]]
        self.load_cols(pcol, rows)
        S.act(c1, pcol[:, :, 7], AF.Exp, scale=-1.0)
        S.act(c1, c1, AF.Ln, bias=1.0)
        S.ts("dve", c2, c1, -16.0, None, ALU.mult)
        S.ts("dve", c1, c1, -8.0, None, ALU.mult)
        self.prenorm(list(range(16)), gslot, hT, junk)
        m1 = A.mark()
        S.tag = "lru.blocks"
        HT = 1024
        wx = [A.alloc((8, 256), BF16) for _ in range(2)]
        wy = [A.alloc((8, 256), BF16) for _ in range(2)]
        ga = [A.alloc((2, 256), BF16) for _ in range(2)]
        gx = [A.alloc((2, 256), BF16) for _ in range(2)]
        prev3 = A.alloc((8, 3), F32)
        carry = A.alloc((8,), F32)

        class U:
            pass
        units = []
        g1v = gslot.bitcast(BF16)
        for ui in range(2):
            u = U()
            u.xc = [A.alloc((HT,), F32) for _ in range(2)]
            u.xcb = A.alloc((2, HT), BF16)
            u.A1 = A.alloc((HT,), F32)
            u.A2 = A.alloc((HT,), F32)
            u.A3 = A.alloc((HT,), F32)
            u.G1 = g1v[:, ui * HT:(ui + 1) * HT]
            units.append(u)

        def load_blk(n):
            sl = n % 2
            S.dma("pool", wx[sl], w_in[:, :, n * 256:(n + 1) * 256])
            S.dma("pool", wy[sl], w_in[:, :, D + n * 256: D + (n + 1) * 256])
            S.dma("pool", ga[sl], dr["o_gate_a_w"][o, n].rearrange("(dc dp) e -> dp dc e", dp=128))
            S.dma("pool", gx[sl], dr["o_gate_x_w"][o, n].rearrange("(dc dp) e -> dp dc e", dp=128))

        def unit(n, hf, k):
            u = units[k % 2]
            sl = n % 2
            tok0 = hf * HT
            pbase = 4 * (k % 2)
            X = [self.bank(pbase, 2), self.bank(pbase + 2, 2)]
            for c in range(2):
                for tb in range(2):
                    S.mm(X[c][:, tb * 512:(tb + 1) * 512],
                         [(wx[sl][:, kc, c * 128:(c + 1) * 128], hT[:, kc, tok0 + tb * 512: tok0 + (tb + 1) * 512])
                          for kc in range(8)])
            yield
            for c in range(2):
                ch = 2 * n + c
                xc = u.xc[c]
                S.ts("dve", xc, X[c], pcol[:, ch, 3:4], pcol[:, ch, 4:5], ALU.mult, ALU.add)
                for sh in (1, 2, 3):
                    S.stt("dve", xc[:, sh:], X[c][:, 0:HT - sh], pcol[:, ch, 3 - sh:4 - sh], xc[:, sh:], ALU.mult, ALU.add)
                if hf == 0:
                    S.copy("dve", prev3[:, ch, :], X[c][:, HT - 3:HT])
                else:
                    S.stt("dve", xc[:, 0:3], prev3[:, ch, 0:3], pcol[:, ch, 0:1], xc[:, 0:3], ALU.mult, ALU.add)
                    S.stt("dve", xc[:, 0:2], prev3[:, ch, 1:3], pcol[:, ch, 1:2], xc[:, 0:2], ALU.mult, ALU.add)
                    S.stt("dve", xc[:, 0:1], prev3[:, ch, 2:3], pcol[:, ch, 2:3], xc[:, 0:1], ALU.mult, ALU.add)
                S.copy("act", u.xcb[:, c, :], xc)
            yield
            for ec in range(2):
                ch = 2 * n + ec
                Ga = self.bank(pbase, 2)
                Gx = self.bank(pbase + 2, 2)
                for tb in range(2):
                    S.mm(Ga[:, tb * 512:(tb + 1) * 512],
                         [(ga[sl][:, dc, ec * 128:(ec + 1) * 128], u.xcb[:, dc, tb * 512:(tb + 1) * 512]) for dc in range(2)])
                for tb in range(2):
                    S.mm(Gx[:, tb * 512:(tb + 1) * 512],
                         [(gx[sl][:, dc, ec * 128:(ec + 1) * 128], u.xcb[:, dc, tb * 512:(tb + 1) * 512]) for dc in range(2)])
                yield
                S.act(u.A1, Ga, AF.Sigmoid, bias=pcol[:, ch, 5:6])
                S.act(u.A2, Gx, AF.Sigmoid, bias=pcol[:, ch, 6:7])
                Y = Ga
                for tb in range(2):
                    S.mm(Y[:, tb * 512:(tb + 1) * 512],
                         [(wy[sl][:, kc, ec * 128:(ec + 1) * 128], hT[:, kc, tok0 + tb * 512: tok0 + (tb + 1) * 512])
                          for kc in range(8)])
                S.act(u.A3, u.A1, AF.Exp, scale=c1[:, ch:ch + 1])
                S.act(u.A1, u.A1, AF.Exp, scale=c2[:, ch:ch + 1])
                S.act(u.G1, Y, AF.Gelu_apprx_tanh)
                yield
                S.ts("dve", u.A1, u.A1, -1.0, -1e-30, ALU.add, ALU.min)
                S.act(u.A1, u.A1, AF.Sqrt, scale=-1.0)
                yield
                S.tt("dve", u.A2, u.A2, u.A1, ALU.mult)
                S.tt("dve", u.A2, u.A2, u.xc[ec], ALU.mult)
                init = 0.0 if hf == 0 else carry[:, ch:ch + 1]
                S.scan(u.A1, u.A3, u.A2, init, ALU.mult, ALU.add)
                if hf == 0:
                    S.copy("dve", carry[:, ch:ch + 1], u.A1[:, HT - 1:HT])
                S.tt("dve", mT[:, ch, tok0:tok0 + HT], u.A1, u.G1, ALU.mult)
                if ec == 0:
                    yield
            if hf == 1 and n + 2 < 4:
                load_blk(n + 2)
        load_blk(0)
        load_blk(1)
        pipeline((unit(n, hf, 2 * n + hf) for n in range(4) for hf in range(2)), depth=2, skew=1)
        A.release(m1)
        wout = A.alloc((8, D), BF16)
        tmp = [A.alloc((D,), F32) for _ in range(2)]
        ss = A.alloc((16,), F32)
        rs = A.alloc((16,), F32)
        w_o = dr["o_w_out"][o].rearrange("(kc kp) n -> kp kc n", kp=128)
        S.dma("pool", wout[:, 0:4, :], w_o[:, 0:4, :])
        S.dma("pool", wout[:, 4:8, :], w_o[:, 4:8, :])
        self.load_gain(gslot, dr["norm_gains"][l, 1, :], 32.0)
        for i in range(16):
            py = self.bank(4 + 2 * (i % 2), 2)
            for nb in range(2):
                S.mm(py[:, nb * 512:(nb + 1) * 512],
                     [(mT[:, kc, i * 128:(i + 1) * 128], wout[:, kc, nb * 512:(nb + 1) * 512]) for kc in range(8)])
            self.postnorm_add(py, gslot, i, ss, rs, i, junk, tmp[i % 2])
        A.release(m0)

    def cload(self, name, shape, dtype, parts=128):
        t = self.A.alloc(shape, dtype, parts=parts)
        self.S.dma("sp", t, self.dram[name])
        return t

    def rope(self, x1, x2, cos, sin, o1, o2, rt):
        S = self.S
        S.tt("dve", rt[0], x1, cos, ALU.mult)
        S.tt("dve", rt[1], x2, sin, ALU.mult)
        S.tt("pool", o1, rt[0], rt[1], ALU.subtract)
        S.tt("dve", rt[2], x1, sin, ALU.mult)
        S.tt("dve", rt[3], x2, cos, ALU.mult)
        S.tt("dve", o2, rt[2], rt[3], ALU.add)

    def gdn_mla(self, l):
        S, A = self.S, self.A
        e = l // 2
        dr = self.dram
        m0 = A.mark()
        PI = float(np.pi)
        w_in = dr["e_w_in"][e].rearrange("(kc kp) n -> kp kc n", kp=128)
        gslot = A.alloc((D,), F32)
        junk = A.alloc((D,), BF16)
        mixT = A.alloc((8, T), BF16)
        ones_bf = self.cload("ones_bf", (128,), BF16)
        self.tr_banks = (4, 5, 6, 7)
        if not self.cfg.get("skip_mla"):
            self.mla(l, gslot, junk, mixT, ones_bf, w_in)
        else:
            S.memset("pool", mixT[:, 4:8, :], 0.0)
        if not self.cfg.get("skip_gdn"):
            self.gdn(l, gslot, junk, mixT, ones_bf, w_in)
        else:
            S.memset("pool", mixT[:, 0:4, :], 0.0)
        S.tag = "mix0.out"
        wout = A.alloc((8, D), BF16)
        tmp = [A.alloc((D,), F32) for _ in range(2)]
        ss = A.alloc((16,), F32)
        rs = A.alloc((16,), F32)
        w_o = dr["e_w_out"][e].rearrange("(kc kp) n -> kp kc n", kp=128)
        S.dma("pool", wout[:, 0:4, :], w_o[:, 0:4, :])
        S.dma("pool", wout[:, 4:8, :], w_o[:, 4:8, :])
        self.load_gain(gslot, dr["norm_gains"][l, 1, :], 32.0)
        for i in range(16):
            py = self.bank(4 + 2 * (i % 2), 2)
            for nb in range(2):
                S.mm(py[:, nb * 512:(nb + 1) * 512],
                     [(mixT[:, kc, i * 128:(i + 1) * 128], wout[:, kc, nb * 512:(nb + 1) * 512]) for kc in range(8)])
            self.postnorm_add(py, gslot, i, ss, rs, i, junk, tmp[i % 2])
        A.release(m0)

    def mla(self, l, gslot, junk, mixT, ones_bf, w_in):
        S, A = self.S, self.A
        e = l // 2
        dr = self.dram
        PI = float(np.pi)
        scale = 192.0 ** -0.5
        mm0 = A.mark()
        c_qnT = A.alloc((2, T), BF16)
        c_kvnT = A.alloc((2, T), BF16)
        kr = A.alloc((T,), BF16)
        S.memset("pool", kr[64:128, :], 0.0)
        cos = A.alloc((T,), F32)
        sin = A.alloc((T,), F32)
        pcoln = A.alloc((2, 2), F32)
        bigP = A.alloc((T,), BF16)
        bigPT = A.alloc((16, 128), BF16)
        rt = [bigP[:, 0:1024].bitcast(F32), bigP[:, 1024:2048].bitcast(F32),
              bigPT[:, 0:8, :].rearrange("p a b -> p (a b)").bitcast(F32),
              bigPT[:, 8:16, :].rearrange("p a b -> p (a b)").bitcast(F32)]
        negmask = self.cload("negmask_bf", (128,), BF16)
        self.load_cols(pcoln, [dr["e_q_norm"][e:e + 1, :], dr["e_kv_norm"][e:e + 1, :]])
        S.tag = "mla.rope"
        m1 = A.mark()
        invf = self.cload("inv_freq", (1,), F32, parts=32)
        posi = A.alloc((T,), I32)
        ang = A.alloc((T,), F32)
        tf = A.alloc((T,), F32)
        ti = A.alloc((T,), I32)
        S.dma("sp", posi[0:32, :], dr["positions"].partition_broadcast(32))
        S.copy("dve", ang[0:32, :], posi[0:32, :])
        S.ts("dve", ang[0:32, :], ang[0:32, :], invf[0:32, 0:1], None, ALU.mult)
        for dst, shift in ((sin, 0.0), (cos, PI / 2)):
            S.ts("dve", tf[0:32, :], ang[0:32, :], shift, 1.0 / (2 * PI), ALU.add, ALU.mult)
            S.copy("dve", ti[0:32, :], tf[0:32, :])
            S.copy("dve", tf[0:32, :], ti[0:32, :])
            S.stt("dve", tf[0:32, :], tf[0:32, :], -2 * PI, ang[0:32, :], ALU.mult, ALU.add)
            S.ts("dve", tf[0:32, :], tf[0:32, :], shift, 3.1415925, ALU.add, ALU.min)
            S.ts("dve", tf[0:32, :], tf[0:32, :], -3.1415925, None, ALU.max)
            S.act(dst[0:32, :], tf[0:32, :], AF.Sin)
        A.release(m1)
        S.tag = "mla.a"
        m1 = A.mark()
        hT = A.alloc((8, T), BF16)
        w576 = A.alloc((8, 576), BF16)
        sq = [A.alloc((512,), BF16) for _ in range(4)]
        rq = A.alloc((512,), F32)
        rkv = A.alloc((512,), F32)
        S.dma("pool", w576, w_in[:, :, 2056:2632])
        self.load_gain(gslot, dr["norm_gains"][l, 0, :], 32.0)
        self.prenorm(list(range(16)), gslot, hT, junk)
        for tb in range(4):
            tbs = slice(tb * 512, (tb + 1) * 512)
            for cc in range(4):
                S.mm(self.bank(cc), [(w576[:, kc, cc * 128:(cc + 1) * 128], hT[:, kc, tbs]) for kc in range(8)])
                S.act(sq[cc], self.bank(cc), AF.Square)
            S.mm(self.bank(4), [(ones_bf, sq[0]), (ones_bf, sq[1])])
            S.mm(self.bank(5), [(ones_bf, sq[2]), (ones_bf, sq[3])])
            S.act(rq, self.bank(4), AF.Ln, scale=1.0 / 256, bias=EPS)
            S.act(rq, rq, AF.Exp, scale=-0.5)
            S.act(rkv, self.bank(5), AF.Ln, scale=1.0 / 256, bias=EPS)
            S.act(rkv, rkv, AF.Exp, scale=-0.5)
            for cc in range(2):
                S.stt("dve", c_qnT[:, cc, tbs], self.bank(cc), pcoln[:, cc, 0:1], rq, ALU.mult, ALU.mult)
                S.stt("dve", c_kvnT[:, cc, tbs], self.bank(2 + cc), pcoln[:, cc, 1:2], rkv, ALU.mult, ALU.mult)
            x1 = self.bank(6)[0:32, :]
            x2 = self.bank(7)[0:32, :]
            S.mm(x1, [(w576[:, kc, 512:544], hT[:, kc, tbs]) for kc in range(8)])
            S.mm(x2, [(w576[:, kc, 544:576], hT[:, kc, tbs]) for kc in range(8)])
            self.rope(x1, x2, cos[0:32, tbs], sin[0:32, tbs], kr[0:32, tbs], kr[32:64, tbs],
                      [r[0:32, :] for r in rt])
        A.release(m1)
        S.tag = "mla.c"
        wuq = A.alloc((2, 768), BF16)
        wukv = A.alloc((2, 1024), BF16)
        S.dma("pool", wuq, dr["e_w_uq"][e].rearrange("(kc kp) n -> kp kc n", kp=128))
        S.dma("pool", wukv, dr["e_w_ukv"][e].rearrange("(kc kp) n -> kp kc n", kp=128))
        qn = [A.alloc((T,), BF16) for _ in range(2)]
        qr = [A.alloc((T,), BF16) for _ in range(2)]
        for t_ in qr:
            S.memset("pool", t_[64:128, :], 0.0)
        kn = [A.alloc((T,), BF16) for _ in range(2)]
        Vh = [A.alloc((16, 128), BF16) for _ in range(2)]

        class Res:
            pass
        big, small = Res(), Res()
        big.S = self.bank(0, 4)
        small.S = self.bank(4, 2)
        b6 = self.bank_bf(6)
        b7 = self.bank_bf(7)
        big.ptb = [b6[:, 0:512], b6[:, 512:1024]]
        small.ptb = [b7[:, 512:1024]]
        big.pso = self.bank(7)[:, 0:128]
        small.pso = self.bank(7)[:, 128:256]
        big.P, big.PT = bigP, bigPT
        small.P = A.alloc((1024,), BF16)
        small.PT = A.alloc((8, 128), BF16)
        for R in (big, small):
            R.mx = A.alloc((1,), F32)
            R.nb = A.alloc((1,), F32)
            R.sm = A.alloc((1,), F32)
            R.rinv = A.alloc((1,), F32)
        kk = [0]

        def proj(h):
            hb = h % 2
            for tb in range(4):
                tbs = slice(tb * 512, (tb + 1) * 512)

                def nbank():
                    b = self.bank(4 + kk[0] % 4)
                    kk[0] += 1
                    return b
                pb = nbank()
                S.mm(pb, [(wuq[:, kc, h * 192:h * 192 + 128], c_qnT[:, kc, tbs]) for kc in range(2)])
                S.copy("act", qn[hb][:, tbs], pb)
                pb = nbank()
                S.mm(pb, [(wukv[:, kc, h * 256:h * 256 + 128], c_kvnT[:, kc, tbs]) for kc in range(2)])
                S.copy("act", kn[hb][:, tbs], pb)
                x1 = nbank()[0:32, :]
                x2 = nbank()[0:32, :]
                S.mm(x1, [(wuq[:, kc, h * 192 + 128:h * 192 + 160], c_qnT[:, kc, tbs]) for kc in range(2)])
                S.mm(x2, [(wuq[:, kc, h * 192 + 160:h * 192 + 192], c_qnT[:, kc, tbs]) for kc in range(2)])
                self.rope(x1, x2, cos[0:32, tbs], sin[0:32, tbs], qr[hb][0:32, tbs], qr[hb][32:64, tbs],
                          [r[0:32, :] for r in rt])
                pb = nbank()
                for j in range(4):
                    t = tb * 4 + j
                    S.mm(pb[:, j * 128:(j + 1) * 128],
                         [(c_kvnT[:, kc, t * 128:(t + 1) * 128], wukv[:, kc, h * 256 + 128:h * 256 + 256]) for kc in range(2)])
                S.copy("dve", Vh[hb][:, tb * 4:(tb + 1) * 4, :], pb.rearrange("p (j d) -> p j d", d=128))

        def attn_block(h, i, R):
            hb = h % 2
            blk = slice(i * 128, (i + 1) * 128)
            nk = (i + 1) * 128
            Sb = R.S
            nb4 = (nk + 511) // 512
            for b4 in range(nb4):
                cols = slice(b4 * 512, min(nk, (b4 + 1) * 512))
                grp = [(qn[hb][:, blk], kn[hb][:, cols]),
                       (qr[hb][:, blk], kr[:, cols])]
                if b4 == nb4 - 1:
                    grp.append((Sb[:, blk], self.ident, negmask))
                S.mm(Sb[:, cols], grp)
            yield
            S.reduce("dve", R.mx, Sb[:, 0:nk], ALU.max)
            S.ts("dve", R.nb, R.mx, -scale, None, ALU.mult)
            Pi = R.P
            S.act(Pi[:, 0:nk], Sb[:, 0:nk], AF.Exp, bias=R.nb[:, 0:1], scale=scale, accum_out=R.sm[:, 0:1])
            S.op("dve", lambda hh: hh.reciprocal(out=R.rinv, in_=R.sm), [R.sm], [R.rinv])
            S.ts("dve", Pi[:, 0:nk], Pi[:, 0:nk], R.rinv[:, 0:1], None, ALU.mult)
            yield
            ng = (i + 4) // 4
            for g4 in range(ng):
                pt = R.ptb[g4 % len(R.ptb)]
                n4 = min(4, i + 1 - g4 * 4)
                for j in range(n4):
                    kb = g4 * 4 + j
                    S.tr(pt[:, j * 128:(j + 1) * 128], Pi[:, kb * 128:(kb + 1) * 128], self.ident)
                S.copy("act" if g4 % 2 == 0 else "dve", R.PT[:, g4 * 4:g4 * 4 + n4, :],
                       pt[:, 0:n4 * 128].rearrange("p (c t) -> p c t", t=128))
            yield
            S.mm(R.pso, [(Vh[hb][:, kb, :], R.PT[:, kb, :]) for kb in range(i + 1)])
            S.copy("act", mixT[:, 4 + h, blk], R.pso)

        proj(0)
        for h in range(4):
            if h + 1 < 4:
                proj(h + 1)
            order = []
            for j in range(8):
                order.append(attn_block(h, 8 + j, big))
                order.append(attn_block(h, j, small))
            pipeline(order, depth=2)
        A.release(mm0)

    def gdn(self, l, gslot, junk, mixT, ones_bf, w_in):
        S, A = self.S, self.A
        e = l // 2
        dr = self.dram
        mg0 = A.mark()
        qkvT = A.alloc((12, T), BF16)
        zs = A.alloc((16, 512), BF16)
        abraw = A.alloc((16, 8), F32)
        S.tag = "gdn.proj"
        m1 = A.mark()
        hTh = A.alloc((8, 1024), BF16)
        wsl = [A.alloc((8, 128), BF16) for _ in range(2)]
        wab = A.alloc((8, 8), BF16)
        sq = A.alloc((1024,), BF16)
        prev3 = A.alloc((12, 3), F32)
        pcolc = A.alloc((12, 4), F32)
        xcs = [mixT[:, 0, :].bitcast(F32), mixT[:, 1, :].bitcast(F32)]
        rsts = [mixT[:, 2, :].bitcast(F32), mixT[:, 3, :].bitcast(F32)]
        sqs = [sq, junk]
        self.load_cols(pcolc, [dr["e_conv_w"][e, j:j + 1, :] for j in range(4)])
        S.dma("pool", wab, w_in[:, :, 2048:2056])
        nload = [0]

        def wload(col0):
            sl = wsl[nload[0] % 2]
            nload[0] += 1
            S.dma("pool", sl, w_in[:, :, col0:col0 + 128])
            return sl
        qs = 128.0 ** -0.5
        for hf in range(2):
            tiles = list(range(hf * 8, hf * 8 + 8))
            hcol = slice(hf * 1024, (hf + 1) * 1024)
            self.load_gain(gslot, dr["norm_gains"][l, 0, :], 32.0)
            self.prenorm(tiles, gslot, hTh, junk)
            nxt_box = [wload(0)]

            def proj_chunk(c, hf=hf, hcol=hcol):
                p = c % 2
                w = nxt_box[0]
                nxt_box[0] = wload((c + 1) * 128) if c + 1 < 12 else wload(1536)
                X = self.bank(2 * p, 2)
                for tb in range(2):
                    S.mm(X[:, tb * 512:(tb + 1) * 512],
                         [(w[:, kc, :], hTh[:, kc, tb * 512:(tb + 1) * 512]) for kc in range(8)])
                yield
                xcp = xcs[p]
                S.ts("dve", xcp, X, pcolc[:, c, 3:4], None, ALU.mult)
                for sh in (1, 2, 3):
                    S.stt("dve", xcp[:, sh:], X[:, 0:1024 - sh], pcolc[:, c, 3 - sh:4 - sh], xcp[:, sh:], ALU.mult, ALU.add)
                if hf == 0:
                    S.copy("dve", prev3[:, c, :], X[:, 1021:1024])
                else:
                    S.stt("dve", xcp[:, 0:3], prev3[:, c, 0:3], pcolc[:, c, 0:1], xcp[:, 0:3], ALU.mult, ALU.add)
                    S.stt("dve", xcp[:, 0:2], prev3[:, c, 1:3], pcolc[:, c, 1:2], xcp[:, 0:2], ALU.mult, ALU.add)
                    S.stt("dve", xcp[:, 0:1], prev3[:, c, 2:3], pcolc[:, c, 2:3], xcp[:, 0:1], ALU.mult, ALU.add)
                yield
                if c >= 8:
                    S.act(qkvT[:, c, hcol], xcp, AF.Silu)
                    return
                S.act(xcp, xcp, AF.Silu)
                S.act(sqs[p], xcp, AF.Square)
                yield
                SS = self.bank(4 + 2 * p, 2)
                for tb in range(2):
                    S.mm(SS[:, tb * 512:(tb + 1) * 512], [(ones_bf, sqs[p][:, tb * 512:(tb + 1) * 512])])
                S.act(rsts[p], SS, AF.Ln, bias=EPS)
                S.act(rsts[p], rsts[p], AF.Exp, scale=-0.5)
                yield
                S.stt("dve", qkvT[:, c, hcol], xcp, (qs if c < 4 else 1.0), rsts[p], ALU.mult, ALU.mult)
            pipeline((proj_chunk(c) for c in range(12)), depth=2)
            nxt = nxt_box[0]
            for zc in range(4):
                w = nxt
                if zc + 1 < 4:
                    nxt = wload(1536 + (zc + 1) * 128)
                for tg in range(2):
                    pb = self.bank(6 + tg)
                    for j in range(4):
                        t = tg * 4 + j
                        S.mm(pb[:, j * 128:(j + 1) * 128],
                             [(hTh[:, kc, t * 128:(t + 1) * 128], w[:, kc, :]) for kc in range(8)])
                    S.act(zs[:, hf * 8 + tg * 4: hf * 8 + tg * 4 + 4, zc * 128:(zc + 1) * 128],
                          pb.rearrange("p (j d) -> p j d", d=128), AF.Silu)
            pb = self.bank(4)
            for t in range(8):
                S.mm(pb[:, t * 8:(t + 1) * 8], [(hTh[:, kc, t * 128:(t + 1) * 128], wab[:, kc, :]) for kc in range(8)])
            S.copy("dve", abraw[:, hf * 8:(hf + 1) * 8, :], pb[:, 0:64].rearrange("p (t c) -> p t c", c=8))
        A.release(m1)
        S.tag = "gdn.gates"
        g_all = A.alloc((16, 4), F32)
        beta_all = A.alloc((16, 4), F32)
        alog = A.alloc((4,), F32)
        dtb = A.alloc((4,), F32)
        onb = A.alloc((128,), F32)
        S.dma("sp", alog, dr["e_a_log"][e].partition_broadcast(128))
        S.dma("sp", dtb, dr["e_dt_bias"][e].partition_broadcast(128))
        S.dma("sp", onb, dr["e_o_norm"][e].partition_broadcast(128))
        S.act(alog, alog, AF.Exp)
        S.ts("dve", alog, alog, -1.0, None, ALU.mult)
        S.tt("dve", g_all, abraw[:, :, 0:4], dtb.unsqueeze(1).to_broadcast([128, 16, 4]), ALU.add)
        S.act(g_all, g_all, AF.Exp)
        S.act(g_all, g_all, AF.Ln, bias=1.0)
        S.tt("dve", g_all, g_all, alog.unsqueeze(1).to_broadcast([128, 16, 4]), ALU.mult)
        S.act(beta_all, abraw[:, :, 4:8], AF.Sigmoid)
        S.tag = "gdn.core"
        triu = self.cload("triu_f32", (128,), F32)
        ones_f = self.cload("ones_f32", (128,), F32)
        maskL = self.cload("maskL_f32", (128,), F32)
        maskA = self.cload("maskA_f32", (128,), F32)
        same01 = self.cload("same01_bf", (128,), BF16)
        off01 = self.cload("off01_bf", (128,), BF16)
        H4 = (4, 128)
        ngtri = A.alloc(H4, F32)
        gbc = A.alloc(H4, F32)
        decL = A.alloc(H4, F32)
        decA = A.alloc(H4, F32)
        egcR = A.alloc(H4, F32)
        Lb = A.alloc(H4, BF16)
        Lo = A.alloc(H4, BF16)
        Pbuf = [A.alloc(H4, BF16) for _ in range(2)]
        Qbuf = [A.alloc(H4, BF16) for _ in range(2)]
        Ybuf = [A.alloc(H4, BF16) for _ in range(2)]
        U0 = A.alloc(H4, BF16)
        aqkT = A.alloc(H4, BF16)
        kbg = A.alloc(H4, BF16)
        kdec = A.alloc(H4, BF16)
        vb = A.alloc(H4, BF16)
        u_sb = A.alloc(H4, F32)
        wT = A.alloc(H4, BF16)
        qgT = A.alloc(H4, BF16)
        vnew = A.alloc(H4, BF16)
        S_f = A.alloc(H4, F32)
        S_b = A.alloc(H4, BF16)
        outa = A.alloc(H4, BF16)
        sqo = gslot[:, 0:512].rearrange("p (h d) -> p h d", d=128)
        on = gslot[:, 512:1024].rearrange("p (h d) -> p h d", d=128)
        gz = junk.bitcast(F32).rearrange("p (h d) -> p h d", d=128)
        ng = A.alloc((4,), F32)
        gcs = A.alloc((4,), F32)
        egc = A.alloc((4,), F32)
        bg = A.alloc((4,), F32)
        edec = A.alloc((4,), F32)
        ssn = A.alloc((4,), F32)
        rsn = A.alloc((4,), F32)
        S.memset("dve", S_f, 0.0)
        S.memset("dve", S_b, 0.0)

        def b4(b):
            return self.bank(b).rearrange("p (h d) -> p h d", d=128)

        def bc_h(t2):
            return t2.unsqueeze(1).to_broadcast([128, 4, 128])

        def bc_d(t4):
            return t4.unsqueeze(2).to_broadcast([128, 4, 128])
        identb = self.ident
        for i in range(self.cfg.get("gdn_blocks", 16)):
            blk = slice(i * 128, (i + 1) * 128)
            g_i = g_all[:, i, :]
            beta_i = beta_all[:, i, :]
            qTb = qkvT[:, 0:4, blk]
            kTb = qkvT[:, 4:8, blk]
            vTb = qkvT[:, 8:12, blk]
            S.ts("dve", ng, g_i, -1.0, None, ALU.mult)
            S.tt("dve", ngtri, bc_h(triu), bc_d(ng), ALU.mult)
            S.copy("dve", gbc, bc_d(g_i))
            Dm = b4(0)
            Rn = b4(1)
            for h in range(4):
                S.mm(Dm[:, h, :], [(triu, gbc[:, h, :]), (ones_f, ngtri[:, h, :])])
            for h in range(4):
                S.mm(Rn[:, h, :], [(ones_f, ngtri[:, h, :])])
            gcp = self.bank(2)[:, 0:4]
            S.mm(gcp, [(triu, g_i)])
            S.tt("dve", decL, Dm, bc_h(maskL), ALU.add)
            S.act(decL, decL, AF.Exp)
            S.tt("dve", decA, bc_h(maskA), Dm, ALU.subtract)
            S.act(decA, decA, AF.Exp)
            S.act(egcR, Rn, AF.Exp, scale=-1.0)
            S.act(egc, gcp, AF.Exp)
            S.act(edec, Dm[:, :, 127], AF.Exp, scale=-1.0)
            S.tt("dve", bg, beta_i, egc, ALU.mult)
            if self.cfg.get("gdn_cut", 99) <= 1:
                continue
            KK = b4(3)
            QK = b4(4)
            for h in range(4):
                S.mm(KK[:, h, :], [(kTb[:, h, :], kTb[:, h, :])])
            for h in range(4):
                S.mm(QK[:, h, :], [(kTb[:, h, :], qTb[:, h, :])])
            for h in range(4):
                S.stt("dve", Lb[:, h, :], KK[:, h, :], beta_i[:, h:h + 1], decL[:, h, :], ALU.mult, ALU.mult)
            S.tt("dve", aqkT, QK, decA, ALU.mult)
            if self.cfg.get("gdn_cut", 99) <= 2:
                continue
            tb5 = self.bank_bf(5)
            for h in range(4):
                S.tr(tb5[:, h * 128:(h + 1) * 128], kTb[:, h, :], identb)
            for h in range(4):
                S.tr(tb5[:, (4 + h) * 128:(5 + h) * 128], vTb[:, h, :], identb)
            kTM = tb5[:, 0:512].rearrange("p (h d) -> p h d", d=128)
            vTM = tb5[:, 512:1024].rearrange("p (h d) -> p h d", d=128)
            S.tt("dve", kbg, kTM, bc_d(bg), ALU.mult)
            S.tt("dve", kdec, kTM, bc_d(edec), ALU.mult)
            S.tt("dve", vb, vTM, bc_d(beta_i), ALU.mult)
            if self.cfg.get("gdn_cut", 99) <= 3:
                continue
            S.tt("dve", Lo, Lb, bc_h(off01), ALU.mult)
            S.tt("dve", Lb, Lb, bc_h(same01), ALU.mult)
            tb6 = self.bank_bf(6)
            for h in range(4):
                S.tr(tb6[:, h * 128:(h + 1) * 128], Lb[:, h, :], identb)
            S.copy("act", U0, tb6[:, 0:512].rearrange("p (h d) -> p h d", d=128))
            Yc = Ybuf[0]
            S.tt("dve", Yc, bc_h(identb), U0, ALU.subtract)
            Pc, Qc = Lb, U0
            for j in range(1, 6):
                PB = b4(3)
                for h in range(4):
                    S.mm(PB[:, h, :], [(Qc[:, h, :], Pc[:, h, :])])
                Pn = Pbuf[j % 2]
                S.copy("act", Pn, PB)
                if j < 5:
                    QB = b4(4)
                    for h in range(4):
                        S.mm(QB[:, h, :], [(Pc[:, h, :], Qc[:, h, :])])
                    Qn = Qbuf[j % 2]
                    S.copy("act", Qn, QB)
                else:
                    Qn = None
                YB = b4(7)
                for h in range(4):
                    S.mm(YB[:, h, :], [(Pn[:, h, :], Yc[:, h, :])])
                Yn = Ybuf[j % 2]
                S.tt("dve", Yn, YB, Yc, ALU.add)
                Pc, Qc, Yc = Pn, Qn, Yn
            Ydt = Yc
            Td = Qbuf[1]
            tb6 = self.bank_bf(6)
            for h in range(4):
                S.tr(tb6[:, h * 128:(h + 1) * 128], Ydt[:, h, :], identb)
            S.copy("act", Td, tb6[:, 0:512].rearrange("p (h d) -> p h d", d=128))
            AB = b4(3)
            for h in range(4):
                S.mm(AB[:, h, :], [(Lo[:, h, :], Ydt[:, h, :])])
            Ab = Pbuf[0]
            S.copy("act", Ab, AB)
            TB = b4(4)
            for h in range(4):
                S.mm(TB[:, h, :], [(Td[:, h, :], Ab[:, h, :])])
            Tt = Ybuf[0]
            S.tt("dve", Tt, Ydt, TB, ALU.subtract)
            if self.cfg.get("gdn_cut", 99) <= 4:
                continue
            UB = b4(2)
            for h in range(4):
                S.mm(UB[:, h, :], [(Tt[:, h, :], vb[:, h, :])])
            S.copy("act", u_sb, UB)
            WB = b4(3)
            for h in range(4):
                S.mm(WB[:, h, :], [(kbg[:, h, :], Tt[:, h, :])])
            S.copy("act", wT, WB)
            S.tt("dve", qgT, qTb, egcR, ALU.mult)
            if self.cfg.get("gdn_cut", 99) <= 5:
                continue
            sub = self.cfg.get("gdn_sub", 99)
            P1 = b4(4)
            for h in range(4):
                S.mm(P1[:, h, :], [(wT[:, h, :], S_b[:, h, :])])
            if sub <= 1:
                continue
            S.tt("dve", vnew, u_sb, P1, ALU.subtract)
            if sub <= 2:
                continue
            PO = b4(2)
            for h in range(4):
                S.mm(PO[:, h, :], [(qgT[:, h, :], S_b[:, h, :]), (aqkT[:, h, :], vnew[:, h, :])])
            if sub <= 3:
                continue
            PS = b4(7)
            for h in range(4):
                S.mm(PS[:, h, :], [(kdec[:, h, :], vnew[:, h, :])])
            if sub <= 4:
                continue
            S.tt("dve", S_f, S_f, egcR[:, :, 127:128].to_broadcast([128, 4, 128]), ALU.mult)
            S.tt("dve", S_f, PS, S_f, ALU.add)
            if sub <= 5:
                continue
            S.copy("act", S_b, S_f)
            if self.cfg.get("gdn_cut", 99) <= 6:
                continue
            S.act(sqo, PO, AF.Square)
            S.reduce("dve", ssn, sqo, ALU.add)
            S.act(rsn, ssn, AF.Ln, scale=1.0 / 128, bias=EPS)
            S.act(rsn, rsn, AF.Exp, scale=-0.5)
            S.tt("dve", on, PO, bc_d(rsn), ALU.mult)
            S.tt("pool", gz, zs[:, i, :].rearrange("p (h d) -> p h d", d=128), bc_h(onb), ALU.mult)
            S.tt("dve", outa, on, gz, ALU.mult)
            tb5 = self.bank_bf(5)
            for h in range(4):
                S.tr(tb5[:, h * 128:(h + 1) * 128], outa[:, h, :], identb)
            S.copy("act", mixT[:, 0:4, blk], tb5[:, 0:512].rearrange("p (h d) -> p h d", d=128))
        A.release(mg0)

    def build(self):
        cfg = self.cfg
        S, A = self.S, self.A
        x_in = self.din("x", (T, D))
        self.din("norm_gains", (2, 6, D))
        self.din("ffn_w_in", (2, D, 2 * DFF))
        self.din("ffn_w_out", (2, DFF, D))
        self.din("mem", (256, D))
        self.din("mem_norm", (D,))
        self.din("xa_wq", (2, D, D))
        self.din("xa_wkv", (2, D, 2 * D))
        self.din("xa_wo", (2, D, D))
        self.din("o_w_in", (1, D, 2 * D))
        self.din("o_conv_w", (1, 4, D))
        self.din("o_conv_b", (1, D))
        self.din("o_gate_a_w", (1, 4, 256, 256))
        self.din("o_gate_a_b", (1, D))
        self.din("o_gate_x_w", (1, 4, 256, 256))
        self.din("o_gate_x_b", (1, D))
        self.din("o_a_param", (1, D))
        self.din("o_w_out", (1, D, D))
        identf_in = self.din("ident_f32", (128, 128), F32)
        self.din("positions", (T,), I32)
        self.din("e_w_in", (1, D, E_IN))
        self.din("e_conv_w", (1, 4, 1536))
        self.din("e_a_log", (1, 4))
        self.din("e_dt_bias", (1, 4))
        self.din("e_o_norm", (1, 128))
        self.din("e_q_norm", (1, 256))
        self.din("e_kv_norm", (1, 256))
        self.din("e_w_uq", (1, 256, 768))
        self.din("e_w_ukv", (1, 256, 1024))
        self.din("e_w_out", (1, D, D))
        self.din("ones_bf", (128, 128), BF16)
        self.din("negmask_bf", (128, 128), BF16)
        self.din("inv_freq", (32, 1), F32)
        self.din("ones_f32", (128, 128), F32)
        self.din("triu_f32", (128, 128), F32)
        self.din("maskL_f32", (128, 128), F32)
        self.din("maskA_f32", (128, 128), F32)
        self.din("same01_bf", (128, 128), BF16)
        self.din("off01_bf", (128, 128), BF16)
        ident_in = self.din("ident_bf", (128, 128), BF16)
        y_out = self.dout("y", (T, D))

        self.x = A.alloc((NT, D), F32)
        self.ident = A.alloc((128,), BF16)
        S.dma("sp", self.ident, ident_in)
        self.identf = A.alloc((128,), F32)
        S.dma("sp", self.identf, identf_in)
        xin_v = x_in.rearrange("(t p) d -> p t d", p=128)
        for q in range(4):
            S.dma("sp", self.x[:, q * 4:(q + 1) * 4, :], xin_v[:, q * 4:(q + 1) * 4, :])

        for step in cfg["steps"]:
            kind, l = step
            S.tag = "%s%d" % (kind, l)
            if kind == "ffn":
                self.ffn(l)
            elif kind == "xa":
                self.xattn(l)
            elif kind == "mix":
                self.mixer(l)

        yv = y_out.rearrange("(t p) d -> p t d", p=128)
        for q in range(4):
            S.dma("sp", yv[:, q * 4:(q + 1) * 4, :], self.x[:, q * 4:(q + 1) * 4, :])
        S.final_wait("sp")
        n = S.emit()
        self.n_ins = n
        self.es.close()
        return self.nc


FULL_STEPS = [("mix", 0), ("xa", 0), ("ffn", 0), ("mix", 1), ("xa", 1), ("ffn", 1)]

_CONST = {}


def consts():
    if not _CONST:
        _CONST["ident_bf"] = np.eye(128, dtype=np.float32).astype(ml_dtypes.bfloat16)
        _CONST["ident_f32"] = np.eye(128, dtype=np.float32)
        _CONST["ones_bf"] = np.ones((128, 128), np.float32).astype(ml_dtypes.bfloat16)
        _CONST["ones_f32"] = np.ones((128, 128), np.float32)
        qi = np.arange(128)[:, None]
        ki = np.arange(128)[None, :]
        _CONST["negmask_bf"] = np.where((qi < 64) & (ki >= 64), NEG, 0.0).astype(np.float32).astype(ml_dtypes.bfloat16)
        _CONST["inv_freq"] = (np.float32(10000.0) ** (-np.arange(0, 64, 2, dtype=np.float32) / np.float32(64))
                              ).astype(np.float32).reshape(32, 1)
        _CONST["triu_f32"] = (qi <= ki).astype(np.float32)
        _CONST["maskL_f32"] = np.where(qi > ki, 0.0, NEG).astype(np.float32)
        _CONST["maskA_f32"] = np.where(ki >= qi, 0.0, NEG).astype(np.float32)
        same = (qi // 64) == (ki // 64)
        _CONST["same01_bf"] = same.astype(np.float32).astype(ml_dtypes.bfloat16)
        _CONST["off01_bf"] = (~same).astype(np.float32).astype(ml_dtypes.bfloat16)
    return _CONST


def kernel(**inputs):
    cfg = {"steps": FULL_STEPS}
    return run(cfg, inputs)


def run(cfg, inputs, trace=False):
    p = Prog(cfg)
    nc = p.build()
    c = consts()
    names = [n for n in p.dram if n != "y"]
    in_maps = []
    for b in range(8):
        m = {}
        for n in names:
            if n in c:
                m[n] = c[n]
            elif n in ("x", "mem", "positions"):
                m[n] = np.ascontiguousarray(inputs[n][b])
            else:
                m[n] = np.ascontiguousarray(inputs[n])
        in_maps.append(m)
    res = run_bass_kernel_spmd(nc, in_maps, core_ids=list(range(8)), trace=trace)
    out = np.stack([np.asarray(r["y"]) for r in res.results], axis=0)
    if trace:
        return out, res
    return out
```

```python
import numpy as np
from contextlib import ExitStack
import ml_dtypes
import concourse.bass as bass
import concourse.mybir as mybir
from concourse.bass_utils import run_bass_kernel_spmd

F32 = mybir.dt.float32
BF16 = mybir.dt.bfloat16
I32 = mybir.dt.int32
U8 = mybir.dt.uint8
AF = mybir.ActivationFunctionType
ALU = mybir.AluOpType
AX = mybir.AxisListType
DSZ = {F32: 4, BF16: 2, I32: 4, U8: 1}

T = 2048
D = 1024
NT = 16
DFF = 2816
NFC = 22
EPS = 1e-6
E_IN = 2632
SB_BYTES = 207 * 1024
PS_BYTES = 16 * 1024
NDMA = 24
NEG = -1.0e30


class _Rec:
    __slots__ = ("p0", "p1", "lo", "hi", "writer", "readers", "pages")


class Sched:
    ENG = ("pe", "act", "dve", "pool", "sp")

    def __init__(self, nc, es):
        self.nc = nc
        self.h = {"pe": nc.tensor, "act": nc.scalar, "dve": nc.vector,
                  "pool": nc.gpsimd, "sp": nc.sync}
        self.sem = {e: es.enter_context(nc.semaphore("sem_" + e)) for e in self.ENG}
        self.dsem = [es.enter_context(nc.semaphore("dsem%d" % i)) for i in range(NDMA)]
        self.dcnt = [0] * NDMA
        self.dnext = 0
        self.dnext_q = {}
        self.ops = {e: [] for e in self.ENG}
        self.known = {e: {} for e in self.ENG}
        self.pages = {"arena": {}, "psum": {}}
        self.pbytes = {"arena": SB_BYTES, "psum": PS_BYTES}
        self.pgsz = {"arena": 1024, "psum": 512}
        self.extra = {}
        self.tag = ""
        self.annot = False
        self.snap = {}
        self.gorder = {}
        self.gcount = 0

    def regions(self, ap):
        name = ap.tensor.name
        if name not in self.pbytes:
            return []
        esz = DSZ[ap.dtype]
        pb = self.pbytes[name]
        offb = ap.offset * esz
        p0 = offb // pb
        lo = offb % pb
        dims = list(ap.ap)
        npart = dims[0][1]
        free = [(abs(s), c) for s, c in dims[1:] if c > 1]
        out = []

        def rec(base, fd):
            if not fd:
                out.append((name, p0, p0 + npart, base, base + esz))
                return
            ext_in = 1 + sum((c - 1) * s for s, c in fd[1:])
            s0, c0 = fd[0]
            if len(fd) >= 2 and s0 > ext_in and c0 <= 64 and len(out) < 256:
                for i in range(c0):
                    rec(base + i * s0 * esz, fd[1:])
            else:
                ext = 1 + sum((c - 1) * s for s, c in fd)
                out.append((name, p0, p0 + npart, base, base + ext * esz))

        free.sort(key=lambda t: -t[0])
        rec(lo, free)
        if name == "psum":
            banks = sorted({b for (_, _, _, l, h) in out for b in range(l // 2048, (h - 1) // 2048 + 1)})
            out = [(name, p0, p0 + npart, b * 2048, (b + 1) * 2048) for b in banks]
        return out

    def _overl(self, r):
        name, p0, p1, lo, hi = r
        pg = self.pgsz[name]
        pages = self.pages[name]
        seen = set()
        res = []
        for k in range(lo // pg, (hi - 1) // pg + 1):
            for rc in pages.get(k, ()):
                if id(rc) in seen:
                    continue
                seen.add(id(rc))
                if rc.lo < hi and lo < rc.hi and rc.p0 < p1 and p0 < rc.p1:
                    res.append(rc)
        return res

    def _add_rec(self, r, writer):
        name, p0, p1, lo, hi = r
        pg = self.pgsz[name]
        rc = _Rec()
        rc.p0, rc.p1, rc.lo, rc.hi = p0, p1, lo, hi
        rc.writer = writer
        rc.readers = {}
        rc.pages = (name, lo // pg, (hi - 1) // pg)
        pages = self.pages[name]
        for k in range(rc.pages[1], rc.pages[2] + 1):
            pages.setdefault(k, []).append(rc)
        return rc

    def _del_rec(self, rc):
        name, k0, k1 = rc.pages
        pages = self.pages[name]
        for k in range(k0, k1 + 1):
            pages[k].remove(rc)

    def _deps(self, reads, writes):
        deps = {}

        def add(src, val):
            if deps.get(src, -1) < val:
                deps[src] = val
        rregs = []
        for ap in reads:
            rregs.extend(self.regions(ap))
        wregs = []
        for ap in writes:
            wregs.extend(self.regions(ap))
        for r in rregs:
            for rc in self._overl(r):
                if rc.writer is not None:
                    add(*rc.writer)
        for r in wregs:
            for rc in self._overl(r):
                if rc.writer is not None:
                    add(*rc.writer)
                for s, v in rc.readers.items():
                    add(s, v)
        return deps, rregs, wregs

    def _commit(self, me, rregs, wregs):
        for r in rregs:
            ov = self._overl(r)
            if not ov:
                rc = self._add_rec(r, None)
                ov = [rc]
            for rc in ov:
                if rc.readers.get(me[0], -1) < me[1]:
                    rc.readers[me[0]] = me[1]
        for r in wregs:
            name, p0, p1, lo, hi = r
            for rc in self._overl(r):
                if rc.lo >= lo and rc.hi <= hi and rc.p0 >= p0 and rc.p1 <= p1:
                    self._del_rec(rc)
            self._add_rec(r, me)

    def op(self, e, fn, reads=(), writes=()):
        deps, rregs, wregs = self._deps(reads, writes)
        if e == "pe":
            deps.pop("pe", None)
        waits = self._resolve(e, deps)
        idx = len(self.ops[e])
        self.ops[e].append([fn, waits, False, None, self.tag])
        self.snap[(e, idx)] = dict(self.known[e])
        self.gcount += 1
        self.gorder[(e, idx)] = self.gcount
        self._commit((e, idx), rregs, wregs)

    def _resolve(self, e, deps):
        kn = self.known[e]
        waits = []
        order = sorted(deps.items(), key=lambda kv: -self.gorder.get(kv, 0))
        for src, val in order:
            if kn.get(src, -1) >= val:
                continue
            kn[src] = val
            waits.append((src, val))
            sn = self.snap.get((src, val))
            if sn:
                for s2, v2 in sn.items():
                    if kn.get(s2, -1) < v2:
                        kn[s2] = v2
        return waits

    def dma(self, q, out, in_, reads=None, writes=None):
        reads = [in_] if reads is None else reads
        writes = [out] if writes is None else writes
        deps, rregs, wregs = self._deps(reads, writes)
        lo, hi = (0, 8) if q == "sp" else (8, NDMA)
        cur = self.dnext_q.get(q, lo)
        slot = cur
        self.dnext_q[q] = lo + (cur + 1 - lo) % (hi - lo)
        src = ("d", slot)
        prev = self.dcnt[slot]
        if prev > 0:
            if deps.get(src, -1) < prev:
                deps[src] = prev
        self.dcnt[slot] = prev + 16
        waits = self._resolve(q, deps)
        self.gcount += 1
        self.gorder[(src, self.dcnt[slot])] = self.gcount
        self.snap[(src, self.dcnt[slot])] = dict(self.known[q])
        self.ops[q].append([lambda h: h.dma_start(out=out, in_=in_), waits, False, slot, self.tag])
        self._commit((src, self.dcnt[slot]), rregs, wregs)

    def final_wait(self, e):
        waits = [(("d", s), self.dcnt[s]) for s in range(NDMA) if self.dcnt[s] > 0]
        self.ops[e].append([None, waits, False, None, "final"])

    def emit(self):
        for e in self.ENG:
            for o in self.ops[e]:
                for src, val in o[1]:
                    if not isinstance(src, tuple):
                        self.ops[src][val][2] = True
        rank = {}
        for e in self.ENG:
            r = 0
            rk = []
            for o in self.ops[e]:
                if o[2]:
                    r += 1
                rk.append(r)
            rank[e] = rk
        n_ins = 0
        for e in self.ENG:
            h = self.h[e]
            for o in self.ops[e]:
                fn, waits, needed, slot, tag = o[:5]
                for src, val in waits:
                    if isinstance(src, tuple):
                        h.wait_ge(self.dsem[src[1]], val)
                    else:
                        h.wait_ge(self.sem[src], rank[src][val])
                    n_ins += 1
                if fn is None:
                    continue
                ins = fn(h)
                n_ins += 1
                if self.annot and tag:
                    ins.annotate(tag)
                if slot is not None:
                    ins.then_inc(self.dsem[slot], 16)
                elif needed:
                    ins.then_inc(self.sem[e], 1)
        return n_ins

    def mm(self, out, pairs, start=True, stop=True):
        n = len(pairs)
        pairs = [(p if len(p) == 3 else (out, p[0], p[1])) for p in pairs]
        reads = []
        for o, a, b in pairs:
            reads += [a, b]

        def fn(h):
            ins = None
            for i, (o, a, b) in enumerate(pairs):
                ins = h.matmul(o, a, b, start=(start and i == 0), stop=(stop and i == n - 1))
            return ins
        rd = reads if start else reads + [out]
        self.op("pe", fn, rd, [out])

    def tr(self, out, in_, ident):
        self.op("pe", lambda h: h.transpose(out, in_, ident), [in_, ident], [out])

    def act(self, out, in_, func, bias=None, scale=None, accum_out=None, eng="act"):
        reads = [in_]
        kw = {}
        if bias is not None:
            kw["bias"] = bias
            if not isinstance(bias, (int, float)):
                reads.append(bias)
        if scale is not None:
            kw["scale"] = scale
            if not isinstance(scale, (int, float)):
                reads.append(scale)
        writes = [out]
        if accum_out is not None:
            kw["accum_out"] = accum_out
            writes.append(accum_out)
        self.op("act", lambda h: h.activation(out=out, in_=in_, func=func, **kw), reads, writes)

    def tt(self, e, out, in0, in1, op):
        self.op(e, lambda h: h.tensor_tensor(out=out, in0=in0, in1=in1, op=op), [in0, in1], [out])

    def ts(self, e, out, in0, s1, s2, op0, op1=None):
        reads = [in0]
        for s in (s1, s2):
            if s is not None and not isinstance(s, (int, float)):
                reads.append(s)
        if op1 is None:
            self.op(e, lambda h: h.tensor_scalar(out=out, in0=in0, scalar1=s1, scalar2=None, op0=op0),
                    reads, [out])
        else:
            self.op(e, lambda h: h.tensor_scalar(out=out, in0=in0, scalar1=s1, scalar2=s2, op0=op0, op1=op1),
                    reads, [out])

    def stt(self, e, out, in0, scalar, in1, op0, op1):
        reads = [in0, in1]
        if not isinstance(scalar, (int, float)):
            reads.append(scalar)
        self.op(e, lambda h: h.scalar_tensor_tensor(out=out, in0=in0, scalar=scalar, in1=in1, op0=op0, op1=op1),
                reads, [out])

    def copy(self, e, out, in_):
        if e == "act":
            self.op(e, lambda h: h.copy(out=out, in_=in_), [in_], [out])
        else:
            self.op(e, lambda h: h.tensor_copy(out=out, in_=in_), [in_], [out])

    def memset(self, e, out, val):
        self.op(e, lambda h: h.memset(out, val), [], [out])

    def reduce(self, e, out, in_, op, axis=AX.X):
        self.op(e, lambda h: h.tensor_reduce(out=out, in_=in_, axis=axis, op=op), [in_], [out])

    def scan(self, out, d0, d1, init, op0, op1):
        reads = [d0, d1]
        if not isinstance(init, (int, float)):
            reads.append(init)
        self.op("dve", lambda h: h.tensor_tensor_scan(out=out, data0=d0, data1=d1, initial=init, op0=op0, op1=op1),
                reads, [out])


def pipeline(gens, depth=2, skew=1):
    it = iter(gens)
    active = []
    exhausted = False
    while True:
        if not exhausted and len(active) < depth and (not active or active[-1][1] >= skew):
            try:
                active.append([next(it), 0])
            except StopIteration:
                exhausted = True
        if not active:
            if exhausted:
                break
            continue
        for a in list(active):
            try:
                next(a[0])
                a[1] += 1
            except StopIteration:
                active.remove(a)


class Arena:
    def __init__(self, base, nbytes):
        self.base = base
        self.nbytes = nbytes
        self.top = 0
        self.peak = 0

    def alloc(self, free_shape, dtype, parts=128):
        n = 1
        for s in free_shape:
            n *= s
        nb = n * DSZ[dtype]
        off = (self.top + 63) // 64 * 64
        assert off + nb <= self.nbytes, "arena overflow: need %d at %d (cap %d)" % (nb, off, self.nbytes)
        self.top = off + nb
        self.peak = max(self.peak, self.top)
        v = self.base[0:parts, off:off + nb]
        if dtype != U8:
            v = v.bitcast(dtype)
        if len(free_shape) > 1:
            names = " ".join("d%d" % i for i in range(len(free_shape)))
            kw = {"d%d" % i: free_shape[i] for i in range(1, len(free_shape))}
            v = v.rearrange("p (%s) -> p %s" % (names, names), **kw)
        return v

    def mark(self):
        return self.top

    def release(self, m):
        self.top = m


class Prog:
    def __init__(self, cfg):
        self.cfg = cfg
        self.nc = bass.Bass("TRN2", target_bir_lowering=False)
        self.es = ExitStack()
        nc = self.nc
        es = self.es
        self.dram = {}
        sb = es.enter_context(nc.sbuf_tensor("arena", [128, SB_BYTES], U8))
        ps = es.enter_context(nc.psum_tensor("psum", [128, PS_BYTES // 4], F32))
        self.S = Sched(nc, es)
        self.S.annot = bool(cfg.get("annot"))
        self.A = Arena(sb, SB_BYTES)
        self.ps = ps

    def din(self, name, shape, dtype=F32):
        t = self.nc.dram_tensor(name, list(shape), dtype, kind="ExternalInput").ap()
        self.dram[name] = t
        return t

    def dout(self, name, shape, dtype=F32):
        t = self.nc.dram_tensor(name, list(shape), dtype, kind="ExternalOutput").ap()
        self.dram[name] = t
        return t

    def bank(self, b, n=1):
        return self.ps[:, b * 512:(b + n) * 512]

    def bank_bf(self, b, n=1):
        return self.ps[:, b * 512:(b + n) * 512].bitcast(BF16)

    def load_gain(self, dst, src_row, mult):
        S = self.S
        S.dma("sp", dst, src_row.partition_broadcast(128))
        S.ts("dve", dst, dst, float(mult), None, ALU.mult)

    def rstd(self, out, ss, c):
        S = self.S
        S.act(out, ss, AF.Ln, bias=float(c))
        S.act(out, out, AF.Exp, scale=-0.5)

    def prenorm(self, tiles, gb, hT, junk, col0=0):
        S, A = self.S, self.A
        m = A.mark()
        n = len(tiles)
        ss = A.alloc((n,), F32)
        rs = A.alloc((n,), F32)
        nb = len(self.tr_banks)
        if A.top + nb * 2048 + 256 > A.nbytes:
            nb = 2
        hb = [A.alloc((D,), BF16) for _ in range(nb)]
        junk2 = hb[1]
        for i, t in enumerate(tiles):
            if i % 2 == 0:
                S.act(junk, self.x[:, t, :], AF.Square, accum_out=ss[:, i:i + 1])
            else:
                xt = self.x[:, t, :]
                S.op("dve", (lambda xt, i: (lambda h: h.scalar_tensor_tensor(
                    out=junk2, in0=xt, scalar=1.0, in1=xt, op0=ALU.mult, op1=ALU.mult,
                    accum_out=ss[:, i:i + 1])))(xt, i), [xt], [junk2, ss[:, i:i + 1]])
        self.rstd(rs, ss, D * EPS)
        for i, t in enumerate(tiles):
            xt = self.x[:, t, :]
            h = hb[i % nb]
            S.stt("dve", h, xt, rs[:, i:i + 1], gb, ALU.mult, ALU.mult)
            pb = self.bank_bf(self.tr_banks[i % nb])
            for c in range(8):
                S.tr(pb[:, c * 128:(c + 1) * 128], h[:, c * 128:(c + 1) * 128], self.ident)
            dst = hT[:, :, col0 + i * 128: col0 + (i + 1) * 128]
            src = pb.rearrange("p (c t) -> p c t", t=128)
            S.copy("act", dst, src)
        A.release(m)

    def postnorm_add(self, psy, gb, t, ss, rs, i, junk, tmp):
        S = self.S
        S.act(junk, psy, AF.Square, accum_out=ss[:, i:i + 1])
        self.rstd(rs[:, i:i + 1], ss[:, i:i + 1], D * EPS)
        S.stt("dve", tmp, psy, rs[:, i:i + 1], gb, ALU.mult, ALU.mult)
        xt = self.x[:, t, :]
        S.tt("pool", xt, xt, tmp, ALU.add)

    def ffn(self, l):
        S, A = self.S, self.A
        m0 = A.mark()
        w_in = self.dram["ffn_w_in"][l].rearrange("(kc kp) n -> kp kc n", kp=128)
        w_out = self.dram["ffn_w_out"][l].rearrange("(fc fp) n -> fp fc n", fp=128)
        gslot = A.alloc((D,), F32)
        wout = A.alloc((NFC, D), BF16)
        hT = A.alloc((8, 1024), BF16)
        actT = A.alloc((NFC, 1024), BF16)
        GC = 256
        NG = DFF // GC
        wg = [A.alloc((8, GC), BF16) for _ in range(2)]
        wu = [A.alloc((8, GC), BF16) for _ in range(2)]
        tmp = [A.alloc((D,), F32) for _ in range(2)]
        sg = [tmp[0][:, 0:512], tmp[0][:, 512:1024]]
        junk = A.alloc((D,), BF16)
        ss = A.alloc((16,), F32)
        rs = A.alloc((16,), F32)
        self.tr_banks = (0, 1, 2, 3)
        for half in range(2):
            tiles = list(range(half * 8, half * 8 + 8))

            def load_g(g):
                s = g % 2
                S.dma("pool", wg[s], w_in[:, :, g * GC:(g + 1) * GC])
                S.dma("pool", wu[s], w_in[:, :, DFF + g * GC: DFF + (g + 1) * GC])
            load_g(0)
            self.load_gain(gslot, self.dram["norm_gains"][l, 4, :], 32.0)
            self.prenorm(tiles, gslot, hT, junk)
            if half == 0:
                for q in range(2):
                    S.dma("pool", wout[:, q * 11:(q + 1) * 11, :], w_out[:, q * 11:(q + 1) * 11, :])
            k = 0
            for g in range(NG):
                if g + 1 < NG:
                    load_g(g + 1)
                s = g % 2
                for j in range(GC // 128):
                    fc = g * (GC // 128) + j
                    for tb in range(2):
                        pg = self.bank(0 + (k % 2) * 2)
                        pu = self.bank(1 + (k % 2) * 2)
                        S.mm(pg, [(wg[s][:, kc, j * 128:(j + 1) * 128], hT[:, kc, tb * 512:(tb + 1) * 512])
                                  for kc in range(8)])
                        S.mm(pu, [(wu[s][:, kc, j * 128:(j + 1) * 128], hT[:, kc, tb * 512:(tb + 1) * 512])
                                  for kc in range(8)])
                        sgb = sg[k % 2]
                        S.act(sgb, pg, AF.Silu)
                        S.tt("dve", actT[:, fc, tb * 512:(tb + 1) * 512], sgb, pu, ALU.mult)
                        k += 1
            self.load_gain(gslot, self.dram["norm_gains"][l, 5, :], 32.0)
            for i, t in enumerate(tiles):
                py = self.bank(4 + (i % 2) * 2, 2)
                for nb in range(2):
                    S.mm(py[:, nb * 512:(nb + 1) * 512],
                         [(actT[:, fc, i * 128:(i + 1) * 128], wout[:, fc, nb * 512:(nb + 1) * 512])
                          for fc in range(NFC)])
                self.postnorm_add(py, gslot, t, ss, rs, half * 8 + i, junk, tmp[i % 2])
                if self.cfg.get("store_after") == ("ffn", l):
                    S.dma("sp", self.y_tiles[:, t, :], self.x[:, t, :])
        A.release(m0)

    def setup_mem(self):
        S, A = self.S, self.A
        self.mem_nT = A.alloc((8, 256), BF16)
        m = A.mark()
        memt = A.alloc((2, D), F32)
        gb = A.alloc((D,), F32)
        junk = A.alloc((D,), BF16)
        ss = A.alloc((2,), F32)
        hb = [A.alloc((D,), BF16) for _ in range(2)]
        S.dma("sp", memt, self.dram["mem"].rearrange("(t p) d -> p t d", p=128))
        self.load_gain(gb, self.dram["mem_norm"], 32.0)
        for t in range(2):
            S.act(junk, memt[:, t, :], AF.Square, accum_out=ss[:, t:t + 1])
        self.rstd(ss, ss, D * EPS)
        for t in range(2):
            S.stt("dve", hb[t], memt[:, t, :], ss[:, t:t + 1], gb, ALU.mult, ALU.mult)
            pb = self.bank_bf(t)
            for c in range(8):
                S.tr(pb[:, c * 128:(c + 1) * 128], hb[t][:, c * 128:(c + 1) * 128], self.ident)
            S.copy("act", self.mem_nT[:, :, t * 128:(t + 1) * 128], pb.rearrange("p (c t) -> p c t", t=128))
        A.release(m)

    def xattn(self, l):
        S, A = self.S, self.A
        m0 = A.mark()
        self.setup_mem()
        scale = 256.0 ** -0.5
        wq = self.dram["xa_wq"][l].rearrange("(kc kp) n -> kp kc n", kp=128)
        wkv = self.dram["xa_wkv"][l].rearrange("(kc kp) n -> kp kc n", kp=128)
        wo = self.dram["xa_wo"][l].rearrange("(kc kp) n -> kp kc n", kp=128)
        gslot = A.alloc((D,), F32)
        hT = A.alloc((8, T), BF16)
        qT = A.alloc((8, T), BF16)
        kT = A.alloc((8, 256), BF16)
        V = A.alloc((2, D), BF16)
        wb = [A.alloc((8, 512), BF16) for _ in range(2)]
        junk = A.alloc((D,), BF16)
        tmp = [A.alloc((D,), F32) for _ in range(2)]
        ss = A.alloc((16,), F32)
        rs = A.alloc((16,), F32)
        P = [A.alloc((4, 256), BF16) for _ in range(2)]
        PT = [A.alloc((8, 128), BF16) for _ in range(2)]
        mx = [A.alloc((4,), F32) for _ in range(2)]
        nb_ = [A.alloc((4,), F32) for _ in range(2)]
        sm = [A.alloc((4,), F32) for _ in range(2)]
        rinv = [A.alloc((4,), F32) for _ in range(2)]
        self.tr_banks = (0, 1, 2, 3)
        S.tag = "xa.kv"
        for g in range(4):
            S.dma("pool", wb[g % 2], wkv[:, :, g * 512:(g + 1) * 512])
            w = wb[g % 2]
            if g < 2:
                for j in range(4):
                    c = 4 * g + j
                    pb = self.bank(j % 2)
                    S.mm(pb[:, 0:256], [(w[:, kc, j * 128:(j + 1) * 128], self.mem_nT[:, kc, :]) for kc in range(8)])
                    S.copy("act" if j % 2 == 0 else "dve", kT[:, c, :], pb[:, 0:256])
            else:
                for mt in range(2):
                    pb = self.bank(mt)
                    S.mm(pb, [(self.mem_nT[:, kc, mt * 128:(mt + 1) * 128], w[:, kc, :]) for kc in range(8)])
                    S.copy("act" if mt == 0 else "dve", V[:, mt, (g - 2) * 512:(g - 1) * 512], pb)
        S.tag = "xa.pre"
        self.load_gain(gslot, self.dram["norm_gains"][l, 2, :], 32.0)
        S.dma("pool", wb[0], wq[:, :, 0:512])
        S.dma("pool", wb[1], wq[:, :, 512:1024])
        self.prenorm(list(range(16)), gslot, hT, junk)
        S.tag = "xa.q"
        k = 0
        for g in range(2):
            w = wb[g]
            for j in range(4):
                c = 4 * g + j
                for tb in range(4):
                    pb = self.bank(4 + k % 4)
                    S.mm(pb, [(w[:, kc, j * 128:(j + 1) * 128], hT[:, kc, tb * 512:(tb + 1) * 512]) for kc in range(8)])
                    S.copy("act" if k % 2 == 0 else "dve", qT[:, c, tb * 512:(tb + 1) * 512], pb)
                    k += 1
        S.dma("pool", wb[0], wo[:, :, 0:512])
        S.dma("pool", wb[1], wo[:, :, 512:1024])
        self.load_gain(gslot, self.dram["norm_gains"][l, 3, :], 32.0)
        oT = hT
        S.tag = "xa.attn"
        def xa_block(i):
            p = i % 2
            blk = slice(i * 128, (i + 1) * 128)
            psS = self.bank(2 * p, 2)
            for h in range(4):
                S.mm(psS[:, h * 256:(h + 1) * 256], [(qT[:, 2 * h + c, blk], kT[:, 2 * h + c, :]) for c in range(2)])
            yield
            S.reduce("dve", mx[p], psS.rearrange("p (h m) -> p h m", m=256), ALU.max)
            S.ts("dve", nb_[p], mx[p], -scale, None, ALU.mult)
            Pi = P[p]
            for h in range(4):
                S.act(Pi[:, h, :], psS[:, h * 256:(h + 1) * 256], AF.Exp, bias=nb_[p][:, h:h + 1], scale=scale,
                      accum_out=sm[p][:, h:h + 1])
            S.op("dve", lambda hh: hh.reciprocal(out=rinv[p], in_=sm[p]), [sm[p]], [rinv[p]])
            S.tt("dve", Pi, Pi, rinv[p].unsqueeze(2).to_broadcast([128, 4, 256]), ALU.mult)
            yield
            pt = self.bank_bf(4 + p)
            for h in range(4):
                for mc in range(2):
                    S.tr(pt[:, (h * 2 + mc) * 128:(h * 2 + mc + 1) * 128], Pi[:, h, mc * 128:(mc + 1) * 128], self.ident)
            PTi = PT[p]
            S.copy("act", PTi, pt.rearrange("p (c t) -> p c t", t=128))
            yield
            psO = self.bank(6, 2)
            for h in range(4):
                for c in range(2):
                    cc = 2 * h + c
                    S.mm(psO[:, cc * 128:(cc + 1) * 128],
                         [(V[:, mc, h * 256 + c * 128: h * 256 + (c + 1) * 128], PTi[:, h * 2 + mc, :]) for mc in range(2)])
            S.copy("dve", oT[:, :, blk], psO.rearrange("p (c t) -> p c t", t=128))
        pipeline((xa_block(i) for i in range(16)), depth=2)
        S.tag = "xa.out"
        for i in range(16):
            py = self.bank(2 * (i % 2), 2)
            for nb in range(2):
                S.mm(py[:, nb * 512:(nb + 1) * 512],
                     [(oT[:, kc, i * 128:(i + 1) * 128], wb[nb][:, kc, :]) for kc in range(8)])
            self.postnorm_add(py, gslot, i, ss, rs, i, junk, tmp[i % 2])
        A.release(m0)

    def load_cols(self, pcol, rows):
        S, A = self.S, self.A
        m = A.mark()
        nrow = len(rows)
        nch = rows[0].shape[1] // 128
        prow = A.alloc((nch * 128,), F32, parts=nrow)
        for j, r in enumerate(rows):
            S.dma("sp", prow[j:j + 1, :], r)
        ps = self.bank(0)
        for c in range(nch):
            S.tr(ps[:, c * nrow:(c + 1) * nrow], prow[0:nrow, c * 128:(c + 1) * 128], self.identf[0:nrow, 0:nrow])
        S.copy("dve", pcol, ps[:, 0:nch * nrow].rearrange("p (c r) -> p c r", r=nrow))
        A.release(m)

    def mixer(self, l):
        if l % 2 == 0:
            self.gdn_mla(l)
        else:
            self.rglru(l)

    def rglru(self, l):
        S, A = self.S, self.A
        o = l // 2
        dr = self.dram
        m0 = A.mark()
        w_in = dr["o_w_in"][o].rearrange("(kc kp) n -> kp kc n", kp=128)
        gslot = A.alloc((D,), F32)
        hT = A.alloc((8, T), BF16)
        mT = A.alloc((8, T), BF16)
        junk = A.alloc((D,), BF16)
        pcol = A.alloc((8, 8), F32)
        c1 = A.alloc((8,), F32)
        c2 = A.alloc((8,), F32)
        self.tr_banks = (4, 5, 6, 7)
        self.load_gain(gslot, dr["norm_gains"][l, 0, :], 32.0)
        rows = [dr["o_conv_w"][o, j:j + 1, :] for j in range(4)]
        rows += [dr["o_conv_b"][o:o + 1, :], dr["o_gate_a_b"][o:o + 1, :], dr["o_gate_x_b"][o:o + 1, :],
                 dr["o_a_param"][o:o + 1, :]]
        self.load_cols(pcol, rows)
        S.act(c1, pcol[:, :, 7], AF.Exp, scale=-1.0)
        S.act(c1, c1, AF.Ln, bias=1.0)
        S.ts("dve", c2, c1, -16.0, None, ALU.mult)
        S.ts("dve", c1, c1, -8.0, None, ALU.mult)
        self.prenorm(list(range(16)), gslot, hT, junk)
        m1 = A.mark()
        S.tag = "lru.blocks"
        HT = 1024
        wx = [A.alloc((8, 256), BF16) for _ in range(2)]
        wy = [A.alloc((8, 256), BF16) for _ in range(2)]
        ga = [A.alloc((2, 256), BF16) for _ in range(2)]
        gx = [A.alloc((2, 256), BF16) for _ in range(2)]
        prev3 = A.alloc((8, 3), F32)
        carry = A.alloc((8,), F32)

        class U:
            pass
        units = []
        g1v = gslot.bitcast(BF16)
        for ui in range(2):
            u = U()
            u.xc = [A.alloc((HT,), F32) for _ in range(2)]
            u.xcb = A.alloc((2, HT), BF16)
            u.A1 = A.alloc((HT,), F32)
            u.A2 = A.alloc((HT,), F32)
            u.A3 = A.alloc((HT,), F32)
            u.G1 = g1v[:, ui * HT:(ui + 1) * HT]
            units.append(u)

        def load_blk(n):
            sl = n % 2
            S.dma("pool", wx[sl], w_in[:, :, n * 256:(n + 1) * 256])
            S.dma("pool", wy[sl], w_in[:, :, D + n * 256: D + (n + 1) * 256])
            S.dma("pool", ga[sl], dr["o_gate_a_w"][o, n].rearrange("(dc dp) e -> dp dc e", dp=128))
            S.dma("pool", gx[sl], dr["o_gate_x_w"][o, n].rearrange("(dc dp) e -> dp dc e", dp=128))

        def unit(n, hf, k):
            u = units[k % 2]
            sl = n % 2
            tok0 = hf * HT
            pbase = 4 * (k % 2)
            X = [self.bank(pbase, 2), self.bank(pbase + 2, 2)]
            for c in range(2):
                for tb in range(2):
                    S.mm(X[c][:, tb * 512:(tb + 1) * 512],
                         [(wx[sl][:, kc, c * 128:(c + 1) * 128], hT[:, kc, tok0 + tb * 512: tok0 + (tb + 1) * 512])
                          for kc in range(8)])
            yield
            for c in range(2):
                ch = 2 * n + c
                xc = u.xc[c]
                S.ts("dve", xc, X[c], pcol[:, ch, 3:4], pcol[:, ch, 4:5], ALU.mult, ALU.add)
                for sh in (1, 2, 3):
                    S.stt("dve", xc[:, sh:], X[c][:, 0:HT - sh], pcol[:, ch, 3 - sh:4 - sh], xc[:, sh:], ALU.mult, ALU.add)
                if hf == 0:
                    S.copy("dve", prev3[:, ch, :], X[c][:, HT - 3:HT])
                else:
                    S.stt("dve", xc[:, 0:3], prev3[:, ch, 0:3], pcol[:, ch, 0:1], xc[:, 0:3], ALU.mult, ALU.add)
                    S.stt("dve", xc[:, 0:2], prev3[:, ch, 1:3], pcol[:, ch, 1:2], xc[:, 0:2], ALU.mult, ALU.add)
                    S.stt("dve", xc[:, 0:1], prev3[:, ch, 2:3], pcol[:, ch, 2:3], xc[:, 0:1], ALU.mult, ALU.add)
                S.copy("act", u.xcb[:, c, :], xc)
            yield
            for ec in range(2):
                ch = 2 * n + ec
                Ga = self.bank(pbase, 2)
                Gx = self.bank(pbase + 2, 2)
                for tb in range(2):
                    S.mm(Ga[:, tb * 512:(tb + 1) * 512],
                         [(ga[sl][:, dc, ec * 128:(ec + 1) * 128], u.xcb[:, dc, tb * 512:(tb + 1) * 512]) for dc in range(2)])
                for tb in range(2):
                    S.mm(Gx[:, tb * 512:(tb + 1) * 512],
                         [(gx[sl][:, dc, ec * 128:(ec + 1) * 128], u.xcb[:, dc, tb * 512:(tb + 1) * 512]) for dc in range(2)])
                yield
                S.act(u.A1, Ga, AF.Sigmoid, bias=pcol[:, ch, 5:6])
                S.act(u.A2, Gx, AF.Sigmoid, bias=pcol[:, ch, 6:7])
                Y = Ga
                for tb in range(2):
                    S.mm(Y[:, tb * 512:(tb + 1) * 512],
                         [(wy[sl][:, kc, ec * 128:(ec + 1) * 128], hT[:, kc, tok0 + tb * 512: tok0 + (tb + 1) * 512])
                          for kc in range(8)])
                S.act(u.A3, u.A1, AF.Exp, scale=c1[:, ch:ch + 1])
                S.act(u.A1, u.A1, AF.Exp, scale=c2[:, ch:ch + 1])
                S.act(u.G1, Y, AF.Gelu_apprx_tanh)
                yield
                S.ts("dve", u.A1, u.A1, -1.0, -1e-30, ALU.add, ALU.min)
                S.act(u.A1, u.A1, AF.Sqrt, scale=-1.0)
                yield
                S.tt("dve", u.A2, u.A2, u.A1, ALU.mult)
                S.tt("dve", u.A2, u.A2, u.xc[ec], ALU.mult)
                init = 0.0 if hf == 0 else carry[:, ch:ch + 1]
                S.scan(u.A1, u.A3, u.A2, init, ALU.mult, ALU.add)
                if hf == 0:
                    S.copy("dve", carry[:, ch:ch + 1], u.A1[:, HT - 1:HT])
                S.tt("dve", mT[:, ch, tok0:tok0 + HT], u.A1, u.G1, ALU.mult)
                if ec == 0:
                    yield
            if hf == 1 and n + 2 < 4:
                load_blk(n + 2)
        load_blk(0)
        load_blk(1)
        pipeline((unit(n, hf, 2 * n + hf) for n in range(4) for hf in range(2)), depth=2, skew=1)
        A.release(m1)
        wout = A.alloc((8, D), BF16)
        tmp = [A.alloc((D,), F32) for _ in range(2)]
        ss = A.alloc((16,), F32)
        rs = A.alloc((16,), F32)
        w_o = dr["o_w_out"][o].rearrange("(kc kp) n -> kp kc n", kp=128)
        S.dma("pool", wout[:, 0:4, :], w_o[:, 0:4, :])
        S.dma("pool", wout[:, 4:8, :], w_o[:, 4:8, :])
        self.load_gain(gslot, dr["norm_gains"][l, 1, :], 32.0)
        for i in range(16):
            py = self.bank(4 + 2 * (i % 2), 2)
            for nb in range(2):
                S.mm(py[:, nb * 512:(nb + 1) * 512],
                     [(mT[:, kc, i * 128:(i + 1) * 128], wout[:, kc, nb * 512:(nb + 1) * 512]) for kc in range(8)])
            self.postnorm_add(py, gslot, i, ss, rs, i, junk, tmp[i % 2])
        A.release(m0)

    def cload(self, name, shape, dtype, parts=128):
        t = self.A.alloc(shape, dtype, parts=parts)
        self.S.dma("sp", t, self.dram[name])
        return t

    def rope(self, x1, x2, cos, sin, o1, o2, rt):
        S = self.S
        S.tt("dve", rt[0], x1, cos, ALU.mult)
        S.tt("dve", rt[1], x2, sin, ALU.mult)
        S.tt("pool", o1, rt[0], rt[1], ALU.subtract)
        S.tt("dve", rt[2], x1, sin, ALU.mult)
        S.tt("dve", rt[3], x2, cos, ALU.mult)
        S.tt("dve", o2, rt[2], rt[3], ALU.add)

    def gdn_mla(self, l):
        S, A = self.S, self.A
        e = l // 2
        dr = self.dram
        m0 = A.mark()
        PI = float(np.pi)
        w_in = dr["e_w_in"][e].rearrange("(kc kp) n -> kp kc n", kp=128)
        gslot = A.alloc((D,), F32)
        junk = A.alloc((D,), BF16)
        mixT = A.alloc((8, T), BF16)
        ones_bf = self.cload("ones_bf", (128,), BF16)
        self.tr_banks = (4, 5, 6, 7)
        if not self.cfg.get("skip_mla"):
            self.mla(l, gslot, junk, mixT, ones_bf, w_in)
        else:
            S.memset("pool", mixT[:, 4:8, :], 0.0)
        if not self.cfg.get("skip_gdn"):
            self.gdn(l, gslot, junk, mixT, ones_bf, w_in)
        else:
            S.memset("pool", mixT[:, 0:4, :], 0.0)
        S.tag = "mix0.out"
        wout = A.alloc((8, D), BF16)
        tmp = [A.alloc((D,), F32) for _ in range(2)]
        ss = A.alloc((16,), F32)
        rs = A.alloc((16,), F32)
        w_o = dr["e_w_out"][e].rearrange("(kc kp) n -> kp kc n", kp=128)
        S.dma("pool", wout[:, 0:4, :], w_o[:, 0:4, :])
        S.dma("pool", wout[:, 4:8, :], w_o[:, 4:8, :])
        self.load_gain(gslot, dr["norm_gains"][l, 1, :], 32.0)
        for i in range(16):
            py = self.bank(4 + 2 * (i % 2), 2)
            for nb in range(2):
                S.mm(py[:, nb * 512:(nb + 1) * 512],
                     [(mixT[:, kc, i * 128:(i + 1) * 128], wout[:, kc, nb * 512:(nb + 1) * 512]) for kc in range(8)])
            self.postnorm_add(py, gslot, i, ss, rs, i, junk, tmp[i % 2])
        A.release(m0)

    def mla(self, l, gslot, junk, mixT, ones_bf, w_in):
        S, A = self.S, self.A
        e = l // 2
        dr = self.dram
        PI = float(np.pi)
        scale = 192.0 ** -0.5
        mm0 = A.mark()
        c_qnT = A.alloc((2, T), BF16)
        c_kvnT = A.alloc((2, T), BF16)
        kr = A.alloc((T,), BF16)
        S.memset("pool", kr[64:128, :], 0.0)
        cos = A.alloc((T,), F32)
        sin = A.alloc((T,), F32)
        pcoln = A.alloc((2, 2), F32)
        bigP = A.alloc((T,), BF16)
        bigPT = A.alloc((16, 128), BF16)
        rt = [bigP[:, 0:1024].bitcast(F32), bigP[:, 1024:2048].bitcast(F32),
              bigPT[:, 0:8, :].rearrange("p a b -> p (a b)").bitcast(F32),
              bigPT[:, 8:16, :].rearrange("p a b -> p (a b)").bitcast(F32)]
        negmask = self.cload("negmask_bf", (128,), BF16)
        self.load_cols(pcoln, [dr["e_q_norm"][e:e + 1, :], dr["e_kv_norm"][e:e + 1, :]])
        S.tag = "mla.rope"
        m1 = A.mark()
        invf = self.cload("inv_freq", (1,), F32, parts=32)
        posi = A.alloc((T,), I32)
        ang = A.alloc((T,), F32)
        tf = A.alloc((T,), F32)
        ti = A.alloc((T,), I32)
        S.dma("sp", posi[0:32, :], dr["positions"].partition_broadcast(32))
        S.copy("dve", ang[0:32, :], posi[0:32, :])
        S.ts("dve", ang[0:32, :], ang[0:32, :], invf[0:32, 0:1], None, ALU.mult)
        for dst, shift in ((sin, 0.0), (cos, PI / 2)):
            S.ts("dve", tf[0:32, :], ang[0:32, :], shift, 1.0 / (2 * PI), ALU.add, ALU.mult)
            S.copy("dve", ti[0:32, :], tf[0:32, :])
            S.copy("dve", tf[0:32, :], ti[0:32, :])
            S.stt("dve", tf[0:32, :], tf[0:32, :], -2 * PI, ang[0:32, :], ALU.mult, ALU.add)
            S.ts("dve", tf[0:32, :], tf[0:32, :], shift, 3.1415925, ALU.add, ALU.min)
            S.ts("dve", tf[0:32, :], tf[0:32, :], -3.1415925, None, ALU.max)
            S.act(dst[0:32, :], tf[0:32, :], AF.Sin)
        A.release(m1)
        S.tag = "mla.a"
        m1 = A.mark()
        hT = A.alloc((8, T), BF16)
        w576 = A.alloc((8, 576), BF16)
        sq = [A.alloc((512,), BF16) for _ in range(4)]
        rq = A.alloc((512,), F32)
        rkv = A.alloc((512,), F32)
        S.dma("pool", w576, w_in[:, :, 2056:2632])
        self.load_gain(gslot, dr["norm_gains"][l, 0, :], 32.0)
        self.prenorm(list(range(16)), gslot, hT, junk)
        for tb in range(4):
            tbs = slice(tb * 512, (tb + 1) * 512)
            for cc in range(4):
                S.mm(self.bank(cc), [(w576[:, kc, cc * 128:(cc + 1) * 128], hT[:, kc, tbs]) for kc in range(8)])
                S.act(sq[cc], self.bank(cc), AF.Square)
            S.mm(self.bank(4), [(ones_bf, sq[0]), (ones_bf, sq[1])])
            S.mm(self.bank(5), [(ones_bf, sq[2]), (ones_bf, sq[3])])
            S.act(rq, self.bank(4), AF.Ln, scale=1.0 / 256, bias=EPS)
            S.act(rq, rq, AF.Exp, scale=-0.5)
            S.act(rkv, self.bank(5), AF.Ln, scale=1.0 / 256, bias=EPS)
            S.act(rkv, rkv, AF.Exp, scale=-0.5)
            for cc in range(2):
                S.stt("dve", c_qnT[:, cc, tbs], self.bank(cc), pcoln[:, cc, 0:1], rq, ALU.mult, ALU.mult)
                S.stt("dve", c_kvnT[:, cc, tbs], self.bank(2 + cc), pcoln[:, cc, 1:2], rkv, ALU.mult, ALU.mult)
            x1 = self.bank(6)[0:32, :]
            x2 = self.bank(7)[0:32, :]
            S.mm(x1, [(w576[:, kc, 512:544], hT[:, kc, tbs]) for kc in range(8)])
            S.mm(x2, [(w576[:, kc, 544:576], hT[:, kc, tbs]) for kc in range(8)])
            self.rope(x1, x2, cos[0:32, tbs], sin[0:32, tbs], kr[0:32, tbs], kr[32:64, tbs],
                      [r[0:32, :] for r in rt])
        A.release(m1)
        S.tag = "mla.c"
        wuq = A.alloc((2, 768), BF16)
        wukv = A.alloc((2, 1024), BF16)
        S.dma("pool", wuq, dr["e_w_uq"][e].rearrange("(kc kp) n -> kp kc n", kp=128))
        S.dma("pool", wukv, dr["e_w_ukv"][e].rearrange("(kc kp) n -> kp kc n", kp=128))
        qn = [A.alloc((T,), BF16) for _ in range(2)]
        qr = [A.alloc((T,), BF16) for _ in range(2)]
        for t_ in qr:
            S.memset("pool", t_[64:128, :], 0.0)
        kn = [A.alloc((T,), BF16) for _ in range(2)]
        Vh = [A.alloc((16, 128), BF16) for _ in range(2)]

        class Res:
            pass
        big, small = Res(), Res()
        big.S = self.bank(0, 4)
        small.S = self.bank(4, 2)
        b6 = self.bank_bf(6)
        b7 = self.bank_bf(7)
        big.ptb = [b6[:, 0:512], b6[:, 512:1024]]
        small.ptb = [b7[:, 512:1024]]
        big.pso = self.bank(7)[:, 0:128]
        small.pso = self.bank(7)[:, 128:256]
        big.P, big.PT = bigP, bigPT
        small.P = A.alloc((1024,), BF16)
        small.PT = A.alloc((8, 128), BF16)
        for R in (big, small):
            R.mx = A.alloc((1,), F32)
            R.nb = A.alloc((1,), F32)
            R.sm = A.alloc((1,), F32)
            R.rinv = A.alloc((1,), F32)
        kk = [0]

        def proj(h):
            hb = h % 2
            for tb in range(4):
                tbs = slice(tb * 512, (tb + 1) * 512)

                def nbank():
                    b = self.bank(4 + kk[0] % 4)
                    kk[0] += 1
                    return b
                pb = nbank()
                S.mm(pb, [(wuq[:, kc, h * 192:h * 192 + 128], c_qnT[:, kc, tbs]) for kc in range(2)])
                S.copy("act", qn[hb][:, tbs], pb)
                pb = nbank()
                S.mm(pb, [(wukv[:, kc, h * 256:h * 256 + 128], c_kvnT[:, kc, tbs]) for kc in range(2)])
                S.copy("act", kn[hb][:, tbs], pb)
                x1 = nbank()[0:32, :]
                x2 = nbank()[0:32, :]
                S.mm(x1, [(wuq[:, kc, h * 192 + 128:h * 192 + 160], c_qnT[:, kc, tbs]) for kc in range(2)])
                S.mm(x2, [(wuq[:, kc, h * 192 + 160:h * 192 + 192], c_qnT[:, kc, tbs]) for kc in range(2)])
                self.rope(x1, x2, cos[0:32, tbs], sin[0:32, tbs], qr[hb][0:32, tbs], qr[hb][32:64, tbs],
                          [r[0:32, :] for r in rt])
                pb = nbank()
                for j in range(4):
                    t = tb * 4 + j
                    S.mm(pb[:, j * 128:(j + 1) * 128],
                         [(c_kvnT[:, kc, t * 128:(t + 1) * 128], wukv[:, kc, h * 256 + 128:h * 256 + 256]) for kc in range(2)])
                S.copy("dve", Vh[hb][:, tb * 4:(tb + 1) * 4, :], pb.rearrange("p (j d) -> p j d", d=128))

        def attn_block(h, i, R):
            hb = h % 2
            blk = slice(i * 128, (i + 1) * 128)
            nk = (i + 1) * 128
            Sb = R.S
            nb4 = (nk + 511) // 512
            for b4 in range(nb4):
                cols = slice(b4 * 512, min(nk, (b4 + 1) * 512))
                grp = [(qn[hb][:, blk], kn[hb][:, cols]),
                       (qr[hb][:, blk], kr[:, cols])]
                if b4 == nb4 - 1:
                    grp.append((Sb[:, blk], self.ident, negmask))
                S.mm(Sb[:, cols], grp)
            yield
            S.reduce("dve", R.mx, Sb[:, 0:nk], ALU.max)
            S.ts("dve", R.nb, R.mx, -scale, None, ALU.mult)
            Pi = R.P
            S.act(Pi[:, 0:nk], Sb[:, 0:nk], AF.Exp, bias=R.nb[:, 0:1], scale=scale, accum_out=R.sm[:, 0:1])
            S.op("dve", lambda hh: hh.reciprocal(out=R.rinv, in_=R.sm), [R.sm], [R.rinv])
            S.ts("dve", Pi[:, 0:nk], Pi[:, 0:nk], R.rinv[:, 0:1], None, ALU.mult)
            yield
            ng = (i + 4) // 4
            for g4 in range(ng):
                pt = R.ptb[g4 % len(R.ptb)]
                n4 = min(4, i + 1 - g4 * 4)
                for j in range(n4):
                    kb = g4 * 4 + j
                    S.tr(pt[:, j * 128:(j + 1) * 128], Pi[:, kb * 128:(kb + 1) * 128], self.ident)
                S.copy("act" if g4 % 2 == 0 else "dve", R.PT[:, g4 * 4:g4 * 4 + n4, :],
                       pt[:, 0:n4 * 128].rearrange("p (c t) -> p c t", t=128))
            yield
            S.mm(R.pso, [(Vh[hb][:, kb, :], R.PT[:, kb, :]) for kb in range(i + 1)])
            S.copy("act", mixT[:, 4 + h, blk], R.pso)

        proj(0)
        for h in range(4):
            if h + 1 < 4:
                proj(h + 1)
            order = []
            for j in range(8):
                order.append(attn_block(h, 8 + j, big))
                order.append(attn_block(h, j, small))
            pipeline(order, depth=2)
        A.release(mm0)

    def gdn(self, l, gslot, junk, mixT, ones_bf, w_in):
        S, A = self.S, self.A
        e = l // 2
        dr = self.dram
        mg0 = A.mark()
        qkvT = A.alloc((12, T), BF16)
        zs = A.alloc((16, 512), BF16)
        abraw = A.alloc((16, 8), F32)
        S.tag = "gdn.proj"
        m1 = A.mark()
        hTh = A.alloc((8, 1024), BF16)
        wsl = [A.alloc((8, 128), BF16) for _ in range(2)]
        wab = A.alloc((8, 8), BF16)
        sq = A.alloc((1024,), BF16)
        prev3 = A.alloc((12, 3), F32)
        pcolc = A.alloc((12, 4), F32)
        xcs = [mixT[:, 0, :].bitcast(F32), mixT[:, 1, :].bitcast(F32)]
        rsts = [mixT[:, 2, :].bitcast(F32), mixT[:, 3, :].bitcast(F32)]
        sqs = [sq, junk]
        self.load_cols(pcolc, [dr["e_conv_w"][e, j:j + 1, :] for j in range(4)])
        S.dma("pool", wab, w_in[:, :, 2048:2056])
        nload = [0]

        def wload(col0):
            sl = wsl[nload[0] % 2]
            nload[0] += 1
            S.dma("pool", sl, w_in[:, :, col0:col0 + 128])
            return sl
        qs = 128.0 ** -0.5
        for hf in range(2):
            tiles = list(range(hf * 8, hf * 8 + 8))
            hcol = slice(hf * 1024, (hf + 1) * 1024)
            self.load_gain(gslot, dr["norm_gains"][l, 0, :], 32.0)
            self.prenorm(tiles, gslot, hTh, junk)
            nxt_box = [wload(0)]

            def proj_chunk(c, hf=hf, hcol=hcol):
                p = c % 2
                w = nxt_box[0]
                nxt_box[0] = wload((c + 1) * 128) if c + 1 < 12 else wload(1536)
                X = self.bank(2 * p, 2)
                for tb in range(2):
                    S.mm(X[:, tb * 512:(tb + 1) * 512],
                         [(w[:, kc, :], hTh[:, kc, tb * 512:(tb + 1) * 512]) for kc in range(8)])
                yield
                xcp = xcs[p]
                S.ts("dve", xcp, X, pcolc[:, c, 3:4], None, ALU.mult)
                for sh in (1, 2, 3):
                    S.stt("dve", xcp[:, sh:], X[:, 0:1024 - sh], pcolc[:, c, 3 - sh:4 - sh], xcp[:, sh:], ALU.mult, ALU.add)
                if hf == 0:
                    S.copy("dve", prev3[:, c, :], X[:, 1021:1024])
                else:
                    S.stt("dve", xcp[:, 0:3], prev3[:, c, 0:3], pcolc[:, c, 0:1], xcp[:, 0:3], ALU.mult, ALU.add)
                    S.stt("dve", xcp[:, 0:2], prev3[:, c, 1:3], pcolc[:, c, 1:2], xcp[:, 0:2], ALU.mult, ALU.add)
                    S.stt("dve", xcp[:, 0:1], prev3[:, c, 2:3], pcolc[:, c, 2:3], xcp[:, 0:1], ALU.mult, ALU.add)
                yield
                if c >= 8:
                    S.act(qkvT[:, c, hcol], xcp, AF.Silu)
                    return
                S.act(xcp, xcp, AF.Silu)
                S.act(sqs[p], xcp, AF.Square)
                yield
                SS = self.bank(4 + 2 * p, 2)
                for tb in range(2):
                    S.mm(SS[:, tb * 512:(tb + 1) * 512], [(ones_bf, sqs[p][:, tb * 512:(tb + 1) * 512])])
                S.act(rsts[p], SS, AF.Ln, bias=EPS)
                S.act(rsts[p], rsts[p], AF.Exp, scale=-0.5)
                yield
                S.stt("dve", qkvT[:, c, hcol], xcp, (qs if c < 4 else 1.0), rsts[p], ALU.mult, ALU.mult)
            pipeline((proj_chunk(c) for c in range(12)), depth=2)
            nxt = nxt_box[0]
            for zc in range(4):
                w = nxt
                if zc + 1 < 4:
                    nxt = wload(1536 + (zc + 1) * 128)
                for tg in range(2):
                    pb = self.bank(6 + tg)
                    for j in range(4):
                        t = tg * 4 + j
                        S.mm(pb[:, j * 128:(j + 1) * 128],
                             [(hTh[:, kc, t * 128:(t + 1) * 128], w[:, kc, :]) for kc in range(8)])
                    S.act(zs[:, hf * 8 + tg * 4: hf * 8 + tg * 4 + 4, zc * 128:(zc + 1) * 128],
                          pb.rearrange("p (j d) -> p j d", d=128), AF.Silu)
            pb = self.bank(4)
            for t in range(8):
                S.mm(pb[:, t * 8:(t + 1) * 8], [(hTh[:, kc, t * 128:(t + 1) * 128], wab[:, kc, :]) for kc in range(8)])
            S.copy("dve", abraw[:, hf * 8:(hf + 1) * 8, :], pb[:, 0:64].rearrange("p (t c) -> p t c", c=8))
        A.release(m1)
        S.tag = "gdn.gates"
        g_all = A.alloc((16, 4), F32)
        beta_all = A.alloc((16, 4), F32)
        alog = A.alloc((4,), F32)
        dtb = A.alloc((4,), F32)
        onb = A.alloc((128,), F32)
        S.dma("sp", alog, dr["e_a_log"][e].partition_broadcast(128))
        S.dma("sp", dtb, dr["e_dt_bias"][e].partition_broadcast(128))
        S.dma("sp", onb, dr["e_o_norm"][e].partition_broadcast(128))
        S.act(alog, alog, AF.Exp)
        S.ts("dve", alog, alog, -1.0, None, ALU.mult)
        S.tt("dve", g_all, abraw[:, :, 0:4], dtb.unsqueeze(1).to_broadcast([128, 16, 4]), ALU.add)
        S.act(g_all, g_all, AF.Exp)
        S.act(g_all, g_all, AF.Ln, bias=1.0)
        S.tt("dve", g_all, g_all, alog.unsqueeze(1).to_broadcast([128, 16, 4]), ALU.mult)
        S.act(beta_all, abraw[:, :, 4:8], AF.Sigmoid)
        S.tag = "gdn.core"
        triu = self.cload("triu_f32", (128,), F32)
        ones_f = self.cload("ones_f32", (128,), F32)
        maskL = self.cload("maskL_f32", (128,), F32)
        maskA = self.cload("maskA_f32", (128,), F32)
        same01 = self.cload("same01_bf", (128,), BF16)
        off01 = self.cload("off01_bf", (128,), BF16)
        H4 = (4, 128)
        ngtri = A.alloc(H4, F32)
        gbc = A.alloc(H4, F32)
        decL = A.alloc(H4, F32)
        decA = A.alloc(H4, F32)
        egcR = A.alloc(H4, F32)
        Lb = A.alloc(H4, BF16)
        Lo = A.alloc(H4, BF16)
        Pbuf = [A.alloc(H4, BF16) for _ in range(2)]
        Qbuf = [A.alloc(H4, BF16) for _ in range(2)]
        Ybuf = [A.alloc(H4, BF16) for _ in range(2)]
        U0 = A.alloc(H4, BF16)
        aqkT = A.alloc(H4, BF16)
        kbg = A.alloc(H4, BF16)
        kdec = A.alloc(H4, BF16)
        vb = A.alloc(H4, BF16)
        u_sb = A.alloc(H4, F32)
        wT = A.alloc(H4, BF16)
        qgT = A.alloc(H4, BF16)
        vnew = A.alloc(H4, BF16)
        S_f = A.alloc(H4, F32)
        S_b = A.alloc(H4, BF16)
        outa = A.alloc(H4, BF16)
        sqo = gslot[:, 0:512].rearrange("p (h d) -> p h d", d=128)
        on = gslot[:, 512:1024].rearrange("p (h d) -> p h d", d=128)
        gz = junk.bitcast(F32).rearrange("p (h d) -> p h d", d=128)
        ng = A.alloc((4,), F32)
        gcs = A.alloc((4,), F32)
        egc = A.alloc((4,), F32)
        bg = A.alloc((4,), F32)
        edec = A.alloc((4,), F32)
        ssn = A.alloc((4,), F32)
        rsn = A.alloc((4,), F32)
        S.memset("dve", S_f, 0.0)
        S.memset("dve", S_b, 0.0)

        def b4(b):
            return self.bank(b).rearrange("p (h d) -> p h d", d=128)

        def bc_h(t2):
            return t2.unsqueeze(1).to_broadcast([128, 4, 128])

        def bc_d(t4):
            return t4.unsqueeze(2).to_broadcast([128, 4, 128])
        identb = self.ident
        for i in range(self.cfg.get("gdn_blocks", 16)):
            blk = slice(i * 128, (i + 1) * 128)
            g_i = g_all[:, i, :]
            beta_i = beta_all[:, i, :]
            qTb = qkvT[:, 0:4, blk]
            kTb = qkvT[:, 4:8, blk]
            vTb = qkvT[:, 8:12, blk]
            S.ts("dve", ng, g_i, -1.0, None, ALU.mult)
            S.tt("dve", ngtri, bc_h(triu), bc_d(ng), ALU.mult)
            S.copy("dve", gbc, bc_d(g_i))
            Dm = b4(0)
            Rn = b4(1)
            for h in range(4):
                S.mm(Dm[:, h, :], [(triu, gbc[:, h, :]), (ones_f, ngtri[:, h, :])])
            for h in range(4):
                S.mm(Rn[:, h, :], [(ones_f, ngtri[:, h, :])])
            gcp = self.bank(2)[:, 0:4]
            S.mm(gcp, [(triu, g_i)])
            S.tt("dve", decL, Dm, bc_h(maskL), ALU.add)
            S.act(decL, decL, AF.Exp)
            S.tt("dve", decA, bc_h(maskA), Dm, ALU.subtract)
            S.act(decA, decA, AF.Exp)
            S.act(egcR, Rn, AF.Exp, scale=-1.0)
            S.act(egc, gcp, AF.Exp)
            S.act(edec, Dm[:, :, 127], AF.Exp, scale=-1.0)
            S.tt("dve", bg, beta_i, egc, ALU.mult)
            if self.cfg.get("gdn_cut", 99) <= 1:
                continue
            KK = b4(3)
            QK = b4(4)
            for h in range(4):
                S.mm(KK[:, h, :], [(kTb[:, h, :], kTb[:, h, :])])
            for h in range(4):
                S.mm(QK[:, h, :], [(kTb[:, h, :], qTb[:, h, :])])
            for h in range(4):
                S.stt("dve", Lb[:, h, :], KK[:, h, :], beta_i[:, h:h + 1], decL[:, h, :], ALU.mult, ALU.mult)
            S.tt("dve", aqkT, QK, decA, ALU.mult)
            if self.cfg.get("gdn_cut", 99) <= 2:
                continue
            tb5 = self.bank_bf(5)
            for h in range(4):
                S.tr(tb5[:, h * 128:(h + 1) * 128], kTb[:, h, :], identb)
            for h in range(4):
                S.tr(tb5[:, (4 + h) * 128:(5 + h) * 128], vTb[:, h, :], identb)
            kTM = tb5[:, 0:512].rearrange("p (h d) -> p h d", d=128)
            vTM = tb5[:, 512:1024].rearrange("p (h d) -> p h d", d=128)
            S.tt("dve", kbg, kTM, bc_d(bg), ALU.mult)
            S.tt("dve", kdec, kTM, bc_d(edec), ALU.mult)
            S.tt("dve", vb, vTM, bc_d(beta_i), ALU.mult)
            if self.cfg.get("gdn_cut", 99) <= 3:
                continue
            S.tt("dve", Lo, Lb, bc_h(off01), ALU.mult)
            S.tt("dve", Lb, Lb, bc_h(same01), ALU.mult)
            tb6 = self.bank_bf(6)
            for h in range(4):
                S.tr(tb6[:, h * 128:(h + 1) * 128], Lb[:, h, :], identb)
            S.copy("act", U0, tb6[:, 0:512].rearrange("p (h d) -> p h d", d=128))
            Yc = Ybuf[0]
            S.tt("dve", Yc, bc_h(identb), U0, ALU.subtract)
            Pc, Qc = Lb, U0
            for j in range(1, 6):
                PB = b4(3)
                for h in range(4):
                    S.mm(PB[:, h, :], [(Qc[:, h, :], Pc[:, h, :])])
                Pn = Pbuf[j % 2]
                S.copy("act", Pn, PB)
                if j < 5:
                    QB = b4(4)
                    for h in range(4):
                        S.mm(QB[:, h, :], [(Pc[:, h, :], Qc[:, h, :])])
                    Qn = Qbuf[j % 2]
                    S.copy("act", Qn, QB)
                else:
                    Qn = None
                YB = b4(7)
                for h in range(4):
                    S.mm(YB[:, h, :], [(Pn[:, h, :], Yc[:, h, :])])
                Yn = Ybuf[j % 2]
                S.tt("dve", Yn, YB, Yc, ALU.add)
                Pc, Qc, Yc = Pn, Qn, Yn
            Ydt = Yc
            Td = Qbuf[1]
            tb6 = self.bank_bf(6)
            for h in range(4):
                S.tr(tb6[:, h * 128:(h + 1) * 128], Ydt[:, h, :], identb)
            S.copy("act", Td, tb6[:, 0:512].rearrange("p (h d) -> p h d", d=128))
            AB = b4(3)
            for h in range(4):
                S.mm(AB[:, h, :], [(Lo[:, h, :], Ydt[:, h, :])])
            Ab = Pbuf[0]
            S.copy("act", Ab, AB)
            TB = b4(4)
            for h in range(4):
                S.mm(TB[:, h, :], [(Td[:, h, :], Ab[:, h, :])])
            Tt = Ybuf[0]
            S.tt("dve", Tt, Ydt, TB, ALU.subtract)
            if self.cfg.get("gdn_cut", 99) <= 4:
                continue
            UB = b4(2)
            for h in range(4):
                S.mm(UB[:, h, :], [(Tt[:, h, :], vb[:, h, :])])
            S.copy("act", u_sb, UB)
            WB = b4(3)
            for h in range(4):
                S.mm(WB[:, h, :], [(kbg[:, h, :], Tt[:, h, :])])
            S.copy("act", wT, WB)
            S.tt("dve", qgT, qTb, egcR, ALU.mult)
            if self.cfg.get("gdn_cut", 99) <= 5:
                continue
            sub = self.cfg.get("gdn_sub", 99)
            P1 = b4(4)
            for h in range(4):
                S.mm(P1[:, h, :], [(wT[:, h, :], S_b[:, h, :])])
            if sub <= 1:
                continue
            S.tt("dve", vnew, u_sb, P1, ALU.subtract)
            if sub <= 2:
                continue
            PO = b4(2)
            for h in range(4):
                S.mm(PO[:, h, :], [(qgT[:, h, :], S_b[:, h, :]), (aqkT[:, h, :], vnew[:, h, :])])
            if sub <= 3:
                continue
            PS = b4(7)
            for h in range(4):
                S.mm(PS[:, h, :], [(kdec[:, h, :], vnew[:, h, :])])
            if sub <= 4:
                continue
            S.tt("dve", S_f, S_f, egcR[:, :, 127:128].to_broadcast([128, 4, 128]), ALU.mult)
            S.tt("dve", S_f, PS, S_f, ALU.add)
            if sub <= 5:
                continue
            S.copy("act", S_b, S_f)
            if self.cfg.get("gdn_cut", 99) <= 6:
                continue
            S.act(sqo, PO, AF.Square)
            S.reduce("dve", ssn, sqo, ALU.add)
            S.act(rsn, ssn, AF.Ln, scale=1.0 / 128, bias=EPS)
            S.act(rsn, rsn, AF.Exp, scale=-0.5)
            S.tt("dve", on, PO, bc_d(rsn), ALU.mult)
            S.tt("pool", gz, zs[:, i, :].rearrange("p (h d) -> p h d", d=128), bc_h(onb), ALU.mult)
            S.tt("dve", outa, on, gz, ALU.mult)
            tb5 = self.bank_bf(5)
            for h in range(4):
                S.tr(tb5[:, h * 128:(h + 1) * 128], outa[:, h, :], identb)
            S.copy("act", mixT[:, 0:4, blk], tb5[:, 0:512].rearrange("p (h d) -> p h d", d=128))
        A.release(mg0)

    def build(self):
        cfg = self.cfg
        S, A = self.S, self.A
        x_in = self.din("x", (T, D))
        self.din("norm_gains", (2, 6, D))
        self.din("ffn_w_in", (2, D, 2 * DFF))
        self.din("ffn_w_out", (2, DFF, D))
        self.din("mem", (256, D))
        self.din("mem_norm", (D,))
        self.din("xa_wq", (2, D, D))
        self.din("xa_wkv", (2, D, 2 * D))
        self.din("xa_wo", (2, D, D))
        self.din("o_w_in", (1, D, 2 * D))
        self.din("o_conv_w", (1, 4, D))
        self.din("o_conv_b", (1, D))
        self.din("o_gate_a_w", (1, 4, 256, 256))
        self.din("o_gate_a_b", (1, D))
        self.din("o_gate_x_w", (1, 4, 256, 256))
        self.din("o_gate_x_b", (1, D))
        self.din("o_a_param", (1, D))
        self.din("o_w_out", (1, D, D))
        identf_in = self.din("ident_f32", (128, 128), F32)
        self.din("positions", (T,), I32)
        self.din("e_w_in", (1, D, E_IN))
        self.din("e_conv_w", (1, 4, 1536))
        self.din("e_a_log", (1, 4))
        self.din("e_dt_bias", (1, 4))
        self.din("e_o_norm", (1, 128))
        self.din("e_q_norm", (1, 256))
        self.din("e_kv_norm", (1, 256))
        self.din("e_w_uq", (1, 256, 768))
        self.din("e_w_ukv", (1, 256, 1024))
        self.din("e_w_out", (1, D, D))
        self.din("ones_bf", (128, 128), BF16)
        self.din("negmask_bf", (128, 128), BF16)
        self.din("inv_freq", (32, 1), F32)
        self.din("ones_f32", (128, 128), F32)
        self.din("triu_f32", (128, 128), F32)
        self.din("maskL_f32", (128, 128), F32)
        self.din("maskA_f32", (128, 128), F32)
        self.din("same01_bf", (128, 128), BF16)
        self.din("off01_bf", (128, 128), BF16)
        ident_in = self.din("ident_bf", (128, 128), BF16)
        y_out = self.dout("y", (T, D))

        self.x = A.alloc((NT, D), F32)
        self.ident = A.alloc((128,), BF16)
        S.dma("sp", self.ident, ident_in)
        self.identf = A.alloc((128,), F32)
        S.dma("sp", self.identf, identf_in)
        xin_v = x_in.rearrange("(t p) d -> p t d", p=128)
        for q in range(4):
            S.dma("sp", self.x[:, q * 4:(q + 1) * 4, :], xin_v[:, q * 4:(q + 1) * 4, :])

        self.y_tiles = y_out.rearrange("(t p) d -> p t d", p=128)
        streamed = bool(cfg["steps"]) and cfg["steps"][-1][0] == "ffn"
        if streamed:
            cfg["store_after"] = tuple(cfg["steps"][-1])
        for step in cfg["steps"]:
            kind, l = step
            S.tag = "%s%d" % (kind, l)
            if kind == "ffn":
                self.ffn(l)
            elif kind == "xa":
                self.xattn(l)
            elif kind == "mix":
                self.mixer(l)

        yv = y_out.rearrange("(t p) d -> p t d", p=128)
        if not streamed:
            for q in range(4):
                S.dma("sp", yv[:, q * 4:(q + 1) * 4, :], self.x[:, q * 4:(q + 1) * 4, :])
        S.final_wait("sp")
        n = S.emit()
        self.n_ins = n
        self.es.close()
        return self.nc


FULL_STEPS = [("mix", 0), ("xa", 0), ("ffn", 0), ("mix", 1), ("xa", 1), ("ffn", 1)]

_CONST = {}


def consts():
    if not _CONST:
        _CONST["ident_bf"] = np.eye(128, dtype=np.float32).astype(ml_dtypes.bfloat16)
        _CONST["ident_f32"] = np.eye(128, dtype=np.float32)
        _CONST["ones_bf"] = np.ones((128, 128), np.float32).astype(ml_dtypes.bfloat16)
        _CONST["ones_f32"] = np.ones((128, 128), np.float32)
        qi = np.arange(128)[:, None]
        ki = np.arange(128)[None, :]
        _CONST["negmask_bf"] = np.where((qi < 64) & (ki >= 64), NEG, 0.0).astype(np.float32).astype(ml_dtypes.bfloat16)
        _CONST["inv_freq"] = (np.float32(10000.0) ** (-np.arange(0, 64, 2, dtype=np.float32) / np.float32(64))
                              ).astype(np.float32).reshape(32, 1)
        _CONST["triu_f32"] = (qi <= ki).astype(np.float32)
        _CONST["maskL_f32"] = np.where(qi > ki, 0.0, NEG).astype(np.float32)
        _CONST["maskA_f32"] = np.where(ki >= qi, 0.0, NEG).astype(np.float32)
        same = (qi // 64) == (ki // 64)
        _CONST["same01_bf"] = same.astype(np.float32).astype(ml_dtypes.bfloat16)
        _CONST["off01_bf"] = (~same).astype(np.float32).astype(ml_dtypes.bfloat16)
    return _CONST


def kernel(**inputs):
    cfg = {"steps": FULL_STEPS}
    return run(cfg, inputs)


def run(cfg, inputs, trace=False):
    p = Prog(cfg)
    nc = p.build()
    c = consts()
    names = [n for n in p.dram if n != "y"]
    in_maps = []
    for b in range(8):
        m = {}
        for n in names:
            if n in c:
                m[n] = c[n]
            elif n in ("x", "mem", "positions"):
                m[n] = np.ascontiguousarray(inputs[n][b])
            else:
                m[n] = np.ascontiguousarray(inputs[n])
        in_maps.append(m)
    res = run_bass_kernel_spmd(nc, in_maps, core_ids=list(range(8)), trace=trace)
    out = np.stack([np.asarray(r["y"]) for r in res.results], axis=0)
    if trace:
        return out, res
    return out
```
